# Optimizing a Trainium2 kernel written in Bass

```python
import math
import jax
import jax.numpy as jnp
from jax import lax
import numpy as np

D_MODEL = 1024
BATCH = 8
SEQ = 8192
DEPTH = 2
DEC_BATCH = 8
DEC_SEQ = 64
PAST_LEN = 1024

CHUNK = 64
N_META = 16
N_MIXERS = 2
N_GDN_LAYERS = (DEPTH + 1) // 2
N_RWKV_LAYERS = DEPTH // 2
EPS = 1e-6
D_FF = -(-8 * D_MODEL // (3 * 256)) * 256

GDN_HEADS = 8
GDN_DK = 128
GDN_DV = 128
GDN_KDIM = GDN_HEADS * GDN_DK
GDN_VDIM = GDN_HEADS * GDN_DV
GDN_QKV = 2 * GDN_KDIM + GDN_VDIM
GDN_IN = GDN_QKV + GDN_VDIM + 2 * GDN_HEADS
CONV_W = 4

RWKV_N = 64
RWKV_HEADS = D_MODEL // RWKV_N
D_DECAY_LORA = max(32, int(round(1.8 * D_MODEL ** 0.5 / 32)) * 32)
D_AAA_LORA = max(32, int(round(1.8 * D_MODEL ** 0.5 / 32)) * 32)
D_GATE_LORA = max(32, int(round(0.6 * D_MODEL ** 0.8 / 32)) * 32)
N_MU = 6
GN_EPS = 64e-5

kernel_name = 'hybrid_gdn_rwkv7_stream_step'


def rms_norm(x, g, eps=EPS):
    x32 = x.astype(jnp.float32)
    y = x32 * lax.rsqrt(jnp.mean(x32 * x32, axis=-1, keepdims=True) + eps)
    return (y * g.astype(jnp.float32)).astype(x.dtype)


def l2_normalize(x, eps=1e-6):
    return x * lax.rsqrt(jnp.sum(x * x, axis=-1, keepdims=True) + eps)


def swiglu(x, w_gate, w_up, w_down):
    return (jax.nn.silu(x @ w_gate) * (x @ w_up)) @ w_down


def causal_short_conv(x, buf, w):
    T = x.shape[1]
    xe = jnp.concatenate([buf.astype(x.dtype), x], axis=1)
    y = sum(xe[:, j:j + T] * w[j] for j in range(CONV_W))
    return y, xe[:, -(CONV_W - 1):]


def chunk_segments(T, n_lead):
    segs = []
    if n_lead > 0:
        segs.append((0, n_lead, n_lead))
    rest = T - n_lead
    segs.append((n_lead, rest, min(CHUNK, rest)))
    return segs


def gdn_chunk_scan(q, k, v, g, beta, S):
    L = q.shape[3]
    G = jnp.cumsum(g, axis=-1)
    idx = jnp.arange(L)
    causal = idx[:, None] >= idx[None, :]
    strict = idx[:, None] > idx[None, :]
    decay_mat = jnp.exp(jnp.where(causal, G[..., :, None] - G[..., None, :], -jnp.inf))
    kk = jnp.einsum('bhcid,bhcjd->bhcij', k, k)
    A = jnp.where(strict, beta[..., :, None] * kk * decay_mat, 0.0)
    M = A + jnp.eye(L, dtype=jnp.float32)
    rhs = jnp.concatenate([v * beta[..., None], k * (beta * jnp.exp(G))[..., None]], axis=-1)
    sol = lax.linalg.triangular_solve(M, rhs, left_side=True, lower=True, unit_diagonal=True)
    u, w = sol[..., :GDN_DV], sol[..., GDN_DV:]
    qk = jnp.einsum('bhcid,bhcjd->bhcij', q, k) * decay_mat
    qg = q * jnp.exp(G)[..., None]
    kd = k * jnp.exp(G[..., -1:] - G)[..., None]
    gL = jnp.exp(G[..., -1])

    def step(S, xs):
        u_c, w_c, qk_c, qg_c, kd_c, gL_c = xs
        v_new = u_c - jnp.einsum('bhld,bhde->bhle', w_c, S)
        o_c = jnp.einsum('bhld,bhde->bhle', qg_c, S) + jnp.einsum('bhij,bhje->bhie', qk_c, v_new)
        S = S * gL_c[..., None, None] + jnp.einsum('bhld,bhle->bhde', kd_c, v_new)
        return S, o_c

    xs = tuple(jnp.moveaxis(t, 2, 0) for t in (u, w, qk, qg, kd, gL))
    S, o = lax.scan(step, S, xs)
    return jnp.moveaxis(o, 0, 2), S


def gdn_mixer(h, conv_buf, S0, w_in, conv_w, a_log, dt_bias, o_norm, w_out, n_lead):
    B, T, _ = h.shape
    f32 = jnp.float32
    proj = h @ w_in
    qkv, z, a_in, b_in = jnp.split(proj, [GDN_QKV, GDN_QKV + GDN_VDIM, GDN_QKV + GDN_VDIM + GDN_HEADS], axis=-1)
    qkv, new_buf = causal_short_conv(qkv, conv_buf, conv_w)
    qkv = jax.nn.silu(qkv.astype(f32))
    q, k, v = jnp.split(qkv, [GDN_KDIM, 2 * GDN_KDIM], axis=-1)
    q = l2_normalize(q.reshape(B, T, GDN_HEADS, GDN_DK)) * GDN_DK ** -0.5
    k = l2_normalize(k.reshape(B, T, GDN_HEADS, GDN_DK))
    v = v.reshape(B, T, GDN_HEADS, GDN_DV)
    beta = jax.nn.sigmoid(b_in.astype(f32))
    g = -jnp.exp(a_log.astype(f32)) * jax.nn.softplus(a_in.astype(f32) + dt_bias.astype(f32))
    q, k, v = (jnp.swapaxes(t, 1, 2) for t in (q, k, v))
    g, beta = jnp.swapaxes(g, 1, 2), jnp.swapaxes(beta, 1, 2)
    S = S0.astype(f32)
    outs = []
    for start, length, L in chunk_segments(T, n_lead):
        def blk(t):
            return t[:, :, start:start + length].reshape((B, GDN_HEADS, length // L, L) + t.shape[3:])
        o_seg, S = gdn_chunk_scan(blk(q), blk(k), blk(v), blk(g), blk(beta), S)
        outs.append(o_seg.reshape(B, GDN_HEADS, length, GDN_DV))
    o = jnp.swapaxes(jnp.concatenate(outs, axis=2), 1, 2)
    o = rms_norm(o, o_norm) * jax.nn.silu(z.astype(f32)).reshape(B, T, GDN_HEADS, GDN_DV)
    y = o.reshape(B, T, GDN_VDIM).astype(h.dtype) @ w_out
    return y, new_buf, S.astype(h.dtype)


def rwkv_mixer(h, shift_buf, S0, mu, w0, w1, w2, a0, a1, a2, g1, g2, k_k, k_a, r_k,
               w_r, w_k, w_v, w_o, ln_w, ln_b):
    B, T, D = h.shape
    f32 = jnp.float32
    h_prev = jnp.concatenate([shift_buf.astype(h.dtype), h[:, :-1]], axis=1)
    xx = h_prev - h
    xr, xw, xk, xv, xa, xg = h[None] + xx[None] * mu[:, None, None, :]
    r = (xr @ w_r).astype(f32)
    k = (xk @ w_k).astype(f32)
    v = (xv @ w_v).astype(f32)
    w_log = -jax.nn.softplus(-(w0 + jnp.tanh(xw @ w1) @ w2).astype(f32)) - 0.5
    a = jax.nn.sigmoid((a0 + (xa @ a1) @ a2).astype(f32))
    gate = (jax.nn.sigmoid(xg @ g1) @ g2).astype(f32)
    kk = l2_normalize((k * k_k.astype(f32)).reshape(B, T, RWKV_HEADS, RWKV_N))
    k = k * (1.0 + (a - 1.0) * k_a.astype(f32))
    decay = jnp.exp(-jnp.exp(w_log))

    def heads(t):
        return t.reshape(B, T, RWKV_HEADS, RWKV_N)

    r, k, v, a, decay = heads(r), heads(k), heads(v), heads(a), heads(decay)

    def step(S, inp):
        d_t, kk_t, b_t, v_t, k_t, r_t = inp
        sa = jnp.einsum('bhij,bhj->bhi', S, -kk_t)
        S = S * d_t[:, :, None, :] + sa[..., :, None] * b_t[:, :, None, :] + v_t[..., :, None] * k_t[:, :, None, :]
        return S, jnp.einsum('bhij,bhj->bhi', S, r_t)

    seq = tuple(jnp.moveaxis(t, 1, 0) for t in (decay, kk, kk * a, v, k, r))
    S, y = lax.scan(step, S0.astype(f32), seq)
    y = jnp.moveaxis(y, 0, 1)
    mean = jnp.mean(y, axis=-1, keepdims=True)
    var = jnp.mean(jnp.square(y - mean), axis=-1, keepdims=True)
    y = ((y - mean) * lax.rsqrt(var + GN_EPS)).reshape(B, T, D) * ln_w.astype(f32) + ln_b.astype(f32)
    bonus = jnp.sum(r * k * r_k.astype(f32), axis=-1, keepdims=True) * v
    y = y + bonus.reshape(B, T, D)
    y = (y * gate).astype(h.dtype) @ w_o
    return y, h[:, -1:], S.astype(h.dtype)


def setup_inputs(seed: int = 0) -> dict:
    key = jax.random.key(seed)
    keys = jax.random.split(key, 48)
    it = iter(range(48))

    def nrm(shape, scale):
        return jax.random.normal(keys[next(it)], shape, jnp.float32) * scale

    def uni(shape, lo, hi):
        return jax.random.uniform(keys[next(it)], shape, jnp.float32, lo, hi)

    NG, NR, D = N_GDN_LAYERS, N_RWKV_LAYERS, D_MODEL
    dt = jnp.exp(uni((NG, GDN_HEADS), math.log(1e-3), math.log(1e-1)))
    return {
        'x_prompt': nrm((BATCH, SEQ, D), 1.0),
        'x_sample': nrm((DEC_BATCH, DEC_SEQ, D), 1.0),
        'cache_gdn_conv': nrm((NG, DEC_BATCH, CONV_W - 1, GDN_QKV), 1.0),
        'state_gdn': nrm((NG, DEC_BATCH, GDN_HEADS, GDN_DK, GDN_DV), 0.1),
        'cache_rwkv_shift': nrm((NR, DEC_BATCH, 1, D), 1.0),
        'state_rwkv': nrm((NR, DEC_BATCH, RWKV_HEADS, RWKV_N, RWKV_N), 0.1),
        'meta_tokens': nrm((N_META, D), 1.0),
        'norm_mix': 1.0 + nrm((DEPTH, D), 0.02),
        'norm_ffn': 1.0 + nrm((DEPTH, D), 0.02),
        'norm_final': 1.0 + nrm((D,), 0.02),
        'gdn_w_in': nrm((NG, D, GDN_IN), D ** -0.5),
        'gdn_conv_w': nrm((NG, CONV_W, GDN_QKV), CONV_W ** -0.5),
        'gdn_a_log': jnp.log(uni((NG, GDN_HEADS), 1.0, 16.0)),
        'gdn_dt_bias': dt + jnp.log(-jnp.expm1(-dt)),
        'gdn_o_norm': 1.0 + nrm((NG, GDN_DV), 0.02),
        'gdn_w_out': nrm((NG, GDN_VDIM, D), GDN_VDIM ** -0.5),
        'rwkv_mu': uni((NR, N_MU, D), 0.0, 1.0),
        'rwkv_w0': uni((NR, D), -5.5, -0.5),
        'rwkv_w1': nrm((NR, D, D_DECAY_LORA), D ** -0.5),
        'rwkv_w2': nrm((NR, D_DECAY_LORA, D), 0.1 * D_DECAY_LORA ** -0.5),
        'rwkv_a0': nrm((NR, D), 0.1),
        'rwkv_a1': nrm((NR, D, D_AAA_LORA), D ** -0.5),
        'rwkv_a2': nrm((NR, D_AAA_LORA, D), 0.1 * D_AAA_LORA ** -0.5),
        'rwkv_g1': nrm((NR, D, D_GATE_LORA), D ** -0.5),
        'rwkv_g2': nrm((NR, D_GATE_LORA, D), D_GATE_LORA ** -0.5),
        'rwkv_k_k': 0.85 + nrm((NR, D), 0.02),
        'rwkv_k_a': 1.0 + nrm((NR, D), 0.02),
        'rwkv_r_k': nrm((NR, RWKV_HEADS, RWKV_N), 0.1),
        'rwkv_w_r': nrm((NR, D, D), D ** -0.5),
        'rwkv_w_k': nrm((NR, D, D), D ** -0.5),
        'rwkv_w_v': nrm((NR, D, D), D ** -0.5),
        'rwkv_w_o': nrm((NR, D, D), D ** -0.5),
        'rwkv_ln_w': 1.0 + nrm((NR, D), 0.02),
        'rwkv_ln_b': nrm((NR, D), 0.02),
        'ffn_w_gate': nrm((DEPTH, D, D_FF), D ** -0.5),
        'ffn_w_up': nrm((DEPTH, D, D_FF), D ** -0.5),
        'ffn_w_down': nrm((DEPTH, D_FF, D), D_FF ** -0.5),
    }


def reference(x_prompt, x_sample, cache_gdn_conv, state_gdn, cache_rwkv_shift, state_rwkv,
              meta_tokens, norm_mix, norm_ffn, norm_final,
              gdn_w_in, gdn_conv_w, gdn_a_log, gdn_dt_bias, gdn_o_norm, gdn_w_out,
              rwkv_mu, rwkv_w0, rwkv_w1, rwkv_w2, rwkv_a0, rwkv_a1, rwkv_a2, rwkv_g1, rwkv_g2,
              rwkv_k_k, rwkv_k_a, rwkv_r_k, rwkv_w_r, rwkv_w_k, rwkv_w_v, rwkv_w_o, rwkv_ln_w, rwkv_ln_b,
              ffn_w_gate, ffn_w_up, ffn_w_down):

    def run_trunk(h, n_lead, conv_bufs, gdn_states, shift_bufs, rwkv_states):
        new_conv, new_gdn, new_shift, new_rwkv = [], [], [], []
        for i in range(DEPTH):
            j = i // N_MIXERS
            hn = rms_norm(h, norm_mix[i])
            if i % N_MIXERS == 0:
                y, cb, S = gdn_mixer(hn, conv_bufs[j], gdn_states[j], gdn_w_in[j], gdn_conv_w[j],
                                     gdn_a_log[j], gdn_dt_bias[j], gdn_o_norm[j], gdn_w_out[j], n_lead)
                new_conv.append(cb)
                new_gdn.append(S)
            else:
                y, sb, S = rwkv_mixer(hn, shift_bufs[j], rwkv_states[j], rwkv_mu[j], rwkv_w0[j], rwkv_w1[j],
                                      rwkv_w2[j], rwkv_a0[j], rwkv_a1[j], rwkv_a2[j], rwkv_g1[j], rwkv_g2[j],
                                      rwkv_k_k[j], rwkv_k_a[j], rwkv_r_k[j], rwkv_w_r[j], rwkv_w_k[j],
                                      rwkv_w_v[j], rwkv_w_o[j], rwkv_ln_w[j], rwkv_ln_b[j])
                new_shift.append(sb)
                new_rwkv.append(S)
            h = h + y
            h = h + swiglu(rms_norm(h, norm_ffn[i]), ffn_w_gate[i], ffn_w_up[i], ffn_w_down[i])
        return (rms_norm(h, norm_final), jnp.stack(new_conv), jnp.stack(new_gdn),
                jnp.stack(new_shift), jnp.stack(new_rwkv))

    B, dt = x_prompt.shape[0], x_prompt.dtype
    meta = jnp.broadcast_to(meta_tokens.astype(dt)[None], (B, N_META, D_MODEL))
    h_p = jnp.concatenate([meta, x_prompt], axis=1)
    zero_conv = jnp.zeros((N_GDN_LAYERS, B, CONV_W - 1, GDN_QKV), dt)
    zero_gdn = jnp.zeros((N_GDN_LAYERS, B, GDN_HEADS, GDN_DK, GDN_DV), dt)
    zero_shift = jnp.zeros((N_RWKV_LAYERS, B, 1, D_MODEL), dt)
    zero_rwkv = jnp.zeros((N_RWKV_LAYERS, B, RWKV_HEADS, RWKV_N, RWKV_N), dt)
    out_p, p_gdn_conv, p_gdn_state, p_rwkv_shift, p_rwkv_state = run_trunk(
        h_p, N_META, zero_conv, zero_gdn, zero_shift, zero_rwkv)
    y_prompt = out_p[:, N_META:]

    y_sample, s_gdn_conv, s_gdn_state, s_rwkv_shift, s_rwkv_state = run_trunk(
        x_sample, 0, cache_gdn_conv, state_gdn, cache_rwkv_shift, state_rwkv)

    return (y_prompt, y_sample, p_gdn_conv, p_gdn_state, p_rwkv_shift, p_rwkv_state,
            s_gdn_conv, s_gdn_state, s_rwkv_shift, s_rwkv_state)
```

```python
import math
from contextlib import ExitStack

import numpy as np
import concourse.bass as bass
import concourse.mybir as mybir
from concourse.bass_utils import run_bass_kernel_spmd

F32 = mybir.dt.float32
BF16 = mybir.dt.bfloat16
AF = mybir.ActivationFunctionType
ALU = mybir.AluOpType

D = 1024
SEQ = 8192
NMETA = 16
TP = SEQ + NMETA
DEC = 64
NROWS = TP + DEC
DFF = 2816
NFC = DFF // 128
GIN = 4112
EPS = 1e-6
GN_EPS = 64e-5
NEG = -1.0e5

ENGS = ("pe", "act", "dve", "pool", "sp")
DBG_A = False
DBG_B = False


class Reg:
    __slots__ = ("name", "w", "r", "psum")

    def __init__(self, name, psum=False):
        self.name = name
        self.w = None
        self.r = []
        self.psum = psum


class Prog:
    def __init__(self, sems):
        self.free_sems = list(sems)
        self.q = {e: [] for e in ENGS}
        self.cnt = {e: 0 for e in ENGS}
        self.esem = {e: self.free_sems.pop() for e in ENGS}
        self.waited = {e: {} for e in ENGS}
        self.dsem = {}

    def _waits(self, eng, deps):
        best = {}
        for (sk, sem, val) in deps:
            if sk == "pe" and eng == "pe":
                continue
            if self.waited[eng].get(sk, 0) >= val:
                continue
            if best.get(sk, (None, 0))[1] < val:
                best[sk] = (sem, val)
        out = []
        for sk, (sem, val) in best.items():
            self.waited[eng][sk] = val
            out.append((sem, val))
        return out

    def op(self, eng, fn, reads=(), writes=(), dma=None, skip_waw=False):
        deps = []
        for b in list(reads) + ([] if skip_waw else list(writes)):
            if b.w is not None:
                deps.append(b.w)
        for b in writes:
            deps.extend(b.r)
        for b in reads:
            if b.psum:
                deps.extend(ev_ for ev_ in b.r if ev_[0] != eng)
        waits = self._waits(eng, deps)
        if dma is None:
            self.cnt[eng] += 1
            ev = (eng, self.esem[eng], self.cnt[eng])
            inc = (self.esem[eng], 1)
        else:
            if dma not in self.dsem:
                self.dsem[dma] = [self.free_sems.pop(), 0]
            ent = self.dsem[dma]
            ent[1] += 16
            ev = ("d:" + dma, ent[0], ent[1])
            inc = (ent[0], 16)
        self.q[eng].append((waits, fn, inc))
        for b in writes:
            b.w = ev
            b.r = []
        for b in reads:
            if b not in writes:
                b.r.append(ev)
        return ev

    def barrier(self):
        evs = [(e, self.esem[e], self.cnt[e]) for e in ENGS if self.cnt[e] > 0]
        evs += [("d:" + k, v[0], v[1]) for k, v in self.dsem.items() if v[1] > 0]
        for e in ENGS:
            ws = []
            for (sk, sem, val) in evs:
                if sk == e:
                    continue
                if self.waited[e].get(sk, 0) >= val:
                    continue
                self.waited[e][sk] = val
                ws.append((sem, val))
            if ws:
                self.q[e].append((ws, None, None))

    def replay(self, eng, e):
        for (waits, fn, inc) in self.q[eng]:
            for (sem, val) in waits:
                e.wait_ge(sem, val)
            if fn is not None:
                ins = fn(e)
                ins.then_inc(inc[0], inc[1])


class Arena:
    def __init__(self, ap, nwords):
        self.ap = ap
        self.n = nwords
        self.off = 0

    def alloc(self, shape, dtype):
        free = 1
        for s in shape[1:]:
            free *= s
        words = free if dtype == F32 else (free + 1) // 2
        words = (words + 7) // 8 * 8
        assert self.off + words <= self.n, f"arena overflow {self.off}+{words}>{self.n}"
        v = self.ap[:, self.off:self.off + words]
        self.off += words
        if dtype != F32:
            v = v.bitcast(dtype)
        v = v[:, 0:free]
        if len(shape) == 3:
            v = v.rearrange("p (a b) -> p a b", a=shape[1], b=shape[2])
        elif len(shape) == 4:
            v = v.rearrange("p (a b c) -> p a b c", a=shape[1], b=shape[2], c=shape[3])
        return v


def build_program(SEQ=SEQ, phases=(1, 2, 3, 4)):
    TP = SEQ + NMETA
    NROWS = TP + DEC
    nc = bass.Bass("TRN2", target_bir_lowering=False)

    def din(name, shape):
        return nc.dram_tensor(name, list(shape), F32, kind="ExternalInput").ap()

    def dout(name, shape):
        return nc.dram_tensor(name, list(shape), F32, kind="ExternalOutput").ap()

    x_prompt = din("x_prompt", [SEQ, D])
    x_sample = din("x_sample", [DEC, D])
    cache_gdn_conv = din("cache_gdn_conv", [3, 3072])
    state_gdn = din("state_gdn", [8, 128, 128])
    cache_rwkv_shift = din("cache_rwkv_shift", [1, D])
    state_rwkv = din("state_rwkv", [16, 64, 64])
    meta_tokens = din("meta_tokens", [NMETA, D])
    norm_mix = din("norm_mix", [2, D])
    norm_ffn = din("norm_ffn", [2, D])
    norm_final = din("norm_final", [D])
    gdn_w_in = din("gdn_w_in", [D, GIN])
    gdn_conv_w = din("gdn_conv_w", [4, 3072])
    gdn_a_log = din("gdn_a_log", [8])
    gdn_dt_bias = din("gdn_dt_bias", [8])
    gdn_o_norm = din("gdn_o_norm", [128])
    gdn_w_out = din("gdn_w_out", [D, D])
    rwkv_mu = din("rwkv_mu", [6, D])
    rwkv_w0 = din("rwkv_w0", [D])
    rwkv_w1 = din("rwkv_w1", [D, 64])
    rwkv_w2 = din("rwkv_w2", [64, D])
    rwkv_a0 = din("rwkv_a0", [D])
    rwkv_a1 = din("rwkv_a1", [D, 64])
    rwkv_a2 = din("rwkv_a2", [64, D])
    rwkv_g1 = din("rwkv_g1", [D, 160])
    rwkv_g2 = din("rwkv_g2", [160, D])
    rwkv_k_k = din("rwkv_k_k", [D])
    rwkv_k_a = din("rwkv_k_a", [D])
    rwkv_r_k = din("rwkv_r_k", [D])
    rwkv_w_r = din("rwkv_w_r", [D, D])
    rwkv_w_k = din("rwkv_w_k", [D, D])
    rwkv_w_v = din("rwkv_w_v", [D, D])
    rwkv_w_o = din("rwkv_w_o", [D, D])
    rwkv_ln_w = din("rwkv_ln_w", [D])
    rwkv_ln_b = din("rwkv_ln_b", [D])
    ffn_w_gate = din("ffn_w_gate", [2, D, DFF])
    ffn_w_up = din("ffn_w_up", [2, D, DFF])
    ffn_w_down = din("ffn_w_down", [2, DFF, D])

    y_prompt = dout("y_prompt", [SEQ, D])
    y_sample = dout("y_sample", [DEC, D])
    o_gdn_conv = {"p": dout("p_gdn_conv", [3, 3072]), "s": dout("s_gdn_conv", [3, 3072])}
    o_gdn_state = {"p": dout("p_gdn_state", [8, 128, 128]), "s": dout("s_gdn_state", [8, 128, 128])}
    o_rwkv_shift = {"p": dout("p_rwkv_shift", [1, D]), "s": dout("s_rwkv_shift", [1, D])}
    o_rwkv_state = {"p": dout("p_rwkv_state", [16, 64, 64]), "s": dout("s_rwkv_state", [16, 64, 64])}

    hs = [nc.dram_tensor(f"hscr{i}", [NROWS, D], F32, kind="Internal").ap() for i in range(3)]

    tiles = [("p", 0, NMETA, ("meta", 0))]
    for k in range(SEQ // 128):
        tiles.append(("p", NMETA + 128 * k, 128, ("x", k)))
    tiles.append(("s", TP, DEC, ("samp", 0)))

    with ExitStack() as es:
        sems = [es.enter_context(nc.semaphore(f"sm{i}")) for i in range(100)]
        P = Prog(sems)
        NW = 53100
        arena_t = es.enter_context(nc.sbuf_tensor("arena", [128, NW], F32))
        A = Arena(arena_t[:], NW)
        psf = [es.enter_context(nc.psum_tensor(f"ps{i}", [128, 512], F32)) for i in range(8)]
        psr = [Reg(f"ps{i}", psum=True) for i in range(8)]
        pcount = [0]

        def bank():
            i = pcount[0] % 8
            pcount[0] += 1
            return psf[i][:], psr[i]

        def bankb(b):
            return b.bitcast(BF16)

        rr = [0]

        def ev2():
            rr[0] += 1
            return "act" if rr[0] % 2 else "dve"

        def mm(out, lhsT, rhs, start, stop, reads, writes):
            P.op("pe", lambda e: e.matmul(out, lhsT, rhs, start=start, stop=stop), reads, writes)

        def tr(out, in_, ident, reads, writes):
            P.op("pe", lambda e: e.transpose(out, in_, ident), reads, writes)

        def act(out, in_, func, reads, writes, bias=None, scale=None, accum_out=None):
            kw = {}
            if bias is not None:
                kw["bias"] = bias
            if scale is not None:
                kw["scale"] = scale
            if accum_out is not None:
                kw["accum_out"] = accum_out
            P.op("act", lambda e: e.activation(out, in_, func, **kw), reads, writes)

        def tt(eng, out, in0, in1, op, reads, writes):
            P.op(eng, lambda e: e.tensor_tensor(out, in0, in1, op), reads, writes)

        def ts(eng, out, in0, s1, op0, reads, writes, s2=None, op1=None):
            if op1 is None:
                P.op(eng, lambda e: e.tensor_scalar(out, in0, s1, None, op0), reads, writes)
            else:
                P.op(eng, lambda e: e.tensor_scalar(out, in0, s1, s2, op0, op1), reads, writes)

        def stt(out, in0, scalar, in1, op0, op1, reads, writes):
            P.op("dve", lambda e: e.scalar_tensor_tensor(out, in0, scalar, in1, op0, op1), reads, writes)

        def cp(eng, out, in_, reads, writes):
            if eng == "act":
                P.op("act", lambda e: e.copy(out, in_), reads, writes)
            else:
                P.op(eng, lambda e: e.tensor_copy(out, in_), reads, writes)

        def recip(out, in_, reads, writes):
            P.op("dve", lambda e: e.reciprocal(out, in_), reads, writes)

        def memset(eng, ap, val, writes):
            P.op(eng, lambda e: e.memset(ap, val), (), writes)

        def dma(eng, out, in_, reads, writes, key, skip_waw=False, **kw):
            P.op(eng, lambda e: e.dma_start(out=out, in_=in_, **kw), reads, writes, dma=key, skip_waw=skip_waw)

        Rc = Reg("consts")
        ident_f = A.alloc([128, 128], F32)
        ident_b = A.alloc([128, 128], BF16)
        ones_f = A.alloc([128, 128], F32)
        ones_b = A.alloc([128, 128], BF16)
        zeros_f = A.alloc([128, 128], F32)
        imask_u = A.alloc([128, 128], F32)
        smask_u = A.alloc([128, 128], F32)
        smask_l = A.alloc([128, 128], F32)
        bdmask = A.alloc([128, 128], F32)
        offmask = A.alloc([128, 128], F32)
        negmask = A.alloc([128, 128], F32)
        eps_t = A.alloc([128, 4], F32)
        memset("pool", ones_f, 1.0, [Rc])
        memset("pool", zeros_f, 0.0, [Rc])
        memset("pool", eps_t[:, 0:1], EPS, [Rc])
        memset("pool", eps_t[:, 1:2], 1.0, [Rc])
        memset("pool", eps_t[:, 2:3], GN_EPS, [Rc])
        memset("pool", eps_t[:, 3:4], 0.0, [Rc])

        def asel(out, in_, cmp_op, fill, base, cm, step):
            P.op("pool", lambda e: e.affine_select(out, in_, [[step, 128]], cmp_op, fill,
                                                   base=base, channel_multiplier=cm), [Rc], [Rc])

        asel(imask_u, ones_f, ALU.is_ge, 0.0, 0, -1, 1)
        asel(smask_u, ones_f, ALU.is_gt, 0.0, 0, -1, 1)
        asel(smask_l, ones_f, ALU.is_gt, 0.0, 0, 1, -1)
        asel(ident_f, ones_f, ALU.is_equal, 0.0, 0, -1, 1)
        asel(negmask, zeros_f, ALU.is_ge, NEG, 0, -1, 1)
        cp("pool", bdmask, smask_u, [Rc], [Rc])
        memset("pool", bdmask[0:64, 64:128], 0.0, [Rc])
        memset("pool", offmask, 0.0, [Rc])
        memset("pool", offmask[0:64, 64:128], 1.0, [Rc])
        cp("pool", ident_b, ident_f, [Rc], [Rc])
        cp("pool", ones_b, ones_f, [Rc], [Rc])
        persist_off = A.off

        def load_w(dst, src, K, key, reg):
            for kc in range(K):
                dma("pool", dst[:, kc, :], src[kc * 128:(kc + 1) * 128, :], [], [reg], key,
                    skip_waw=(kc > 0), max_dma_last_dim=4096)

        def load_featmajor(dst, src_vec, C, stage, stage_reg, dst_reg, key):
            dma("sp", stage[0:C, 0:128], src_vec.rearrange("(c p) -> c p", p=128), [], [stage_reg], key)
            b, br = bank()
            tr(b[:, 0:C], stage[0:C, 0:128], ident_f[0:C, 0:C], [stage_reg, Rc], [br])
            cp("dve", dst, b[:, 0:C], [br], [dst_reg])

        def rmsnorm_T(xt, Rx, n, gfm, Rg, hn, Rhn, hnT, RhnT, small, Rsm, junk, Rjunk):
            act(junk[:n, :], xt[:n, :], AF.Square, [Rx], [Rjunk, Rsm], accum_out=small[:n, 0:1])
            act(small[:n, 1:2], small[:n, 0:1], AF.Sqrt, [Rsm, Rc], [Rsm], bias=eps_t[:n, 0:1], scale=1.0 / D)
            recip(small[:n, 2:3], small[:n, 1:2], [Rsm], [Rsm])
            ts("dve", hn[:n, :], xt[:n, :], small[:n, 2:3], ALU.mult, [Rx, Rsm], [Rhn])
            b, br = bank()
            bb = bankb(b).rearrange("p (a b) -> p a b", a=8, b=128)
            for kc in range(8):
                tr(bb[:, kc, :n], hn[:n, kc * 128:(kc + 1) * 128], ident_b[:n, :n], [Rhn, Rc], [br])
            for kc in range(8):
                eng = ev2()
                if eng == "act":
                    act(hnT[:, kc, :n], bb[:, kc, :n], AF.Copy, [br, Rg], [RhnT], scale=gfm[:, kc:kc + 1])
                else:
                    ts("dve", hnT[:, kc, :n], bb[:, kc, :n], gfm[:, kc:kc + 1], ALU.mult, [br, Rg], [RhnT])

        def neumann_gen(Bsrc, BTsrc, RB, RBT, n, nf, PR, RPR, PT, RPT, G, res, psum_acc=False):
            cur = 0
            tt("pool", PR[0][:n, :, 1, :n], Bsrc, ident_f[:n, :n].unsqueeze(1).to_broadcast([n, G, n]), ALU.add,
               [RB, Rc], [RPR[0]])
            Pk = Bsrc
            PTk = BTsrc
            RPk, RPTk = RB, RBT
            Rk = PR[0][:n, :, 1, :n]
            RRk = RPR[0]
            for k in range(0, nf):
                last = (k == nf - 1)
                if k == 0:
                    if nf == 1:
                        break
                    nxt = 1 - cur
                    b1, b1r = bank()
                    b2, b2r = bank()
                    v1 = b1.rearrange("p (g c) -> p g c", g=4, c=128)
                    v2 = b2.rearrange("p (g c) -> p g c", g=4, c=128)
                    for hh in range(G):
                        mm(v1[:n, hh, :n], PTk[:, hh, :], Pk[:, hh, :], True, True, [RPk, RPTk], [b1r])
                        mm(v2[:n, hh, :n], Pk[:, hh, :], PTk[:, hh, :], True, True, [RPk, RPTk], [b2r])
                    cp("act", PR[nxt][:n, :, 0, :n], v1[:n, 0:G, :n], [b1r], [RPR[nxt]])
                    cp("dve", PT[nxt][:n, :, :n], v2[:n, 0:G, :n], [b2r], [RPT[nxt]])
                    cp("pool", PR[nxt][:n, :, 1, :n], Rk, [RRk], [RPR[nxt]])
                    cur = nxt
                    Pk = PR[cur][:n, :, 0, :n]
                    PTk = PT[cur][:n, :, :n]
                    RPk, RPTk = RPR[cur], RPT[cur]
                    Rk = PR[cur][:n, :, 1, :n]
                    RRk = RPR[cur]
                    yield
                    continue
                nxt = 1 - cur
                if not last:
                    ba, bar = bank()
                    bb_, bbr = bank()
                    bc, bcr = bank()
                    va = ba.rearrange("p (g t c) -> p g t c", g=2, t=2, c=128)
                    vb = bb_.rearrange("p (g t c) -> p g t c", g=2, t=2, c=128)
                    vc = bc.rearrange("p (g c) -> p g c", g=4, c=128)
                    for hh in range(G):
                        tgt, tgr = (va, bar) if hh < 2 else (vb, bbr)
                        if n == 128:
                            mm(tgt[:n, hh % 2, :, :n], PTk[:, hh, :], PR[cur][:n, hh, :, :n], True, not psum_acc,
                               [RPk, RPTk, RRk], [tgr])
                        else:
                            for t_ in range(2):
                                mm(tgt[:n, hh % 2, t_, :n], PTk[:, hh, :], PR[cur][:n, hh, t_, :n], True,
                                   not (psum_acc and t_ == 1), [RPk, RPTk, RRk], [tgr])
                        if psum_acc:
                            mm(tgt[:n, hh % 2, 1, :n], ident_b[:n, :n], PR[cur][:n, hh, 1, :n], False, True,
                               [Rc, RRk], [tgr])
                        mm(vc[:n, hh, :n], Pk[:, hh, :], PTk[:, hh, :], True, True, [RPk, RPTk], [bcr])
                    for (vv, vr, h0) in ((va, bar, 0), (vb, bbr, 2)):
                        gcount = min(2, G - h0)
                        if gcount <= 0:
                            continue
                        cp("act", PR[nxt][:n, h0:h0 + gcount, 0, :n], vv[:n, 0:gcount, 0, :n], [vr], [RPR[nxt]])
                        if psum_acc:
                            cp("act", PR[nxt][:n, h0:h0 + gcount, 1, :n], vv[:n, 0:gcount, 1, :n], [vr], [RPR[nxt]])
                        else:
                            tt("dve", PR[nxt][:n, h0:h0 + gcount, 1, :n], vv[:n, 0:gcount, 1, :n],
                               PR[cur][:n, h0:h0 + gcount, 1, :n], ALU.add, [vr, RPR[cur]], [RPR[nxt]])
                    cp("act", PT[nxt][:n, :, :n], vc[:n, 0:G, :n], [bcr], [RPT[nxt]])
                else:
                    ba, bar = bank()
                    va = ba.rearrange("p (g c) -> p g c", g=4, c=128)
                    for hh in range(G):
                        mm(va[:n, hh, :n], PTk[:, hh, :], PR[cur][:n, hh, 1, :n], True, not psum_acc, [RPTk, RRk], [bar])
                        if psum_acc:
                            mm(va[:n, hh, :n], ident_b[:n, :n], PR[cur][:n, hh, 1, :n], False, True, [Rc, RRk], [bar])
                    if psum_acc:
                        cp("act", PR[nxt][:n, :, 1, :n], va[:n, 0:G, :n], [bar], [RPR[nxt]])
                    else:
                        tt("dve", PR[nxt][:n, :, 1, :n], va[:n, 0:G, :n], PR[cur][:n, :, 1, :n], ALU.add,
                           [bar, RPR[cur]], [RPR[nxt]])
                cur = nxt
                Pk = PR[cur][:n, :, 0, :n]
                PTk = PT[cur][:n, :, :n]
                RPk, RPTk = RPR[cur], RPT[cur]
                Rk = PR[cur][:n, :, 1, :n]
                RRk = RPR[cur]
                yield
            res.append((Rk, RRk))

        def neumann(Bsrc, BTsrc, RB, RBT, n, nf, PR, RPR, PT, RPT, G):
            res = []
            for _ in neumann_gen(Bsrc, BTsrc, RB, RBT, n, nf, PR, RPR, PT, RPT, G, res):
                pass
            return res[0]

        def phase_gdn():
            A.off = persist_off
            Rw = Reg("w_in")
            Rwo = Reg("w_out")
            w_in = A.alloc([128, 8, GIN], BF16)
            w_out = A.alloc([128, 8, D], BF16)
            diag = A.alloc([128, 96, 128], BF16)
            Rdiag = Reg("diag")
            load_w(w_in, gdn_w_in, 8, "w_in", Rw)
            load_w(w_out, gdn_w_out, 8, "w_out", Rwo)

            stage = A.alloc([128, 128], F32)
            Rst = Reg("stage")
            gfm = A.alloc([128, 8], F32)
            Rg = Reg("gfm")
            load_featmajor(gfm, norm_mix[0], 8, stage, Rst, Rg, "st")
            cw = A.alloc([128, 96], F32)
            Rcw = Reg("cw")
            dma("sp", stage[0:96, :], gdn_conv_w.rearrange("t (c p) -> (t c) p", p=128), [], [Rst], "st")
            b, br = bank()
            tr(b[:, 0:96], stage[0:96, :], ident_f[0:96, 0:96], [Rst, Rc], [br])
            cp("dve", cw, b[:, 0:96], [br], [Rcw])
            for idx in range(96):
                ts("pool" if idx % 2 else "dve", diag[:, idx, :], ident_f, cw[:, idx:idx + 1],
                   ALU.mult, [Rc, Rcw], [Rdiag])
            prm = A.alloc([128, 32], F32)
            Rprm = Reg("prm")
            dma("sp", prm[:, 0:8], gdn_a_log.partition_broadcast(128), [], [Rprm], "prm")
            dma("sp", prm[:, 8:16], gdn_dt_bias.partition_broadcast(128), [], [Rprm], "prm2")
            act(prm[:, 0:8], prm[:, 0:8], AF.Exp, [Rprm], [Rprm])
            ts("dve", prm[:, 0:8], prm[:, 0:8], -1.0, ALU.mult, [Rprm], [Rprm])
            onb = A.alloc([128, 128], F32)
            Ronb = Reg("onb")
            dma("sp", onb, gdn_o_norm.partition_broadcast(128), [], [Ronb], "onb")

            xt = [A.alloc([128, D], F32) for _ in range(2)]
            Rxt = [Reg("xt0"), Reg("xt1")]
            junk = A.alloc([128, D], BF16)
            Rjunk = Reg("junk")
            small = A.alloc([128, 8], F32)
            Rsm = Reg("small")
            hn = A.alloc([128, D], BF16)
            Rhn = Reg("hn")
            hnT = A.alloc([128, 8, 128], BF16)
            RhnT = Reg("hnT")
            yg = A.alloc([128, D], BF16)
            Ryg = Reg("yg")
            ygT = A.alloc([128, 8, 128], BF16)
            RygT = Reg("ygT")
            pre = A.alloc([128, 24, 132], BF16)
            Rpre = [Reg(f"pre{i}") for i in range(6)]
            nbT = A.alloc([128, 3, 24], F32)
            RnbT = Reg("nbT")
            tmpc = [A.alloc([128, 4, 128], F32) for _ in range(2)]
            Rtmpc = [Reg("tmpc0"), Reg("tmpc1")]
            sq = [A.alloc([128, 4, 128], BF16)]
            Rsq = [Reg("sq0")]
            tmp2 = [A.alloc([128, 4, 128], F32) for _ in range(2)]
            Rtmp2 = [Reg("tmp20"), Reg("tmp21")]
            qkn = A.alloc([128, 16, 128], BF16)
            Rqkn = [Reg(f"qkn{i}") for i in range(4)]
            vT = A.alloc([128, 8, 128], BF16)
            RvT = [Reg("vT0"), Reg("vT1")]
            kG = A.alloc([128, 8, 128], BF16)
            kd = A.alloc([128, 8, 128], BF16)
            vtok = A.alloc([128, 8, 128], BF16)
            Rktok = [Reg("ktok0"), Reg("ktok1")]
            zss = [A.alloc([128, D], BF16) for _ in range(2)]
            Rzss = [Reg("zs0"), Reg("zs1")]
            gs = A.alloc([128, 96], F32)
            Rgs = Reg("gs")
            Rgs2 = Reg("gs2")
            ET = A.alloc([128, 8, 128], F32)
            RET = [Reg(f"ET{i}") for i in range(4)]
            qgT = A.alloc([128, 8, 128], BF16)
            RqgT = [Reg(f"qgT{i}") for i in range(4)]
            qkT = A.alloc([128, 8, 128], BF16)
            RqkT = [Reg(f"qkT{i}") for i in range(4)]
            NSETS = 2
            NACT = NSETS
            SETS = []
            for si in range(NSETS):
                PRs = [A.alloc([128, 2, 2, 128], F32) for _ in range(2)]
                PTs = [A.alloc([128, 2, 128], F32) for _ in range(2)]
                SETS.append(dict(PR=PRs, RPR=[Reg(f"gPR{si}a"), Reg(f"gPR{si}b")], PT=PTs,
                                 RPT=[Reg(f"gPT{si}a"), Reg(f"gPT{si}b")],
                                 B0=A.alloc([128, 2, 128], F32), RB0=Reg(f"gB0{si}"),
                                 B0T=A.alloc([128, 2, 128], F32), RB0T=Reg(f"gB0T{si}"),
                                 tmpA=A.alloc([128, 2, 128], F32), RtmpA=Reg(f"gtmpA{si}"),
                                 OffT=A.alloc([128, 2, 128], F32), ROff=Reg(f"gOff{si}"),
                                 tmpE=A.alloc([128, 2, 128], F32), RtmpE=Reg(f"gtmpE{si}")))
            MinvT = A.alloc([128, 8, 128], BF16)
            RMinv = [Reg(f"Minv{i}") for i in range(4)]
            nsolw = A.alloc([128, 8, 128], BF16)
            Rnsolw = [Reg(f"nsolw{i}") for i in range(4)]
            vnew = A.alloc([128, 8, 128], BF16)
            Rvnew = [Reg(f"vnew{i}") for i in range(4)]
            S = A.alloc([128, 8, 128], F32)
            RS = [Reg(f"S{i}") for i in range(4)]
            Sb = A.alloc([128, 8, 128], BF16)
            RSb = [Reg(f"Sb{i}") for i in range(4)]
            o = A.alloc([128, D], F32)
            Ro = [Reg(f"o{i}") for i in range(4)]

            def load_x(ti):
                seq, row0, n, (kind, k) = tiles[ti]
                slot = ti % 2
                if kind == "meta":
                    src = meta_tokens
                elif kind == "x":
                    src = x_prompt[k * 128:(k + 1) * 128, :]
                else:
                    src = x_sample
                dma("sp", xt[slot][:n, :], src, [], [Rxt[slot]], f"xt{slot}")

            def head(ti):
                seq, row0, n, (kind, k) = tiles[ti]
                slot = ti % 2
                x_ = xt[slot]
                Rx = Rxt[slot]
                first = (ti == 0) or (kind == "samp")
                lastt = (ti == len(tiles) - 2) or (kind == "samp")
                prev_n = tiles[ti - 1][2] if ti > 0 else None
                zs, Rzs = zss[ti % 2], Rzss[ti % 2]
                if first and seq == "p":
                    memset("pool", S, 0.0, RS)
                    memset("pool", Sb, 0.0, RSb)
                    memset("pool", pre[:, :, 0:3], 0.0, Rpre)
                elif first:
                    dma("sp", S, state_gdn.rearrange("h d e -> d h e"), [], RS, "Sld")
                    cp("pool", Sb, S, RS, RSb)
                    if DBG_B:
                        memset("pool", pre[:, :, 0:3], 0.0, Rpre)
                    else:
                        dma("sp", stage[0:72, :], cache_gdn_conv.rearrange("r (c p) -> (r c) p", p=128), [], [Rst], "st")
                        b, br = bank()
                        mm(b[:, 0:72], stage[0:72, :], ident_f[0:72, 0:72], True, True, [Rst, Rc], [br])
                        cp("dve", pre[:, :, 0:3], b[:, 0:72].rearrange("p (r c) -> p c r", r=3, c=24), [br], Rpre)
                else:
                    cp("pool", pre[:, :, 0:3], pre[:, :, prev_n:prev_n + 3], Rpre, Rpre)

                yield
                rmsnorm_T(x_, Rx, n, gfm, Rg, hn, Rhn, hnT, RhnT, small, Rsm, junk, Rjunk)

                yield
                for g4 in range(6):
                    b, br = bank()
                    bv = b.rearrange("p (a b) -> p a b", a=4, b=128)
                    for j in range(4):
                        c = g4 * 4 + j
                        for kc in range(8):
                            mm(bv[:, j, :n], w_in[:, kc, c * 128:(c + 1) * 128], hnT[:, kc, :n], kc == 0, kc == 7,
                               [Rw, RhnT], [br])
                    cp(ev2(), pre[:, g4 * 4:(g4 + 1) * 4, 3:3 + n], bv[:, :, :n], [br], [Rpre[g4]])
                    if lastt and not DBG_A:
                        cp("dve", nbT.rearrange("p r c -> p c r")[:, g4 * 4:(g4 + 1) * 4, :], bv[:, :, n - 3:n], [br], [RnbT])
                    yield
                if lastt and not DBG_A:
                    b, br = bank()
                    mm(b[0:72, 0:128], nbT.rearrange("p r c -> p (r c)"), ident_f, True, True, [RnbT, Rc], [br])
                    cp("dve", stage[0:72, :], b[0:72, 0:128], [br], [Rst])
                    dma("sp", o_gdn_conv[seq].rearrange("r (c p) -> (r c) p", p=128), stage[0:72, :], [Rst], [], "nb3" + seq)
                for s in range(2):
                    b, br = bank()
                    for kc in range(8):
                        mm(b[:n, :], hnT[:, kc, :n], w_in[:, kc, 3072 + s * 512:3072 + (s + 1) * 512], kc == 0, kc == 7,
                           [Rw, RhnT], [br])
                    act(zs[:n, s * 512:(s + 1) * 512], b[:n, :], AF.Silu, [br], [Rzs])
                    yield
                b, br = bank()
                for kc in range(8):
                    mm(b[:n, 0:16], hnT[:, kc, :n], w_in[:, kc, 4096:4112], kc == 0, kc == 7, [Rw, RhnT], [br])
                act(gs[:n, 0:8], b[:n, 8:16], AF.Sigmoid, [br], [Rgs])
                ts("dve", gs[:n, 8:16], gs[:n, 0:8], -1.0, ALU.mult, [Rgs], [Rgs])
                tt("dve", gs[:n, 16:24], b[:n, 0:8], prm[:n, 8:16], ALU.add, [br, Rprm], [Rgs])
                yield
                stt(gs[:n, 24:32], gs[:n, 16:24], -1.0, gs[:n, 16:24], ALU.mult, ALU.max, [Rgs], [Rgs])
                act(gs[:n, 24:32], gs[:n, 24:32], AF.Exp, [Rgs], [Rgs], scale=-1.0)
                act(gs[:n, 24:32], gs[:n, 24:32], AF.Ln, [Rgs, Rc], [Rgs], bias=eps_t[:n, 1:2])
                stt(gs[:n, 16:24], gs[:n, 16:24], 0.0, gs[:n, 24:32], ALU.max, ALU.add, [Rgs], [Rgs])
                tt("dve", gs[:n, 16:24], gs[:n, 16:24], prm[:n, 0:8], ALU.mult, [Rgs, Rprm], [Rgs])

                yield
                b, br = bank()
                mm(b[:n, 0:8], imask_u[:n, :n], gs[:n, 16:24], True, True, [Rc, Rgs], [br])
                mm(b[:, 8:16], ones_f[:n, :], gs[:n, 16:24], True, True, [Rc, Rgs], [br])
                cp("dve", gs[:n, 32:40], b[:n, 0:8], [br], [Rgs])
                act(gs[:n, 40:48], b[:n, 0:8], AF.Exp, [br], [Rgs])
                cp("dve", gs[:, 48:56], b[:, 8:16], [br], [Rgs])
                act(gs[:, 56:64], b[:, 8:16], AF.Exp, [br], [Rgs])
                tt("dve", gs[:n, 64:72], gs[:n, 48:56], gs[:n, 32:40], ALU.subtract, [Rgs], [Rgs])
                act(gs[:n, 64:72], gs[:n, 64:72], AF.Exp, [Rgs], [Rgs])


                yield

            def mid(ti):
                seq, row0, n, (kind, k) = tiles[ti]
                slot = ti % 2
                x_ = xt[slot]
                Rx = Rxt[slot]
                first = (ti == 0) or (kind == "samp")
                lastt = (ti == len(tiles) - 2) or (kind == "samp")
                prev_n = tiles[ti - 1][2] if ti > 0 else None
                zs, Rzs = zss[ti % 2], Rzss[ti % 2]
                nf = int(math.log2(min(n, 64)))

                def cd_gen():
                    for half in range(2):
                        for g4 in (half, 2 + half, 4 + half):
                            b, br = bank()
                            bv = b.rearrange("p (a b) -> p a b", a=4, b=128)
                            for j in range(4):
                                c = g4 * 4 + j
                                for tap in range(4):
                                    mm(bv[:, j, :n], diag[:, tap * 24 + c, :], pre[:, c, tap:tap + n], tap == 0, tap == 3,
                                       [Rdiag, Rpre[g4]], [br])
                            sl = g4 % 2
                            act(tmpc[sl][:, :, :n], bv[:, :, :n], AF.Silu, [br], [Rtmpc[sl]])
                            if g4 < 4:
                                tt("pool", sq[0][:, :, :n], tmpc[sl][:, :, :n], tmpc[sl][:, :, :n], ALU.mult,
                                   [Rtmpc[sl]], [Rsq[0]])
                                b2, b2r = bank()
                                b2v = b2.rearrange("p (a b) -> p a b", a=4, b=128)
                                if n == 128:
                                    mm(b2, ones_b, sq[0].rearrange("p a b -> p (a b)"), True, True, [Rc, Rsq[0]], [b2r])
                                else:
                                    for j in range(4):
                                        mm(b2v[:, j, :n], ones_b, sq[0][:, j, :n], True, True, [Rc, Rsq[0]], [b2r])
                                act(tmp2[sl][:, :, :n], b2v[:, :, :n], AF.Sqrt, [b2r, Rc], [Rtmp2[sl]], bias=eps_t[:, 0:1])
                                recip(tmp2[sl][:, :, :n], tmp2[sl][:, :, :n], [Rtmp2[sl]], [Rtmp2[sl]])
                                if g4 < 2:
                                    stt(qkn[:, g4 * 4:(g4 + 1) * 4, :n], tmpc[sl][:, :, :n], 128.0 ** -0.5, tmp2[sl][:, :, :n],
                                        ALU.mult, ALU.mult, [Rtmpc[sl], Rtmp2[sl]], [Rqkn[g4]])
                                else:
                                    tt("pool", qkn[:, g4 * 4:(g4 + 1) * 4, :n], tmpc[sl][:, :, :n], tmp2[sl][:, :, :n], ALU.mult,
                                       [Rtmpc[sl], Rtmp2[sl]], [Rqkn[g4]])
                            else:
                                cp("pool", vT[:, half * 4:(half + 1) * 4, :n], tmpc[sl][:, :, :n], [Rtmpc[sl]], [RvT[half]])
                            yield None
                        hs_ = slice(half * 4, half * 4 + 4)
                        b, br = bank()
                        bb = bankb(b).rearrange("p (a b) -> p a b", a=8, b=128)
                        for hh in range(4):
                            tr(bb[:n, hh, :], qkn[:, 8 + half * 4 + hh, :n], ident_b, [Rqkn[2 + half], Rc], [br])
                        bV2, bV2r = bank()
                        bbv = bankb(bV2).rearrange("p (a b) -> p a b", a=8, b=128)
                        for hh in range(4):
                            tr(bbv[:n, hh, :], vT[:, half * 4 + hh, :n], ident_b, [RvT[half], Rc], [bV2r])
                        tt("dve", kG[:n, hs_, :], bb[:n, 0:4, :], gs[:n, 40 + half * 4:44 + half * 4].unsqueeze(2).to_broadcast([n, 4, 128]),
                           ALU.mult, [br, Rgs], [Rktok[half]])
                        tt("dve", kd[:n, hs_, :], bb[:n, 0:4, :], gs[:n, 64 + half * 4:68 + half * 4].unsqueeze(2).to_broadcast([n, 4, 128]),
                           ALU.mult, [br, Rgs], [Rktok[half]])
                        cp("act", vtok[:n, hs_, :], bbv[:n, 0:4, :], [bV2r], [Rktok[half]])
                        yield half

                def f_gen(p, S_):
                    h0 = p * 2
                    half = p // 2
                    PR, RPR, PT, RPT = S_["PR"], S_["RPR"], S_["PT"], S_["RPT"]
                    B0, RB0, B0T, RB0T = S_["B0"], S_["RB0"], S_["B0T"], S_["RB0T"]
                    tmpA, RtmpA, OffT, ROff, tmpE, RtmpE = S_["tmpA"], S_["RtmpA"], S_["OffT"], S_["ROff"], S_["tmpE"], S_["RtmpE"]
                    Rq, Rk_ = Rqkn[half], Rqkn[2 + half]
                    bG, bGr = bank()
                    vG = bG.rearrange("p (g c) -> p g c", g=4, c=128)
                    for hh in range(2):
                        h = h0 + hh
                        mm(vG[:, hh, :n], gs[:n, 16 + h:17 + h].to_broadcast([n, 128]), imask_u[:n, :n], True, True,
                           [Rgs, Rc], [bGr])
                    for hh in range(2):
                        h = h0 + hh
                        stt(ET[:n, h, :n], vG[:n, hh, :n], gs[:n, 32 + h:33 + h], negmask[:n, :n], ALU.subtract, ALU.add,
                            [bGr, Rgs, Rc], [RET[p]])
                    act(ET[:n, h0:h0 + 2, :n], ET[:n, h0:h0 + 2, :n], AF.Exp, [RET[p]], [RET[p]])
                    act(tmpE[:, :, :n], vG[:, 0:2, :n], AF.Exp, [bGr], [RtmpE])
                    tt("pool", qgT[:, h0:h0 + 2, :n], qkn[:, h0:h0 + 2, :n], tmpE[:, :, :n], ALU.mult, [Rq, RtmpE], [RqgT[p]])
                    yield
                    bK, bKr = bank()
                    vK = bK.rearrange("p (g c) -> p g c", g=4, c=128)
                    for hh in range(2):
                        h = h0 + hh
                        mm(vK[:n, hh, :n], qkn[:, 8 + h, :n], qkn[:, 8 + h, :n], True, True, [Rk_], [bKr])
                        mm(vK[:n, 2 + hh, :n], qkn[:, 8 + h, :n], qkn[:, h, :n], True, True, [Rk_, Rq], [bKr])
                    for hh in range(2):
                        h = h0 + hh
                        stt(tmpA[:n, hh, :n], vK[:n, hh, :n], gs[:n, 8 + h:9 + h], ET[:n, h, :n], ALU.mult, ALU.mult,
                            [bKr, Rgs, RET[p]], [RtmpA])
                    tt("dve", qkT[:n, h0:h0 + 2, :n], vK[:n, 2:4, :n], ET[:n, h0:h0 + 2, :n], ALU.mult, [bKr, RET[p]], [RqkT[p]])
                    tt("pool", B0[:n, :, :n], tmpA[:n, :, :n], bdmask[:n, :n].unsqueeze(1).to_broadcast([n, 2, n]), ALU.mult,
                       [RtmpA, Rc], [RB0])
                    if n == 128:
                        tt("pool", OffT[:n, :, :n], tmpA[:n, :, :n], offmask[:n, :n].unsqueeze(1).to_broadcast([n, 2, n]), ALU.mult,
                           [RtmpA, Rc], [ROff])
                    yield
                    bT, bTr = bank()
                    vT_ = bT.rearrange("p (g c) -> p g c", g=4, c=128)
                    for hh in range(2):
                        mm(vT_[:n, hh, :n], B0[:n, hh, :n], ident_f[:n, :n], True, True, [RB0, Rc], [bTr])
                    cp("act", B0T[:n, :, :n], vT_[:n, 0:2, :n], [bTr], [RB0T])
                    yield
                    res = []
                    for _ in neumann_gen(B0[:n, :, :n], B0T[:n, :, :n], RB0, RB0T, n, nf, PR, RPR, PT, RPT, 2, res):
                        yield
                    Rfin, RRfin = res[0]
                    if n == 128:
                        Dm, RDm = PT[0], RPT[0]
                        T1, RT1 = PT[1], RPT[1]
                        T1T, RT1T = tmpA, RtmpA
                        b1, b1r = bank()
                        v1 = b1.rearrange("p (g c) -> p g c", g=4, c=128)
                        for hh in range(2):
                            mm(v1[:, hh, :], Rfin[:, hh, :], ident_f, True, True, [RRfin, Rc], [b1r])
                        cp("act", Dm, v1[:, 0:2, :], [b1r], [RDm])
                        yield
                        b2, b2r = bank()
                        v2 = b2.rearrange("p (g c) -> p g c", g=4, c=128)
                        for hh in range(2):
                            mm(v2[:, hh, :], Dm[:, hh, :], OffT[:, hh, :], True, True, [RDm, ROff], [b2r])
                        cp("dve", T1, v2[:, 0:2, :], [b2r], [RT1])
                        yield
                        b3, b3r = bank()
                        v3 = b3.rearrange("p (g c) -> p g c", g=4, c=128)
                        for hh in range(2):
                            mm(v3[:, hh, :], T1[:, hh, :], ident_f, True, True, [RT1, Rc], [b3r])
                        cp("act", T1T, v3[:, 0:2, :], [b3r], [RT1T])
                        yield
                        b4, b4r = bank()
                        v4 = b4.rearrange("p (g c) -> p g c", g=4, c=128)
                        for hh in range(2):
                            mm(v4[:, hh, :], T1T[:, hh, :], Rfin[:, hh, :], True, True, [RT1T, RRfin], [b4r])
                        tt("dve", MinvT[:, h0:h0 + 2, :], v4[:, 0:2, :], Rfin, ALU.add, [b4r, RRfin], [RMinv[p]])
                    else:
                        cp("dve", MinvT[:n, h0:h0 + 2, :n], Rfin, [RRfin], [RMinv[p]])
                    yield
                    bW, bWr = bank()
                    vW = bW.rearrange("p (g c) -> p g c", g=4, c=128)
                    for hh in range(2):
                        h = h0 + hh
                        mm(vW[:, hh, :n], kG[:n, h, :], MinvT[:n, h, :n], True, True, [Rktok[half], RMinv[p]], [bWr])
                    ts("dve", nsolw[:, h0:h0 + 2, :n], vW[:, 0:2, :n], -1.0, ALU.mult, [bWr], [Rnsolw[p]])
                    yield
                    bV, bVr = bank()
                    vV = bV.rearrange("p (g c) -> p g c", g=4, c=128)
                    for hh in range(2):
                        h = h0 + hh
                        mm(vV[:n, hh, :], MinvT[:n, h, :n], vtok[:n, h, :], True, False, [RMinv[p], Rktok[half]], [bVr])
                        mm(vV[:n, hh, :], nsolw[:, h, :n], Sb[:, h, :], False, True, [Rnsolw[p], RSb[p]], [bVr])
                    tt("dve", vnew[:n, h0:h0 + 2, :], vV[:n, 0:2, :],
                       gs[:n, h0:h0 + 2].unsqueeze(2).to_broadcast([n, 2, 128]), ALU.mult, [bVr, Rgs], [Rvnew[p]])
                    yield
                    bO, bOr = bank()
                    vO = bO.rearrange("p (g c) -> p g c", g=4, c=128)
                    for hh in range(2):
                        h = h0 + hh
                        mm(vO[:n, hh, :], qgT[:, h, :n], Sb[:, h, :], True, False, [RqgT[p], RSb[p]], [bOr])
                        mm(vO[:n, hh, :], qkT[:n, h, :n], vnew[:n, h, :], False, True, [RqkT[p], Rvnew[p]], [bOr])
                    cp("act", o[:n, h0 * 128:(h0 + 2) * 128], bO[:n, 0:256], [bOr], [Ro[p]])
                    bS, bSr = bank()
                    vS = bS.rearrange("p (g c) -> p g c", g=4, c=128)
                    for hh in range(2):
                        h = h0 + hh
                        mm(vS[:, hh, :], kd[:n, h, :], vnew[:n, h, :], True, True, [Rktok[half], Rvnew[p]], [bSr])
                    for hh in range(2):
                        h = h0 + hh
                        stt(S[:, h, :], S[:, h, :], gs[:, 56 + h:57 + h], vS[:, hh, :], ALU.mult, ALU.add,
                            [RS[p], Rgs, bSr], [RS[p]])
                    cp("pool", Sb[:, h0:h0 + 2, :], S[:, h0:h0 + 2, :], [RS[p]], [RSb[p]])

                cg = cd_gen()
                ready = set()
                next_p = 0
                active = []
                cd_alive = True
                while cd_alive or active or next_p < 4:
                    if cd_alive:
                        try:
                            r = next(cg)
                            if r is not None:
                                ready.add(r)
                        except StopIteration:
                            cd_alive = False
                    while next_p < 4 and (next_p // 2) in ready and len(active) < NACT and (not DBG_B or not cd_alive):
                        active.append(f_gen(next_p, SETS[next_p % NSETS]))
                        next_p += 1
                    for g in list(active):
                        try:
                            next(g)
                        except StopIteration:
                            active.remove(g)


                if lastt:
                    dma("sp", o_gdn_state[seq].rearrange("h d e -> d h e"), S, RS, [], "Sst" + seq)

            def tail(ti):
                seq, row0, n, (kind, k) = tiles[ti]
                slot = ti % 2
                x_ = xt[slot]
                Rx = Rxt[slot]
                first = (ti == 0) or (kind == "samp")
                lastt = (ti == len(tiles) - 2) or (kind == "samp")
                prev_n = tiles[ti - 1][2] if ti > 0 else None
                zs, Rzs = zss[ti % 2], Rzss[ti % 2]
                for h in range(8):
                    act(junk[:n, h * 128:(h + 1) * 128], o[:n, h * 128:(h + 1) * 128], AF.Square, Ro, [Rjunk, Rgs2],
                        accum_out=gs[:n, 72 + h:73 + h])
                yield
                act(gs[:n, 80:88], gs[:n, 72:80], AF.Sqrt, [Rgs2, Rc], [Rgs2], bias=eps_t[:n, 0:1], scale=1.0 / 128)
                recip(gs[:n, 80:88], gs[:n, 80:88], [Rgs2], [Rgs2])
                ov = o.rearrange("p (a b) -> p a b", a=8, b=128)
                tt("dve", ov[:n], ov[:n], gs[:n, 80:88].unsqueeze(2).to_broadcast([n, 8, 128]), ALU.mult, Ro + [Rgs2], Ro)
                yield
                tt("pool", ov[:n], ov[:n], onb[:n, :].unsqueeze(1).to_broadcast([n, 8, 128]), ALU.mult, Ro + [Ronb], Ro)
                yield
                tt("dve", yg[:n, :], o[:n, :], zs[:n, :], ALU.mult, Ro + [Rzs], [Ryg])
                yield
                b, br = bank()
                bb = bankb(b).rearrange("p (a b) -> p a b", a=8, b=128)
                for kc in range(8):
                    tr(bb[:, kc, :n], yg[:n, kc * 128:(kc + 1) * 128], ident_b[:n, :n], [Ryg, Rc], [br])
                cp("act", ygT[:, :, :n], bb[:, :, :n], [br], [RygT])
                yield
                for s in range(2):
                    b, br = bank()
                    for kc in range(8):
                        mm(b[:n, :], ygT[:, kc, :n], w_out[:, kc, s * 512:(s + 1) * 512], kc == 0, kc == 7, [RygT, Rwo], [br])
                    tt("dve", x_[:n, s * 512:(s + 1) * 512], b[:n, :], x_[:n, s * 512:(s + 1) * 512], ALU.add, [br, Rx], [Rx])
                    yield
                dma("sp", hs[0][row0:row0 + n, :], x_[:n, :], [Rx], [], f"xst{slot}")

                yield

            def run_il(gens):
                gens = [g for g in gens if g is not None]
                while gens:
                    for g in list(gens):
                        try:
                            next(g)
                        except StopIteration:
                            gens.remove(g)

            load_x(0)
            if len(tiles) > 1:
                load_x(1)
            run_il([head(0)])
            for ti in range(len(tiles)):
                mid(ti)
                run_il([tail(ti), head(ti + 1) if ti + 1 < len(tiles) else None])
                if ti + 2 < len(tiles):
                    load_x(ti + 2)
            P.barrier()

        def phase_ffn(layer, src, dst, final):
            A.off = persist_off
            Rwg, Rwu, Rwd = Reg("wg"), Reg("wu"), Reg("wd")
            wg = A.alloc([128, 8, DFF], BF16)
            wu = A.alloc([128, 8, DFF], BF16)
            wd = A.alloc([128, NFC, D], BF16)
            load_w(wg, ffn_w_gate[layer], 8, "wg", Rwg)
            load_w(wu, ffn_w_up[layer], 8, "wu", Rwu)
            load_w(wd, ffn_w_down[layer], NFC, "wd", Rwd)
            stage = A.alloc([128, 128], F32)
            Rst = Reg("stage")
            gfm = A.alloc([128, 8], F32)
            Rg = Reg("gfm")
            load_featmajor(gfm, norm_ffn[layer], 8, stage, Rst, Rg, "st")
            if final:
                gfin = A.alloc([128, D], F32)
                Rgfin = Reg("gfin")
                dma("sp", gfin, norm_final.partition_broadcast(128), [], [Rgfin], "gfin")
            NSL = 6
            xt = [A.alloc([128, D], F32) for _ in range(NSL)]
            Rxt = [Reg(f"xt{i}") for i in range(NSL)]
            junk = A.alloc([128, D], BF16)
            Rjunk = Reg("junk")
            small = A.alloc([128, 8], F32)
            Rsm = Reg("small")
            hn = A.alloc([128, D], BF16)
            Rhn = Reg("hn")
            hnT = A.alloc([128, 8, 512], BF16)
            RhnT = Reg("hnT")
            sg = [A.alloc([128, 512], F32) for _ in range(2)]
            Rsg = [Reg("sg0"), Reg("sg1")]
            hT = A.alloc([128, NFC, 512], BF16)
            RhT = Reg("hT")

            macros = []
            cur, tot = [], 0
            for ti, tl in enumerate(tiles):
                if tot + tl[2] > 512:
                    macros.append(cur)
                    cur, tot = [], 0
                cur.append(ti)
                tot += tl[2]
            if cur:
                macros.append(cur)

            def load_x(ti):
                seq, row0, n, _ = tiles[ti]
                slot = ti % NSL
                dma("sp", xt[slot][:n, :], src[row0:row0 + n, :], [], [Rxt[slot]], f"fx{slot}")

            NPRE = 5
            for ti in range(min(NPRE, len(tiles))):
                load_x(ti)
            nloaded = min(NPRE, len(tiles))
            for mac in macros:
                offs = []
                NT = 0
                for ti in mac:
                    offs.append(NT)
                    NT += tiles[ti][2]
                for j, ti in enumerate(mac):
                    seq, row0, n, (kind, k) = tiles[ti]
                    slot = ti % NSL
                    rmsnorm_T(xt[slot], Rxt[slot], n, gfm, Rg, hn, Rhn, hnT[:, :, offs[j]:offs[j] + n], RhnT,
                              small, Rsm, junk, Rjunk)
                for fc in range(NFC):
                    bg, bgr = bank()
                    bu, bur = bank()
                    for kc in range(8):
                        mm(bg[:, :NT], wg[:, kc, fc * 128:(fc + 1) * 128], hnT[:, kc, :NT], kc == 0, kc == 7, [Rwg, RhnT], [bgr])
                    for kc in range(8):
                        mm(bu[:, :NT], wu[:, kc, fc * 128:(fc + 1) * 128], hnT[:, kc, :NT], kc == 0, kc == 7, [Rwu, RhnT], [bur])
                    sl = fc % 2
                    act(sg[sl][:, :NT], bg[:, :NT], AF.Silu, [bgr], [Rsg[sl]])
                    tt("dve", hT[:, fc, :NT], bu[:, :NT], sg[sl][:, :NT], ALU.mult, [bur, Rsg[sl]], [RhT])
                for j, ti in enumerate(mac):
                    seq, row0, n, (kind, k) = tiles[ti]
                    slot = ti % NSL
                    x_ = xt[slot]
                    Rx = Rxt[slot]
                    o0 = offs[j]
                    for s in range(2):
                        b, br = bank()
                        for fc in range(NFC):
                            mm(b[:n, :], hT[:, fc, o0:o0 + n], wd[:, fc, s * 512:(s + 1) * 512], fc == 0, fc == NFC - 1,
                               [RhT, Rwd], [br])
                        tt("dve", x_[:n, s * 512:(s + 1) * 512], b[:n, :], x_[:n, s * 512:(s + 1) * 512], ALU.add, [br, Rx], [Rx])
                    if not final:
                        dma("sp", dst[row0:row0 + n, :], x_[:n, :], [Rx], [], f"fs{slot}")
                    else:
                        act(junk[:n, :], x_[:n, :], AF.Square, [Rx], [Rjunk, Rsm], accum_out=small[:n, 4:5])
                        act(small[:n, 5:6], small[:n, 4:5], AF.Sqrt, [Rsm, Rc], [Rsm], bias=eps_t[:n, 0:1], scale=1.0 / D)
                        recip(small[:n, 6:7], small[:n, 5:6], [Rsm], [Rsm])
                        stt(x_[:n, :], x_[:n, :], small[:n, 6:7], gfin[:n, :], ALU.mult, ALU.mult, [Rx, Rsm, Rgfin], [Rx])
                        if kind == "x":
                            dma("sp", y_prompt[k * 128:(k + 1) * 128, :], x_[:n, :], [Rx], [], f"fs{slot}")
                        elif kind == "samp":
                            dma("sp", y_sample, x_[:n, :], [Rx], [], f"fs{slot}")
                    if nloaded < len(tiles):
                        load_x(nloaded)
                        nloaded += 1
            P.barrier()

        def phase_rwkv(src, dst):
            phase_rwkv_impl(src, dst)

        def phase_rwkv_impl(src, dst):
            A.off = persist_off
            Rwr, Rwk, Rwv, Rwo = Reg("w_r"), Reg("w_k"), Reg("w_v"), Reg("w_o")
            w_r = A.alloc([128, 8, D], BF16)
            w_k = A.alloc([128, 8, D], BF16)
            w_v = A.alloc([128, 8, D], BF16)
            w_o = A.alloc([128, 8, D], BF16)
            w1 = A.alloc([128, 8, 64], BF16)
            a1 = A.alloc([128, 8, 64], BF16)
            g1 = A.alloc([128, 8, 160], BF16)
            w2 = A.alloc([128, D], BF16)
            a2 = A.alloc([128, D], BF16)
            g2 = A.alloc([128, 2, D], BF16)
            Rlo = Reg("lora_w")
            load_w(w_r, rwkv_w_r, 8, "w_r", Rwr)
            load_w(w_k, rwkv_w_k, 8, "w_k", Rwk)
            load_w(w_v, rwkv_w_v, 8, "w_v", Rwv)
            load_w(w_o, rwkv_w_o, 8, "w_o", Rwo)
            load_w(w1, rwkv_w1, 8, "lo", Rlo)
            load_w(a1, rwkv_a1, 8, "lo", Rlo)
            load_w(g1, rwkv_g1, 8, "lo", Rlo)
            dma("pool", w2[0:64, :], rwkv_w2, [], [Rlo], "lo", max_dma_last_dim=4096)
            dma("pool", a2[0:64, :], rwkv_a2, [], [Rlo], "lo", max_dma_last_dim=4096)
            dma("pool", g2[:, 0, :], rwkv_g2[0:128, :], [], [Rlo], "lo", max_dma_last_dim=4096)
            dma("pool", g2[0:32, 1, :], rwkv_g2[128:160, :], [], [Rlo], "lo", max_dma_last_dim=4096)
            stage = A.alloc([128, 128], F32)
            Rst = Reg("stage")
            prm = A.alloc([128, 112], F32)
            Rprm = Reg("prm")
            dma("sp", stage[0:48, :], rwkv_mu.rearrange("m (c p) -> (m c) p", p=128), [], [Rst], "st")
            for i, vec in enumerate([rwkv_w0, rwkv_a0, rwkv_k_k, rwkv_k_a, rwkv_r_k, norm_mix[1]]):
                dma("sp", stage[48 + 8 * i:56 + 8 * i, :], vec.rearrange("(c p) -> c p", p=128), [], [Rst], "st")
            b, br = bank()
            tr(b[:, 0:96], stage[0:96, :], ident_f[0:96, 0:96], [Rst, Rc], [br])
            cp("dve", prm[:, 0:96], b[:, 0:96], [br], [Rprm])
            MU, W0, A0, KK, KA, RK, GM, OMKA = 0, 48, 56, 64, 72, 80, 88, 96
            ts("dve", prm[:, OMKA:OMKA + 8], prm[:, KA:KA + 8], -1.0, ALU.mult, [Rprm], [Rprm], s2=1.0, op1=ALU.add)
            lnw = A.alloc([128, D], F32)
            lnb = A.alloc([128, D], F32)
            Rln = Reg("ln")
            dma("sp", lnw, rwkv_ln_w.partition_broadcast(128), [], [Rln], "lnw")
            dma("sp", lnb, rwkv_ln_b.partition_broadcast(128), [], [Rln], "lnb")
            Rk2 = Reg("consts2")
            blk64 = A.alloc([128, 128], BF16)
            hm = A.alloc([128, 2], F32)
            headind = A.alloc([128, 2], BF16)
            nhmrow = A.alloc([128, 2, 128], BF16)
            hmrow = A.alloc([128, 2, 128], BF16)
            memset("pool", blk64, 0.0, [Rk2])
            memset("pool", blk64[0:64, 0:64], 1.0, [Rk2])
            memset("pool", blk64[64:128, 64:128], 1.0, [Rk2])
            memset("pool", hm, 0.0, [Rk2])
            memset("pool", hm[0:64, 0:1], 1.0, [Rk2])
            memset("pool", hm[64:128, 1:2], 1.0, [Rk2])
            cp("pool", headind, hm, [Rk2], [Rk2])
            memset("pool", hmrow, 0.0, [Rk2])
            memset("pool", hmrow[:, 0, 0:64], 1.0, [Rk2])
            memset("pool", hmrow[:, 1, 64:128], 1.0, [Rk2])
            memset("pool", nhmrow, 0.0, [Rk2])
            memset("pool", nhmrow[:, 0, 0:64], -1.0, [Rk2])
            memset("pool", nhmrow[:, 1, 64:128], -1.0, [Rk2])

            xt = [A.alloc([128, D], F32) for _ in range(2)]
            Rxt = [Reg("xt0"), Reg("xt1")]
            junk = A.alloc([128, D], BF16)
            Rjunk = Reg("junk")
            small = A.alloc([128, 8], F32)
            Rsm = Reg("small")
            hn = A.alloc([128, D], BF16)
            Rhn = Reg("hn")
            hnTe = A.alloc([128, 8, 130], F32)
            RhnT = Reg("hnTe")
            xm = A.alloc([128, 6, 8, 128], BF16)
            Rxm = [Reg(f"xm{m}") for m in range(6)]
            NTMP = 14
            tpblk = A.alloc([128, 2 * NTMP * 128], F32)
            tp = [[tpblk[:, (s_ * NTMP + i_) * 128:(s_ * NTMP + i_ + 1) * 128] for i_ in range(NTMP)] for s_ in range(2)]
            Rtp = [[Reg(f"tp{s}_{i}") for i in range(NTMP)] for s in range(2)]
            xx = tpblk[:, 0:1024].rearrange("p (a b) -> p a b", a=8, b=128)
            Rxx = Rtp[0][0:8]
            sqb = [A.alloc([128, 128], BF16) for _ in range(2)]
            Rsqb = [Reg("sqb0"), Reg("sqb1")]
            bt = A.alloc([128, 8, 128], BF16)
            kt = A.alloc([128, 8, 128], BF16)
            rkr = A.alloc([128, 8, 128], BF16)
            Rbt = [Reg(f"bt{i}") for i in range(8)]
            Rkt = [Reg(f"kt{i}") for i in range(8)]
            Rrkr = Reg("rkr")
            krp = A.alloc([128, 8, 4, 128], BF16)
            Rkrp = [Reg(f"krp{i}") for i in range(8)]
            nbtok = A.alloc([128, 8, 2, 128], BF16)
            ktok = A.alloc([128, 8, 2, 128], BF16)
            Rtok = [Reg(f"tok{i}") for i in range(8)]
            GL = A.alloc([128, 8], F32)
            RGL = [Reg(f"GL{i}") for i in range(8)]
            lo1 = A.alloc([128, 2, 128], BF16)
            lo3 = A.alloc([128, 2, 128], BF16)
            Rlo1, Rlo3 = Reg("lo1"), Reg("lo3")
            vfs = [A.alloc([128, D], F32) for _ in range(2)]
            Rvfs = [Reg("vf0"), Reg("vf1")]
            vb = A.alloc([128, D], BF16)
            Rv = Reg("v")
            gates = [A.alloc([128, D], BF16) for _ in range(2)]
            Rgates = [Reg("gate0"), Reg("gate1")]
            NSETS = 3
            SETS = []
            for si in range(NSETS):
                PRs = [A.alloc([128, 2, 2, 128], BF16) for _ in range(2)]
                RPRs = [Reg(f"PR{si}a"), Reg(f"PR{si}b")]
                PTs = [A.alloc([128, 2, 128], BF16) for _ in range(2)]
                RPTs = [Reg(f"PT{si}a"), Reg(f"PT{si}b")]
                B0s = A.alloc([128, 2, 128], BF16)
                B0Ts = A.alloc([128, 2, 128], BF16)
                SETS.append((PRs, RPRs, PTs, RPTs, B0s, Reg(f"B0_{si}"), B0Ts, Reg(f"B0T_{si}"),
                             A.alloc([128, 2, 128], BF16), A.alloc([128, 2, 128], BF16), A.alloc([128, 2, 128], BF16),
                             Reg(f"mats{si}"), A.alloc([128, 2, 128], BF16), Reg(f"Minv{si}"),
                             A.alloc([128, 2, 64], BF16), A.alloc([128, 2, 64], BF16), Reg(f"RHS{si}"), Reg(f"U{si}")))
            yv = A.alloc([128, D], F32)
            Ry = Reg("y")
            ysq = A.alloc([128, D], F32)
            Rysq = Reg("ysq")
            st16 = A.alloc([128, 96], F32)
            Rst16 = Reg("st16")
            Pst = A.alloc([128, 8, 64], F32)
            Pb = A.alloc([128, 8, 64], BF16)
            PG = A.alloc([128, 2, 64], F32)
            RP = [Reg(f"P{i}") for i in range(8)]
            RPb = [Reg(f"Pb{i}") for i in range(8)]
            RPG = [Reg("PG0"), Reg("PG1")]
            Snat = tpblk[:, NTMP * 128:NTMP * 128 + 1024].rearrange("p (a b) -> p a b", a=16, b=64)
            RSnat = Rtp[1][0:8]
            yg = A.alloc([128, D], BF16)
            Ryg = Reg("yg")
            ygT = A.alloc([128, 8, 128], BF16)
            RygT = Reg("ygT")
            hnT = hnTe[:, :, 1:129]

            def load_x(ti):
                seq, row0, n, _ = tiles[ti]
                slot = ti % 2
                dma("sp", xt[slot][:n, :], src[row0:row0 + n, :], [], [Rxt[slot]], f"xt{slot}")

            def head(ti):
                seq, row0, n, (kind, k) = tiles[ti]
                slot = ti % 2
                x_ = xt[slot]
                Rx = Rxt[slot]
                first = (ti == 0) or (kind == "samp")
                lastt = (ti == len(tiles) - 2) or (kind == "samp")
                prev_n = tiles[ti - 1][2] if ti > 0 else None
                gate, Rgate = gates[ti % 2], Rgates[ti % 2]
                vf, Rvf = vfs[ti % 2], Rvfs[ti % 2]
                XR, XW, XK, XV, XA, XG = 0, 1, 2, 3, 4, 5
                if first and seq == "p":
                    memset("pool", Pst, 0.0, RP)
                    memset("pool", Pb, 0.0, RPb)
                    memset("pool", hnTe[:, :, 0:1], 0.0, [RhnT])
                elif first:
                    dma("sp", Snat[0:64, :, :], state_rwkv.rearrange("h i j -> i h j"), [], RSnat, "Sld")
                    for half in range(2):
                        b, br = bank()
                        bv = b.rearrange("p (a b) -> p a b", a=8, b=64)
                        for j in range(4):
                            oc = half * 4 + j
                            tr(bv[:, j, :], Snat[0:64, 2 * oc:2 * oc + 2, :].rearrange("p a b -> p (a b)"),
                               ident_f[0:64, 0:64], RSnat + [Rc], [br])
                        cp("dve", Pst[:, half * 4:(half + 1) * 4, :], bv[:, 0:4, :], [br], RP)
                    cp("pool", Pb, Pst, RP, RPb)
                    dma("sp", stage[0:8, :], cache_rwkv_shift.rearrange("o (c p) -> (o c) p", p=128), [], [Rst], "st")
                    b, br = bank()
                    tr(b[:, 0:8], stage[0:8, :], ident_f[0:8, 0:8], [Rst, Rc], [br])
                    cp("dve", hnTe[:, :, 0:1], b[:, 0:8].unsqueeze(2), [br], [RhnT])
                else:
                    cp("pool", hnTe[:, :, 0:1], hnTe[:, :, prev_n:prev_n + 1], [RhnT], [RhnT])

                yield
                rmsnorm_T(x_, Rx, n, prm[:, GM:GM + 8], Rprm, hn, Rhn, hnT, RhnT, small, Rsm, junk, Rjunk)
                if lastt:
                    b, br = bank()
                    tr(b[0:8, 0:128], hnTe[:, :, n:n + 1].rearrange("p a b -> p (a b)"), ident_f, [RhnT, Rc], [br])
                    cp("dve", stage[0:8, :], b[0:8, 0:128], [br], [Rst])
                    dma("sp", o_rwkv_shift[seq].rearrange("o (c p) -> (o c) p", p=128), stage[0:8, :], [Rst], [], "shst" + seq)
                yield
                tt("pool", xx[:, :, :n], hnTe[:, :, 0:n], hnTe[:, :, 1:n + 1], ALU.subtract, [RhnT], Rxx)
                for m in range(6):
                    eng = "dve" if m % 2 == 0 else "pool"
                    tt(eng, xm[:, m, :, :n], xx[:, :, :n],
                       prm[:, MU + 8 * m:MU + 8 * m + 8].unsqueeze(2).to_broadcast([128, 8, n]), ALU.mult,
                       Rxx + [Rprm], [Rxm[m]])
                    tt(eng, xm[:, m, :, :n], xm[:, m, :, :n], hnTe[:, :, 1:n + 1], ALU.add, [Rxm[m], RhnT], [Rxm[m]])
                    if m % 2 == 1:
                        yield
                XR, XW, XK, XV, XA, XG = 0, 1, 2, 3, 4, 5
                b, br = bank()
                for kc in range(8):
                    mm(b[0:64, 0:n], w1[:, kc, :], xm[:, XW, kc, :n], kc == 0, kc == 7, [Rlo, Rxm[XW]], [br])
                for kc in range(8):
                    mm(b[0:64, 128:128 + n], a1[:, kc, :], xm[:, XA, kc, :n], kc == 0, kc == 7, [Rlo, Rxm[XA]], [br])
                act(lo1[0:64, 0, :n], b[0:64, 0:n], AF.Tanh, [br], [Rlo1])
                cp("dve", lo1[0:64, 1, :n], b[0:64, 128:128 + n], [br], [Rlo1])
                yield
                b, br = bank()
                for kc in range(8):
                    mm(b[:, 0:n], g1[:, kc, 0:128], xm[:, XG, kc, :n], kc == 0, kc == 7, [Rlo, Rxm[XG]], [br])
                for kc in range(8):
                    mm(b[0:32, 128:128 + n], g1[:, kc, 128:160], xm[:, XG, kc, :n], kc == 0, kc == 7, [Rlo, Rxm[XG]], [br])
                act(lo3[:, 0, :n], b[:, 0:n], AF.Sigmoid, [br], [Rlo3])
                act(lo3[0:32, 1, :n], b[0:32, 128:128 + n], AF.Sigmoid, [br], [Rlo3])
                yield
                for s in range(2):
                    b, br = bank()
                    mm(b[:n, :], lo3[:, 0, :n], g2[:, 0, s * 512:(s + 1) * 512], True, False, [Rlo3, Rlo], [br])
                    mm(b[:n, :], lo3[0:32, 1, :n], g2[0:32, 1, s * 512:(s + 1) * 512], False, True, [Rlo3, Rlo], [br])
                    cp("act", gate[:n, s * 512:(s + 1) * 512], b[:n, :], [br], [Rgate])
                    yield
                for s in range(2):
                    b, br = bank()
                    for kc in range(8):
                        mm(b[:n, :], xm[:, XV, kc, :n], w_v[:, kc, s * 512:(s + 1) * 512], kc == 0, kc == 7, [Rxm[XV], Rwv], [br])
                    cp("act", vf[:n, s * 512:(s + 1) * 512], b[:n, :], [br], [Rvf])
                    cp("dve", vb[:n, s * 512:(s + 1) * 512], b[:n, :], [br], [Rv])
                    yield


                yield

            def mid(ti):
                seq, row0, n, (kind, k) = tiles[ti]
                slot = ti % 2
                x_ = xt[slot]
                Rx = Rxt[slot]
                first = (ti == 0) or (kind == "samp")
                lastt = (ti == len(tiles) - 2) or (kind == "samp")
                prev_n = tiles[ti - 1][2] if ti > 0 else None
                gate, Rgate = gates[ti % 2], Rgates[ti % 2]
                vf, Rvf = vfs[ti % 2], Rvfs[ti % 2]
                XR, XW, XK, XV, XA, XG = 0, 1, 2, 3, 4, 5
                nf = int(math.log2(n))
                by = [(psf[6][:], psr[6]), (psf[7][:], psr[7])]
                LD, C_, AT, KX, RT_, KKn, TKA, K2, TMP, E1, E2, E3, RR, KR = range(14)

                def prep_gen():
                    for pair in range(4):
                        ocs = (2 * pair, 2 * pair + 1)
                        T2 = {oc: tp[oc % 2] for oc in ocs}
                        R2 = {oc: Rtp[oc % 2] for oc in ocs}
                        bl2 = {}
                        for oc in ocs:
                            T_, RT = T2[oc], R2[oc]
                            osl = slice(oc * 128, (oc + 1) * 128)
                            br_, brr = bank()
                            bk_, bkr = bank()
                            bl_, blr = bank()
                            bl2[oc] = (bl_, blr)
                            for kc in range(8):
                                mm(br_[:, 0:n], w_r[:, kc, osl], xm[:, XR, kc, :n], kc == 0, kc == 7, [Rwr, Rxm[XR]], [brr])
                            for kc in range(8):
                                mm(bk_[:, 0:n], w_k[:, kc, osl], xm[:, XK, kc, :n], kc == 0, kc == 7, [Rwk, Rxm[XK]], [bkr])
                            mm(bl_[:, 0:n], w2[0:64, osl], lo1[0:64, 0, :n], True, True, [Rlo, Rlo1], [blr])
                            mm(bl_[:, 128:128 + n], a2[0:64, osl], lo1[0:64, 1, :n], True, True, [Rlo, Rlo1], [blr])
                            cp("act", T_[RR][:, :n], br_[:, 0:n], [brr], [RT[RR]])
                            cp("act", T_[KR][:, :n], bk_[:, 0:n], [bkr], [RT[KR]])
                            act(T_[KX][:, :n], bk_[:, 0:n], AF.Copy, [bkr, Rprm], [RT[KX]], scale=prm[:, KK + oc:KK + oc + 1])
                            act(T_[LD][:, :n], bl_[:, 0:n], AF.Sigmoid, [blr, Rprm], [RT[LD]], bias=prm[:, W0 + oc:W0 + oc + 1])
                            act(T_[AT][:, :n], bl_[:, 128:128 + n], AF.Sigmoid, [blr, Rprm], [RT[AT]], bias=prm[:, A0 + oc:A0 + oc + 1])
                        steps = []
                        for oc in ocs:
                            T_, RT = T2[oc], R2[oc]
                            sl = oc % 2
                            L = []
                            L.append(lambda T_=T_, RT=RT: ts("pool", T_[LD][:, :n], T_[LD][:, :n], -math.exp(-0.5), ALU.mult, [RT[LD]], [RT[LD]]))
                            L.append(lambda T_=T_, RT=RT: P.op("dve", lambda e, o_=T_[C_][:, :n], d0=ones_f[:, :n], d1=T_[LD][:, :n]:
                                     e.tensor_tensor_scan(o_, d0, d1, 0.0, ALU.mult, ALU.add), [Rc, RT[LD]], [RT[C_]]))
                            L.append(lambda T_=T_, RT=RT, sl=sl: tt("pool", sqb[sl][:, :n], T_[KX][:, :n], T_[KX][:, :n], ALU.mult, [RT[KX]], [Rsqb[sl]]))

                            def ssq(T_=T_, RT=RT, sl=sl):
                                bs_, bsr = bank()
                                mm(bs_[:, 0:n], blk64, sqb[sl][:, :n], True, True, [Rk2, Rsqb[sl]], [bsr])
                                act(T_[RT_][:, :n], bs_[:, 0:n], AF.Sqrt, [bsr, Rc], [RT[RT_]], bias=eps_t[:, 0:1])
                            L.append(ssq)
                            L.append(lambda T_=T_, RT=RT, oc=oc: ts("dve", T_[TKA][:, :n], T_[AT][:, :n], prm[:, KA + oc:KA + oc + 1], ALU.mult, [RT[AT], Rprm], [RT[TKA]],
                                     s2=prm[:, OMKA + oc:OMKA + oc + 1], op1=ALU.add))
                            L.append(lambda T_=T_, RT=RT: recip(T_[RT_][:, :n], T_[RT_][:, :n], [RT[RT_]], [RT[RT_]]))
                            L.append(lambda T_=T_, RT=RT: tt("pool", T_[TMP][:, :n], T_[C_][:, :n], T_[LD][:, :n], ALU.subtract, [RT[C_], RT[LD]], [RT[TMP]]))
                            L.append(lambda T_=T_, RT=RT: act(T_[E1][:, :n], T_[TMP][:, :n], AF.Exp, [RT[TMP]], [RT[E1]]))
                            L.append(lambda T_=T_, RT=RT: tt("dve", T_[K2][:, :n], T_[KR][:, :n], T_[TKA][:, :n], ALU.mult, [RT[KR], RT[TKA]], [RT[K2]]))
                            L.append(lambda T_=T_, RT=RT: act(T_[E2][:, :n], T_[C_][:, :n], AF.Exp, [RT[C_]], [RT[E2]], scale=-1.0))
                            L.append(lambda T_=T_, RT=RT: tt("pool", T_[KKn][:, :n], T_[KX][:, :n], T_[RT_][:, :n], ALU.mult, [RT[KX], RT[RT_]], [RT[KKn]]))
                            L.append(lambda T_=T_, RT=RT: act(T_[E3][:, :n], T_[C_][:, :n], AF.Exp, [RT[C_]], [RT[E3]]))
                            L.append(lambda T_=T_, RT=RT, oc=oc: tt("pool", kt[:, oc, :n], T_[K2][:, :n], T_[E2][:, :n], ALU.mult, [RT[K2], RT[E2]], [Rkt[oc]]))
                            for h2 in range(2):
                                L.append(lambda T_=T_, RT=RT, oc=oc, h2=h2: stt(krp[:, oc, 2 * h2 + 0, :n], T_[KKn][:, :n], hm[:, h2:h2 + 1], T_[E1][:, :n], ALU.mult, ALU.mult,
                                         [RT[KKn], Rk2, RT[E1]], [Rkrp[oc]]))
                            L.append(lambda T_=T_, RT=RT: tt("pool", T_[TMP][:, :n], T_[KKn][:, :n], T_[AT][:, :n], ALU.mult, [RT[KKn], RT[AT]], [RT[TMP]]))
                            for h2 in range(2):
                                L.append(lambda T_=T_, RT=RT, oc=oc, h2=h2: stt(krp[:, oc, 2 * h2 + 1, :n], T_[RR][:, :n], hm[:, h2:h2 + 1], T_[E3][:, :n], ALU.mult, ALU.mult,
                                         [RT[RR], Rk2, RT[E3]], [Rkrp[oc]]))
                            L.append(lambda T_=T_, RT=RT, oc=oc: tt("pool", bt[:, oc, :n], T_[TMP][:, :n], T_[E2][:, :n], ALU.mult, [RT[TMP], RT[E2]], [Rbt[oc]]))
                            L.append(lambda T_=T_, RT=RT, oc=oc: cp("pool", GL[:, oc:oc + 1], T_[E3][:, n - 1:n], [RT[E3]], [RGL[oc]]))
                            L.append(lambda T_=T_, RT=RT, oc=oc: stt(rkr[:, oc, :n], T_[RR][:, :n], prm[:, RK + oc:RK + oc + 1], T_[K2][:, :n], ALU.mult, ALU.mult,
                                     [RT[RR], Rprm, RT[K2]], [Rrkr]))
                            steps.append(L)
                        for i in range(len(steps[0])):
                            for L in steps:
                                L[i]()
                            if i % 6 == 5:
                                yield None
                        yield ocs[0]
                        yield ocs[1]

                def g_gen(oc, S_):
                    (PRs, RPRs, PTs, RPTs, B0s, RB0s, B0Ts, RB0Ts, BrTn, AkT, BkT, Rmats, MinvTs, RMinvs, RHSb, Ub, RRHS, RU) = S_
                    b, br = bank()
                    bb = bankb(b).rearrange("p (a b) -> p a b", a=8, b=128)
                    tr(bb[:n, 0, :], bt[:, oc, :n], ident_b, [Rbt[oc], Rc], [br])
                    tr(bb[:n, 1, :], kt[:, oc, :n], ident_b, [Rkt[oc], Rc], [br])
                    tt("dve", nbtok[:n, oc], bb[:n, 0:1, :].to_broadcast([n, 2, 128]), nhmrow[:n], ALU.mult, [br, Rk2], [Rtok[oc]])
                    tt("dve", ktok[:n, oc], bb[:n, 1:2, :].to_broadcast([n, 2, 128]), hmrow[:n], ALU.mult, [br, Rk2], [Rtok[oc]])
                    yield
                    bm1, bm1r = bank()
                    bm2, bm2r = bank()
                    bm3, bm3r = bank()
                    vm1 = bm1.rearrange("p (h t c) -> p h t c", h=2, t=2, c=128)
                    vm2 = bm2.rearrange("p (h t c) -> p h t c", h=2, t=2, c=128)
                    vm3 = bm3.rearrange("p (g c) -> p g c", g=4, c=128)
                    for h2 in range(2):
                        if n == 128:
                            mm(vm1[:n, h2, :, :n], bt[:, oc, :n], krp[:, oc, 2 * h2:2 * h2 + 2, :n], True, True, [Rbt[oc], Rkrp[oc]], [bm1r])
                            mm(vm2[:n, h2, :, :n], kt[:, oc, :n], krp[:, oc, 2 * h2:2 * h2 + 2, :n], True, True, [Rkt[oc], Rkrp[oc]], [bm2r])
                        else:
                            for t_ in range(2):
                                mm(vm1[:n, h2, t_, :n], bt[:, oc, :n], krp[:, oc, 2 * h2 + t_, :n], True, True, [Rbt[oc], Rkrp[oc]], [bm1r])
                                mm(vm2[:n, h2, t_, :n], kt[:, oc, :n], krp[:, oc, 2 * h2 + t_, :n], True, True, [Rkt[oc], Rkrp[oc]], [bm2r])
                        mm(vm3[:n, h2, :n], krp[:, oc, 2 * h2, :n], bt[:, oc, :n], True, True, [Rkrp[oc], Rbt[oc]], [bm3r])
                    stt(B0s[:n, :, :n], vm1[:n, :, 0, :n], -1.0, smask_u[:n, :n].unsqueeze(1).to_broadcast([n, 2, n]),
                        ALU.mult, ALU.mult, [bm1r, Rc], [RB0s])
                    stt(BrTn[:n, :, :n], vm1[:n, :, 1, :n], -1.0, imask_u[:n, :n].unsqueeze(1).to_broadcast([n, 2, n]),
                        ALU.mult, ALU.mult, [bm1r, Rc], [Rmats])
                    tt("dve", AkT[:n, :, :n], vm2[:n, :, 0, :n], smask_u[:n, :n].unsqueeze(1).to_broadcast([n, 2, n]),
                       ALU.mult, [bm2r, Rc], [Rmats])
                    tt("dve", BkT[:n, :, :n], vm2[:n, :, 1, :n], imask_u[:n, :n].unsqueeze(1).to_broadcast([n, 2, n]),
                       ALU.mult, [bm2r, Rc], [Rmats])
                    stt(B0Ts[:n, :, :n], vm3[:n, 0:2, :n], -1.0, smask_l[:n, :n].unsqueeze(1).to_broadcast([n, 2, n]),
                        ALU.mult, ALU.mult, [bm3r, Rc], [RB0Ts])
                    yield
                    res = []
                    for _ in neumann_gen(B0s[:n, :, :n], B0Ts[:n, :, :n], RB0s, RB0Ts, n, nf, PRs, RPRs, PTs, RPTs, 2, res, psum_acc=True):
                        yield
                    Rfin, RRfin = res[0]
                    cp("act", MinvTs[:n, :, :n], Rfin, [RRfin], [RMinvs])
                    bR, bRr = bank()
                    vR = bR[:, 0:128].rearrange("p (g c) -> p g c", g=2, c=64)
                    for h2 in range(2):
                        hd = 2 * oc + h2
                        mm(vR[:n, h2, :], krp[:, oc, 2 * h2, :n], Pb[:, oc, :], True, False, [Rkrp[oc], RPb[oc]], [bRr])
                        mm(vR[:n, h2, :], AkT[:n, h2, :n], vb[:n, hd * 64:(hd + 1) * 64], False, True, [Rmats, Rv], [bRr])
                    cp("dve", RHSb[:n], vR[:n], [bRr], [RRHS])
                    yield
                    bU, bUr = bank()
                    vU = bU[:, 0:128].rearrange("p (g c) -> p g c", g=2, c=64)
                    for h2 in range(2):
                        mm(vU[:n, h2, :], MinvTs[:n, h2, :n], RHSb[:n, h2, :], True, True, [RMinvs, RRHS], [bUr])
                    cp("act", Ub[:n], vU[:n], [bUr], [RU])
                    yield
                    bY, bYr = bank()
                    vY = bY[:, 0:128].rearrange("p (g c) -> p g c", g=2, c=64)
                    for h2 in range(2):
                        hd = 2 * oc + h2
                        mm(vY[:n, h2, :], krp[:, oc, 2 * h2 + 1, :n], Pb[:, oc, :], True, False, [Rkrp[oc], RPb[oc]], [bYr])
                        mm(vY[:n, h2, :], BrTn[:n, h2, :n], Ub[:n, h2, :], False, False, [Rmats, RU], [bYr])
                        mm(vY[:n, h2, :], BkT[:n, h2, :n], vb[:n, hd * 64:(hd + 1) * 64], False, True, [Rmats, Rv], [bYr])
                    cp("act", yv[:n, oc * 128:(oc + 1) * 128], bY[:n, 0:128], [bYr], [Ry])
                    bP, bPr = bank()
                    for h2 in range(2):
                        hd = 2 * oc + h2
                        mm(bP[:, 0:64], nbtok[:n, oc, h2, :], Ub[:n, h2, :], h2 == 0, False, [Rtok[oc], RU], [bPr])
                        mm(bP[:, 0:64], ktok[:n, oc, h2, :], vb[:n, hd * 64:(hd + 1) * 64], False, h2 == 1, [Rtok[oc], Rv], [bPr])
                    ts("pool", PG[:, oc % 2, :], Pst[:, oc, :], GL[:, oc:oc + 1], ALU.mult, [RP[oc], RGL[oc]], [RPG[oc % 2]])
                    stt(Pst[:, oc, :], bP[:, 0:64], GL[:, oc:oc + 1], PG[:, oc % 2, :], ALU.mult, ALU.add,
                        [bPr, RGL[oc], RPG[oc % 2]], [RP[oc]])
                    cp("pool", Pb[:, oc, :], Pst[:, oc, :], [RP[oc]], [RPb[oc]])

                pg = prep_gen()
                ready = set()
                next_g = 0
                active = []
                prep_alive = True
                while prep_alive or active or next_g < 8:
                    if prep_alive:
                        try:
                            r = next(pg)
                            if r is not None:
                                ready.add(r)
                        except StopIteration:
                            prep_alive = False
                    while next_g < 8 and next_g in ready and len(active) < NSETS:
                        active.append(g_gen(next_g, SETS[next_g % NSETS]))
                        next_g += 1
                    for g in list(active):
                        try:
                            next(g)
                        except StopIteration:
                            active.remove(g)

                b, br = bank()
                for oc in range(8):
                    mm(b[:n, 2 * oc:2 * oc + 2], rkr[:, oc, :n], headind, True, True, [Rrkr, Rk2], [br])
                cp("dve", st16[:n, 32:48], b[:n, 0:16], [br], [Rst16])


                if lastt:
                    for half in range(2):
                        b, br = bank()
                        bv = b.rearrange("p (a b) -> p a b", a=4, b=128)
                        for j in range(4):
                            oc = half * 4 + j
                            tr(bv[0:64, j, :], Pst[:, oc, :], ident_f, RP + [Rc], [br])
                        cp("dve", Snat[0:64, half * 8:(half + 1) * 8, :].rearrange("p a b -> p (a b)"),
                           b[0:64, :], [br], RSnat)
                    dma("sp", o_rwkv_state[seq].rearrange("h i j -> i h j"), Snat[0:64, :, :], RSnat, [], "Sst" + seq)

            def tail(ti):
                seq, row0, n, (kind, k) = tiles[ti]
                slot = ti % 2
                x_ = xt[slot]
                Rx = Rxt[slot]
                first = (ti == 0) or (kind == "samp")
                lastt = (ti == len(tiles) - 2) or (kind == "samp")
                prev_n = tiles[ti - 1][2] if ti > 0 else None
                gate, Rgate = gates[ti % 2], Rgates[ti % 2]
                vf, Rvf = vfs[ti % 2], Rvfs[ti % 2]
                XR, XW, XK, XV, XA, XG = 0, 1, 2, 3, 4, 5
                y3 = yv.rearrange("p (a b) -> p a b", a=16, b=64)
                q3 = ysq.rearrange("p (a b) -> p a b", a=16, b=64)
                v3 = vf.rearrange("p (a b) -> p a b", a=16, b=64)
                P.op("dve", lambda e, o_=st16[:n, 0:16], i_=y3[:n]: e.tensor_reduce(o_, i_, mybir.AxisListType.X, ALU.add),
                     [Ry], [Rst16])
                tt("pool", ysq[:n, :], yv[:n, :], yv[:n, :], ALU.mult, [Ry], [Rysq])
                P.op("dve", lambda e, o_=st16[:n, 16:32], i_=q3[:n]: e.tensor_reduce(o_, i_, mybir.AxisListType.X, ALU.add),
                     [Rysq], [Rst16])
                yield
                ts("dve", st16[:n, 0:16], st16[:n, 0:16], 1.0 / 64, ALU.mult, [Rst16], [Rst16])
                tt("dve", st16[:n, 48:64], st16[:n, 0:16], st16[:n, 0:16], ALU.mult, [Rst16], [Rst16])
                stt(st16[:n, 16:32], st16[:n, 16:32], 1.0 / 64, st16[:n, 48:64], ALU.mult, ALU.subtract, [Rst16], [Rst16])
                act(st16[:n, 16:32], st16[:n, 16:32], AF.Sqrt, [Rst16, Rc], [Rst16], bias=eps_t[:n, 2:3])
                recip(st16[:n, 16:32], st16[:n, 16:32], [Rst16], [Rst16])
                yield
                tt("dve", y3[:n], y3[:n], st16[:n, 0:16].unsqueeze(2).to_broadcast([n, 16, 64]), ALU.subtract, [Ry, Rst16], [Ry])
                tt("pool", y3[:n], y3[:n], st16[:n, 16:32].unsqueeze(2).to_broadcast([n, 16, 64]), ALU.mult, [Ry, Rst16], [Ry])
                yield
                tt("dve", yv[:n, :], yv[:n, :], lnw[:n, :], ALU.mult, [Ry, Rln], [Ry])
                tt("pool", yv[:n, :], yv[:n, :], lnb[:n, :], ALU.add, [Ry, Rln], [Ry])
                yield
                tt("dve", q3[:n], v3[:n], st16[:n, 32:48].unsqueeze(2).to_broadcast([n, 16, 64]), ALU.mult, [Rvf, Rst16], [Rysq])
                tt("pool", yv[:n, :], yv[:n, :], ysq[:n, :], ALU.add, [Ry, Rysq], [Ry])
                yield
                tt("dve", yg[:n, :], yv[:n, :], gate[:n, :], ALU.mult, [Ry, Rgate], [Ryg])
                b, br = bank()
                bb = bankb(b).rearrange("p (a b) -> p a b", a=8, b=128)
                for kc in range(8):
                    tr(bb[:, kc, :n], yg[:n, kc * 128:(kc + 1) * 128], ident_b[:n, :n], [Ryg, Rc], [br])
                yield
                cp("act", ygT[:, :, :n], bb[:, :, :n], [br], [RygT])
                for s in range(2):
                    b, br = bank()
                    for kc in range(8):
                        mm(b[:n, :], ygT[:, kc, :n], w_o[:, kc, s * 512:(s + 1) * 512], kc == 0, kc == 7, [RygT, Rwo], [br])
                    tt("dve", x_[:n, s * 512:(s + 1) * 512], b[:n, :], x_[:n, s * 512:(s + 1) * 512], ALU.add, [br, Rx], [Rx])
                    yield
                dma("sp", dst[row0:row0 + n, :], x_[:n, :], [Rx], [], f"xst{slot}")

                yield

            def run_il(gens):
                gens = [g for g in gens if g is not None]
                while gens:
                    for g in list(gens):
                        try:
                            next(g)
                        except StopIteration:
                            gens.remove(g)

            load_x(0)
            if len(tiles) > 1:
                load_x(1)
            run_il([head(0)])
            for ti in range(len(tiles)):
                mid(ti)
                run_il([tail(ti), head(ti + 1) if ti + 1 < len(tiles) else None])
                if ti + 2 < len(tiles):
                    load_x(ti + 2)
            P.barrier()

        if 1 in phases:
            phase_gdn()
        if 2 in phases:
            phase_ffn(0, hs[0], hs[1], False)
        if 3 in phases:
            phase_rwkv(hs[1], hs[2])
        if 4 in phases:
            phase_ffn(1, hs[2], None, True)

        with nc.Block() as block:
            @block.tensor
            def _(e):
                P.replay("pe", e)

            @block.scalar
            def _(e):
                P.replay("act", e)

            @block.vector
            def _(e):
                P.replay("dve", e)

            @block.gpsimd
            def _(e):
                P.replay("pool", e)

            @block.sync
            def _(e):
                P.replay("sp", e)
    return nc


_NC = None

IN_NAMES_PER_CORE = {
    "x_prompt": lambda a, i: a[i],
    "x_sample": lambda a, i: a[i],
    "cache_gdn_conv": lambda a, i: a[0, i],
    "state_gdn": lambda a, i: a[0, i],
    "cache_rwkv_shift": lambda a, i: a[0, i],
    "state_rwkv": lambda a, i: a[0, i],
}
SQUEEZE0 = ["gdn_w_in", "gdn_conv_w", "gdn_a_log", "gdn_dt_bias", "gdn_o_norm", "gdn_w_out", "rwkv_mu", "rwkv_w0",
            "rwkv_w1", "rwkv_w2", "rwkv_a0", "rwkv_a1", "rwkv_a2", "rwkv_g1", "rwkv_g2", "rwkv_k_k", "rwkv_k_a",
            "rwkv_w_r", "rwkv_w_k", "rwkv_w_v", "rwkv_w_o", "rwkv_ln_w", "rwkv_ln_b"]


def kernel(**inputs):
    global _NC
    if _NC is None:
        _NC = build_program()
    nc = _NC
    f = lambda a: np.ascontiguousarray(np.asarray(a, dtype=np.float32))
    shared = {}
    for k in ["meta_tokens", "norm_mix", "norm_ffn", "norm_final", "ffn_w_gate", "ffn_w_up", "ffn_w_down"]:
        shared[k] = f(inputs[k])
    for k in SQUEEZE0:
        shared[k] = f(np.asarray(inputs[k])[0])
    shared["rwkv_r_k"] = f(np.asarray(inputs["rwkv_r_k"])[0].reshape(D))
    in_maps = []
    for i in range(8):
        m = dict(shared)
        for k, fn in IN_NAMES_PER_CORE.items():
            m[k] = f(fn(np.asarray(inputs[k]), i))
        in_maps.append(m)
    res = run_bass_kernel_spmd(nc, in_maps, core_ids=list(range(8)))
    R = res.results

    def st(name, lead=None):
        a = np.stack([np.asarray(R[i][name], dtype=np.float32) for i in range(8)], axis=0)
        return a if lead is None else a[None]

    return (st("y_prompt"), st("y_sample"),
            st("p_gdn_conv", 1), st("p_gdn_state", 1), st("p_rwkv_shift", 1), st("p_rwkv_state", 1),
            st("s_gdn_conv", 1), st("s_gdn_state", 1), st("s_rwkv_shift", 1), st("s_rwkv_state", 1))
```

```python
import math
from contextlib import ExitStack

import numpy as np
import concourse.bass as bass
import concourse.mybir as mybir
from concourse.bass_utils import run_bass_kernel_spmd

F32 = mybir.dt.float32
BF16 = mybir.dt.bfloat16
AF = mybir.ActivationFunctionType
ALU = mybir.AluOpType

D = 1024
SEQ = 8192
NMETA = 16
TP = SEQ + NMETA
DEC = 64
NROWS = TP + DEC
DFF = 2816
NFC = DFF // 128
GIN = 4112
EPS = 1e-6
GN_EPS = 64e-5
NEG = -1.0e5

ENGS = ("pe", "act", "dve", "pool", "sp")
DBG_A = False
DBG_B = False


class Reg:
    __slots__ = ("name", "w", "r", "psum")

    def __init__(self, name, psum=False):
        self.name = name
        self.w = None
        self.r = []
        self.psum = psum


class Prog:
    def __init__(self, sems):
        self.free_sems = list(sems)
        self.q = {e: [] for e in ENGS}
        self.cnt = {e: 0 for e in ENGS}
        self.esem = {e: self.free_sems.pop() for e in ENGS}
        self.waited = {e: {} for e in ENGS}
        self.dsem = {}

    def _waits(self, eng, deps):
        best = {}
        for (sk, sem, val) in deps:
            if sk == "pe" and eng == "pe":
                continue
            if self.waited[eng].get(sk, 0) >= val:
                continue
            if best.get(sk, (None, 0))[1] < val:
                best[sk] = (sem, val)
        out = []
        for sk, (sem, val) in best.items():
            self.waited[eng][sk] = val
            out.append((sem, val))
        return out

    def op(self, eng, fn, reads=(), writes=(), dma=None, skip_waw=False):
        deps = []
        for b in list(reads) + ([] if skip_waw else list(writes)):
            if b.w is not None:
                deps.append(b.w)
        for b in writes:
            deps.extend(b.r)
        for b in reads:
            if b.psum:
                deps.extend(ev_ for ev_ in b.r if ev_[0] != eng)
        waits = self._waits(eng, deps)
        if dma is None:
            self.cnt[eng] += 1
            ev = (eng, self.esem[eng], self.cnt[eng])
            inc = (self.esem[eng], 1)
        else:
            if dma not in self.dsem:
                self.dsem[dma] = [self.free_sems.pop(), 0]
            ent = self.dsem[dma]
            ent[1] += 16
            ev = ("d:" + dma, ent[0], ent[1])
            inc = (ent[0], 16)
        self.q[eng].append((waits, fn, inc))
        for b in writes:
            b.w = ev
            b.r = []
        for b in reads:
            if b not in writes:
                b.r.append(ev)
        return ev

    def barrier(self):
        evs = [(e, self.esem[e], self.cnt[e]) for e in ENGS if self.cnt[e] > 0]
        evs += [("d:" + k, v[0], v[1]) for k, v in self.dsem.items() if v[1] > 0]
        for e in ENGS:
            ws = []
            for (sk, sem, val) in evs:
                if sk == e:
                    continue
                if self.waited[e].get(sk, 0) >= val:
                    continue
                self.waited[e][sk] = val
                ws.append((sem, val))
            if ws:
                self.q[e].append((ws, None, None))

    def replay(self, eng, e):
        for (waits, fn, inc) in self.q[eng]:
            for (sem, val) in waits:
                e.wait_ge(sem, val)
            if fn is not None:
                ins = fn(e)
                ins.then_inc(inc[0], inc[1])


class Arena:
    def __init__(self, ap, nwords):
        self.ap = ap
        self.n = nwords
        self.off = 0

    def alloc(self, shape, dtype):
        free = 1
        for s in shape[1:]:
            free *= s
        words = free if dtype == F32 else (free + 1) // 2
        words = (words + 7) // 8 * 8
        assert self.off + words <= self.n, f"arena overflow {self.off}+{words}>{self.n}"
        v = self.ap[:, self.off:self.off + words]
        self.off += words
        if dtype != F32:
            v = v.bitcast(dtype)
        v = v[:, 0:free]
        if len(shape) == 3:
            v = v.rearrange("p (a b) -> p a b", a=shape[1], b=shape[2])
        elif len(shape) == 4:
            v = v.rearrange("p (a b c) -> p a b c", a=shape[1], b=shape[2], c=shape[3])
        return v


def build_program(SEQ=SEQ, phases=(1, 2, 3, 4)):
    TP = SEQ + NMETA
    NROWS = TP + DEC
    nc = bass.Bass("TRN2", target_bir_lowering=False)

    def din(name, shape):
        return nc.dram_tensor(name, list(shape), F32, kind="ExternalInput").ap()

    def dout(name, shape):
        return nc.dram_tensor(name, list(shape), F32, kind="ExternalOutput").ap()

    x_prompt = din("x_prompt", [SEQ, D])
    x_sample = din("x_sample", [DEC, D])
    cache_gdn_conv = din("cache_gdn_conv", [3, 3072])
    state_gdn = din("state_gdn", [8, 128, 128])
    cache_rwkv_shift = din("cache_rwkv_shift", [1, D])
    state_rwkv = din("state_rwkv", [16, 64, 64])
    meta_tokens = din("meta_tokens", [NMETA, D])
    norm_mix = din("norm_mix", [2, D])
    norm_ffn = din("norm_ffn", [2, D])
    norm_final = din("norm_final", [D])
    gdn_w_in = din("gdn_w_in", [D, GIN])
    gdn_conv_w = din("gdn_conv_w", [4, 3072])
    gdn_a_log = din("gdn_a_log", [8])
    gdn_dt_bias = din("gdn_dt_bias", [8])
    gdn_o_norm = din("gdn_o_norm", [128])
    gdn_w_out = din("gdn_w_out", [D, D])
    rwkv_mu = din("rwkv_mu", [6, D])
    rwkv_w0 = din("rwkv_w0", [D])
    rwkv_w1 = din("rwkv_w1", [D, 64])
    rwkv_w2 = din("rwkv_w2", [64, D])
    rwkv_a0 = din("rwkv_a0", [D])
    rwkv_a1 = din("rwkv_a1", [D, 64])
    rwkv_a2 = din("rwkv_a2", [64, D])
    rwkv_g1 = din("rwkv_g1", [D, 160])
    rwkv_g2 = din("rwkv_g2", [160, D])
    rwkv_k_k = din("rwkv_k_k", [D])
    rwkv_k_a = din("rwkv_k_a", [D])
    rwkv_r_k = din("rwkv_r_k", [D])
    rwkv_w_r = din("rwkv_w_r", [D, D])
    rwkv_w_k = din("rwkv_w_k", [D, D])
    rwkv_w_v = din("rwkv_w_v", [D, D])
    rwkv_w_o = din("rwkv_w_o", [D, D])
    rwkv_ln_w = din("rwkv_ln_w", [D])
    rwkv_ln_b = din("rwkv_ln_b", [D])
    ffn_w_gate = din("ffn_w_gate", [2, D, DFF])
    ffn_w_up = din("ffn_w_up", [2, D, DFF])
    ffn_w_down = din("ffn_w_down", [2, DFF, D])

    y_prompt = dout("y_prompt", [SEQ, D])
    y_sample = dout("y_sample", [DEC, D])
    o_gdn_conv = {"p": dout("p_gdn_conv", [3, 3072]), "s": dout("s_gdn_conv", [3, 3072])}
    o_gdn_state = {"p": dout("p_gdn_state", [8, 128, 128]), "s": dout("s_gdn_state", [8, 128, 128])}
    o_rwkv_shift = {"p": dout("p_rwkv_shift", [1, D]), "s": dout("s_rwkv_shift", [1, D])}
    o_rwkv_state = {"p": dout("p_rwkv_state", [16, 64, 64]), "s": dout("s_rwkv_state", [16, 64, 64])}

    hs = [nc.dram_tensor(f"hscr{i}", [NROWS, D], F32, kind="Internal").ap() for i in range(3)]

    tiles = [("p", 0, NMETA, ("meta", 0))]
    for k in range(SEQ // 128):
        tiles.append(("p", NMETA + 128 * k, 128, ("x", k)))
    tiles.append(("s", TP, DEC, ("samp", 0)))

    with ExitStack() as es:
        sems = [es.enter_context(nc.semaphore(f"sm{i}")) for i in range(100)]
        P = Prog(sems)
        NW = 53100
        arena_t = es.enter_context(nc.sbuf_tensor("arena", [128, NW], F32))
        A = Arena(arena_t[:], NW)
        psf = [es.enter_context(nc.psum_tensor(f"ps{i}", [128, 512], F32)) for i in range(8)]
        psr = [Reg(f"ps{i}", psum=True) for i in range(8)]
        pcount = [0]

        def bank():
            i = pcount[0] % 8
            pcount[0] += 1
            return psf[i][:], psr[i]

        def bankb(b):
            return b.bitcast(BF16)

        rr = [0]

        def ev2():
            rr[0] += 1
            return "act" if rr[0] % 2 else "dve"

        def mm(out, lhsT, rhs, start, stop, reads, writes):
            P.op("pe", lambda e: e.matmul(out, lhsT, rhs, start=start, stop=stop), reads, writes)

        def tr(out, in_, ident, reads, writes):
            P.op("pe", lambda e: e.transpose(out, in_, ident), reads, writes)

        def act(out, in_, func, reads, writes, bias=None, scale=None, accum_out=None):
            kw = {}
            if bias is not None:
                kw["bias"] = bias
            if scale is not None:
                kw["scale"] = scale
            if accum_out is not None:
                kw["accum_out"] = accum_out
            P.op("act", lambda e: e.activation(out, in_, func, **kw), reads, writes)

        def tt(eng, out, in0, in1, op, reads, writes):
            P.op(eng, lambda e: e.tensor_tensor(out, in0, in1, op), reads, writes)

        def ts(eng, out, in0, s1, op0, reads, writes, s2=None, op1=None):
            if op1 is None:
                P.op(eng, lambda e: e.tensor_scalar(out, in0, s1, None, op0), reads, writes)
            else:
                P.op(eng, lambda e: e.tensor_scalar(out, in0, s1, s2, op0, op1), reads, writes)

        def stt(out, in0, scalar, in1, op0, op1, reads, writes):
            P.op("dve", lambda e: e.scalar_tensor_tensor(out, in0, scalar, in1, op0, op1), reads, writes)

        def cp(eng, out, in_, reads, writes):
            if eng == "act":
                P.op("act", lambda e: e.copy(out, in_), reads, writes)
            else:
                P.op(eng, lambda e: e.tensor_copy(out, in_), reads, writes)

        def recip(out, in_, reads, writes):
            P.op("dve", lambda e: e.reciprocal(out, in_), reads, writes)

        def memset(eng, ap, val, writes):
            P.op(eng, lambda e: e.memset(ap, val), (), writes)

        def dma(eng, out, in_, reads, writes, key, skip_waw=False, **kw):
            P.op(eng, lambda e: e.dma_start(out=out, in_=in_, **kw), reads, writes, dma=key, skip_waw=skip_waw)

        Rc = Reg("consts")
        ident_f = A.alloc([128, 128], F32)
        ident_b = A.alloc([128, 128], BF16)
        ones_f = A.alloc([128, 128], F32)
        ones_b = A.alloc([128, 128], BF16)
        zeros_f = A.alloc([128, 128], F32)
        imask_u = A.alloc([128, 128], F32)
        smask_u = A.alloc([128, 128], F32)
        smask_l = A.alloc([128, 128], F32)
        bdmask = A.alloc([128, 128], F32)
        offmask = A.alloc([128, 128], F32)
        negmask = A.alloc([128, 128], F32)
        eps_t = A.alloc([128, 4], F32)
        memset("pool", ones_f, 1.0, [Rc])
        memset("pool", zeros_f, 0.0, [Rc])
        memset("pool", eps_t[:, 0:1], EPS, [Rc])
        memset("pool", eps_t[:, 1:2], 1.0, [Rc])
        memset("pool", eps_t[:, 2:3], GN_EPS, [Rc])
        memset("pool", eps_t[:, 3:4], 0.0, [Rc])

        def asel(out, in_, cmp_op, fill, base, cm, step):
            P.op("pool", lambda e: e.affine_select(out, in_, [[step, 128]], cmp_op, fill,
                                                   base=base, channel_multiplier=cm), [Rc], [Rc])

        asel(imask_u, ones_f, ALU.is_ge, 0.0, 0, -1, 1)
        asel(smask_u, ones_f, ALU.is_gt, 0.0, 0, -1, 1)
        asel(smask_l, ones_f, ALU.is_gt, 0.0, 0, 1, -1)
        asel(ident_f, ones_f, ALU.is_equal, 0.0, 0, -1, 1)
        asel(negmask, zeros_f, ALU.is_ge, NEG, 0, -1, 1)
        cp("pool", bdmask, smask_u, [Rc], [Rc])
        memset("pool", bdmask[0:64, 64:128], 0.0, [Rc])
        memset("pool", offmask, 0.0, [Rc])
        memset("pool", offmask[0:64, 64:128], 1.0, [Rc])
        cp("pool", ident_b, ident_f, [Rc], [Rc])
        cp("pool", ones_b, ones_f, [Rc], [Rc])
        persist_off = A.off

        def load_w(dst, src, K, key, reg):
            for kc in range(K):
                dma("pool", dst[:, kc, :], src[kc * 128:(kc + 1) * 128, :], [], [reg], key,
                    skip_waw=(kc > 0), max_dma_last_dim=4096)

        def load_featmajor(dst, src_vec, C, stage, stage_reg, dst_reg, key):
            dma("sp", stage[0:C, 0:128], src_vec.rearrange("(c p) -> c p", p=128), [], [stage_reg], key)
            b, br = bank()
            tr(b[:, 0:C], stage[0:C, 0:128], ident_f[0:C, 0:C], [stage_reg, Rc], [br])
            cp("dve", dst, b[:, 0:C], [br], [dst_reg])

        def rmsnorm_T(xt, Rx, n, gfm, Rg, hn, Rhn, hnT, RhnT, small, Rsm, junk, Rjunk):
            act(junk[:n, :], xt[:n, :], AF.Square, [Rx], [Rjunk, Rsm], accum_out=small[:n, 0:1])
            act(small[:n, 1:2], small[:n, 0:1], AF.Ln, [Rsm, Rc], [Rsm], bias=eps_t[:n, 0:1], scale=1.0 / D)
            act(small[:n, 2:3], small[:n, 1:2], AF.Exp, [Rsm], [Rsm], scale=-0.5)
            ts("dve", hn[:n, :], xt[:n, :], small[:n, 2:3], ALU.mult, [Rx, Rsm], [Rhn])
            b, br = bank()
            bb = bankb(b).rearrange("p (a b) -> p a b", a=8, b=128)
            for kc in range(8):
                tr(bb[:, kc, :n], hn[:n, kc * 128:(kc + 1) * 128], ident_b[:n, :n], [Rhn, Rc], [br])
            for kc in range(8):
                eng = ev2()
                if eng == "act":
                    act(hnT[:, kc, :n], bb[:, kc, :n], AF.Copy, [br, Rg], [RhnT], scale=gfm[:, kc:kc + 1])
                else:
                    ts("dve", hnT[:, kc, :n], bb[:, kc, :n], gfm[:, kc:kc + 1], ALU.mult, [br, Rg], [RhnT])

        def neumann_gen(Bsrc, BTsrc, RB, RBT, n, nf, PR, RPR, PT, RPT, G, res, psum_acc=False):
            cur = 0
            tt("pool", PR[0][:n, :, 1, :n], Bsrc, ident_f[:n, :n].unsqueeze(1).to_broadcast([n, G, n]), ALU.add,
               [RB, Rc], [RPR[0]])
            Pk = Bsrc
            PTk = BTsrc
            RPk, RPTk = RB, RBT
            Rk = PR[0][:n, :, 1, :n]
            RRk = RPR[0]
            for k in range(0, nf):
                last = (k == nf - 1)
                if k == 0:
                    if nf == 1:
                        break
                    nxt = 1 - cur
                    b1, b1r = bank()
                    b2, b2r = bank()
                    v1 = b1.rearrange("p (g c) -> p g c", g=4, c=128)
                    v2 = b2.rearrange("p (g c) -> p g c", g=4, c=128)
                    for hh in range(G):
                        mm(v1[:n, hh, :n], PTk[:, hh, :], Pk[:, hh, :], True, True, [RPk, RPTk], [b1r])
                        mm(v2[:n, hh, :n], Pk[:, hh, :], PTk[:, hh, :], True, True, [RPk, RPTk], [b2r])
                    cp("act", PR[nxt][:n, :, 0, :n], v1[:n, 0:G, :n], [b1r], [RPR[nxt]])
                    cp("dve", PT[nxt][:n, :, :n], v2[:n, 0:G, :n], [b2r], [RPT[nxt]])
                    cp("pool", PR[nxt][:n, :, 1, :n], Rk, [RRk], [RPR[nxt]])
                    cur = nxt
                    Pk = PR[cur][:n, :, 0, :n]
                    PTk = PT[cur][:n, :, :n]
                    RPk, RPTk = RPR[cur], RPT[cur]
                    Rk = PR[cur][:n, :, 1, :n]
                    RRk = RPR[cur]
                    yield
                    continue
                nxt = 1 - cur
                if not last:
                    ba, bar = bank()
                    bb_, bbr = bank()
                    bc, bcr = bank()
                    va = ba.rearrange("p (g t c) -> p g t c", g=2, t=2, c=128)
                    vb = bb_.rearrange("p (g t c) -> p g t c", g=2, t=2, c=128)
                    vc = bc.rearrange("p (g c) -> p g c", g=4, c=128)
                    for hh in range(G):
                        tgt, tgr = (va, bar) if hh < 2 else (vb, bbr)
                        if n == 128:
                            mm(tgt[:n, hh % 2, :, :n], PTk[:, hh, :], PR[cur][:n, hh, :, :n], True, not psum_acc,
                               [RPk, RPTk, RRk], [tgr])
                        else:
                            for t_ in range(2):
                                mm(tgt[:n, hh % 2, t_, :n], PTk[:, hh, :], PR[cur][:n, hh, t_, :n], True,
                                   not (psum_acc and t_ == 1), [RPk, RPTk, RRk], [tgr])
                        if psum_acc:
                            mm(tgt[:n, hh % 2, 1, :n], ident_b[:n, :n], PR[cur][:n, hh, 1, :n], False, True,
                               [Rc, RRk], [tgr])
                        mm(vc[:n, hh, :n], Pk[:, hh, :], PTk[:, hh, :], True, True, [RPk, RPTk], [bcr])
                    for (vv, vr, h0) in ((va, bar, 0), (vb, bbr, 2)):
                        gcount = min(2, G - h0)
                        if gcount <= 0:
                            continue
                        cp("act", PR[nxt][:n, h0:h0 + gcount, 0, :n], vv[:n, 0:gcount, 0, :n], [vr], [RPR[nxt]])
                        if psum_acc:
                            cp("act", PR[nxt][:n, h0:h0 + gcount, 1, :n], vv[:n, 0:gcount, 1, :n], [vr], [RPR[nxt]])
                        else:
                            tt("dve", PR[nxt][:n, h0:h0 + gcount, 1, :n], vv[:n, 0:gcount, 1, :n],
                               PR[cur][:n, h0:h0 + gcount, 1, :n], ALU.add, [vr, RPR[cur]], [RPR[nxt]])
                    cp("act", PT[nxt][:n, :, :n], vc[:n, 0:G, :n], [bcr], [RPT[nxt]])
                else:
                    ba, bar = bank()
                    va = ba.rearrange("p (g c) -> p g c", g=4, c=128)
                    for hh in range(G):
                        mm(va[:n, hh, :n], PTk[:, hh, :], PR[cur][:n, hh, 1, :n], True, not psum_acc, [RPTk, RRk], [bar])
                        if psum_acc:
                            mm(va[:n, hh, :n], ident_b[:n, :n], PR[cur][:n, hh, 1, :n], False, True, [Rc, RRk], [bar])
                    if psum_acc:
                        cp("act", PR[nxt][:n, :, 1, :n], va[:n, 0:G, :n], [bar], [RPR[nxt]])
                    else:
                        tt("dve", PR[nxt][:n, :, 1, :n], va[:n, 0:G, :n], PR[cur][:n, :, 1, :n], ALU.add,
                           [bar, RPR[cur]], [RPR[nxt]])
                cur = nxt
                Pk = PR[cur][:n, :, 0, :n]
                PTk = PT[cur][:n, :, :n]
                RPk, RPTk = RPR[cur], RPT[cur]
                Rk = PR[cur][:n, :, 1, :n]
                RRk = RPR[cur]
                yield
            res.append((Rk, RRk))

        def neumann(Bsrc, BTsrc, RB, RBT, n, nf, PR, RPR, PT, RPT, G):
            res = []
            for _ in neumann_gen(Bsrc, BTsrc, RB, RBT, n, nf, PR, RPR, PT, RPT, G, res):
                pass
            return res[0]

        def phase_gdn():
            A.off = persist_off
            Rw = Reg("w_in")
            Rwo = Reg("w_out")
            w_in = A.alloc([128, 8, GIN], BF16)
            w_out = A.alloc([128, 8, D], BF16)
            diag = A.alloc([128, 96, 128], BF16)
            Rdiag = Reg("diag")
            load_w(w_in, gdn_w_in, 8, "w_in", Rw)
            load_w(w_out, gdn_w_out, 8, "w_out", Rwo)

            stage = A.alloc([128, 128], F32)
            Rst = Reg("stage")
            gfm = A.alloc([128, 8], F32)
            Rg = Reg("gfm")
            load_featmajor(gfm, norm_mix[0], 8, stage, Rst, Rg, "st")
            cw = A.alloc([128, 96], F32)
            Rcw = Reg("cw")
            dma("sp", stage[0:96, :], gdn_conv_w.rearrange("t (c p) -> (t c) p", p=128), [], [Rst], "st")
            b, br = bank()
            tr(b[:, 0:96], stage[0:96, :], ident_f[0:96, 0:96], [Rst, Rc], [br])
            cp("dve", cw, b[:, 0:96], [br], [Rcw])
            for idx in range(96):
                ts("pool" if idx % 2 else "dve", diag[:, idx, :], ident_f, cw[:, idx:idx + 1],
                   ALU.mult, [Rc, Rcw], [Rdiag])
            prm = A.alloc([128, 32], F32)
            Rprm = Reg("prm")
            dma("sp", prm[:, 0:8], gdn_a_log.partition_broadcast(128), [], [Rprm], "prm")
            dma("sp", prm[:, 8:16], gdn_dt_bias.partition_broadcast(128), [], [Rprm], "prm2")
            act(prm[:, 0:8], prm[:, 0:8], AF.Exp, [Rprm], [Rprm])
            ts("dve", prm[:, 0:8], prm[:, 0:8], -1.0, ALU.mult, [Rprm], [Rprm])
            onb = A.alloc([128, 128], F32)
            Ronb = Reg("onb")
            dma("sp", onb, gdn_o_norm.partition_broadcast(128), [], [Ronb], "onb")

            xt = [A.alloc([128, D], F32) for _ in range(2)]
            Rxt = [Reg("xt0"), Reg("xt1")]
            junk = A.alloc([128, D], BF16)
            Rjunk = Reg("junk")
            small = A.alloc([128, 8], F32)
            Rsm = Reg("small")
            hn = A.alloc([128, D], BF16)
            Rhn = Reg("hn")
            hnT = A.alloc([128, 8, 128], BF16)
            RhnT = Reg("hnT")
            yg = A.alloc([128, D], BF16)
            Ryg = Reg("yg")
            ygT = A.alloc([128, 8, 128], BF16)
            RygT = Reg("ygT")
            pre = A.alloc([128, 24, 132], BF16)
            Rpre = [Reg(f"pre{i}") for i in range(6)]
            nbT = A.alloc([128, 3, 24], F32)
            RnbT = Reg("nbT")
            tmpc = [A.alloc([128, 4, 128], F32) for _ in range(2)]
            Rtmpc = [Reg("tmpc0"), Reg("tmpc1")]
            sq = [A.alloc([128, 4, 128], BF16)]
            Rsq = [Reg("sq0")]
            tmp2 = [A.alloc([128, 4, 128], F32) for _ in range(2)]
            Rtmp2 = [Reg("tmp20"), Reg("tmp21")]
            qkn = A.alloc([128, 16, 128], BF16)
            Rqkn = [Reg(f"qkn{i}") for i in range(4)]
            vT = A.alloc([128, 8, 128], BF16)
            RvT = [Reg("vT0"), Reg("vT1")]
            kG = A.alloc([128, 8, 128], BF16)
            kd = A.alloc([128, 8, 128], BF16)
            vtok = A.alloc([128, 8, 128], BF16)
            Rktok = [Reg("ktok0"), Reg("ktok1")]
            zss = [A.alloc([128, D], BF16) for _ in range(2)]
            Rzss = [Reg("zs0"), Reg("zs1")]
            gs = A.alloc([128, 96], F32)
            Rgs = Reg("gs")
            Rgs2 = Reg("gs2")
            ET = A.alloc([128, 8, 128], F32)
            RET = [Reg(f"ET{i}") for i in range(4)]
            qgT = A.alloc([128, 8, 128], BF16)
            RqgT = [Reg(f"qgT{i}") for i in range(4)]
            qkT = A.alloc([128, 8, 128], BF16)
            RqkT = [Reg(f"qkT{i}") for i in range(4)]
            NSETS = 2
            NACT = NSETS
            SETS = []
            for si in range(NSETS):
                PRs = [A.alloc([128, 2, 2, 128], F32) for _ in range(2)]
                PTs = [A.alloc([128, 2, 128], F32) for _ in range(2)]
                SETS.append(dict(PR=PRs, RPR=[Reg(f"gPR{si}a"), Reg(f"gPR{si}b")], PT=PTs,
                                 RPT=[Reg(f"gPT{si}a"), Reg(f"gPT{si}b")],
                                 B0=A.alloc([128, 2, 128], F32), RB0=Reg(f"gB0{si}"),
                                 B0T=A.alloc([128, 2, 128], F32), RB0T=Reg(f"gB0T{si}"),
                                 tmpA=A.alloc([128, 2, 128], F32), RtmpA=Reg(f"gtmpA{si}"),
                                 OffT=A.alloc([128, 2, 128], F32), ROff=Reg(f"gOff{si}"),
                                 tmpE=A.alloc([128, 2, 128], F32), RtmpE=Reg(f"gtmpE{si}")))
            MinvT = A.alloc([128, 8, 128], BF16)
            RMinv = [Reg(f"Minv{i}") for i in range(4)]
            nsolw = A.alloc([128, 8, 128], BF16)
            Rnsolw = [Reg(f"nsolw{i}") for i in range(4)]
            vnew = A.alloc([128, 8, 128], BF16)
            Rvnew = [Reg(f"vnew{i}") for i in range(4)]
            S = A.alloc([128, 8, 128], F32)
            RS = [Reg(f"S{i}") for i in range(4)]
            Sb = A.alloc([128, 8, 128], BF16)
            RSb = [Reg(f"Sb{i}") for i in range(4)]
            o = A.alloc([128, D], F32)
            Ro = [Reg(f"o{i}") for i in range(4)]

            def load_x(ti):
                seq, row0, n, (kind, k) = tiles[ti]
                slot = ti % 2
                if kind == "meta":
                    src = meta_tokens
                elif kind == "x":
                    src = x_prompt[k * 128:(k + 1) * 128, :]
                else:
                    src = x_sample
                dma("sp", xt[slot][:n, :], src, [], [Rxt[slot]], f"xt{slot}")

            ext = {"g": None, "done": None}

            def step_ext():
                if ext["g"] is not None:
                    try:
                        next(ext["g"])
                    except StopIteration:
                        ext["g"] = None
                        if ext["done"] is not None:
                            ext["done"]()
                            ext["done"] = None

            def drain_ext():
                while ext["g"] is not None:
                    step_ext()

            def head(ti):
                seq, row0, n, (kind, k) = tiles[ti]
                slot = ti % 2
                x_ = xt[slot]
                Rx = Rxt[slot]
                first = (ti == 0) or (kind == "samp")
                lastt = (ti == len(tiles) - 2) or (kind == "samp")
                prev_n = tiles[ti - 1][2] if ti > 0 else None
                zs, Rzs = zss[ti % 2], Rzss[ti % 2]
                if first and seq == "p":
                    memset("pool", S, 0.0, RS)
                    memset("pool", Sb, 0.0, RSb)
                    memset("pool", pre[:, :, 0:3], 0.0, Rpre)
                elif first:
                    dma("sp", S, state_gdn.rearrange("h d e -> d h e"), [], RS, "Sld")
                    cp("pool", Sb, S, RS, RSb)
                    if DBG_B:
                        memset("pool", pre[:, :, 0:3], 0.0, Rpre)
                    else:
                        dma("sp", stage[0:72, :], cache_gdn_conv.rearrange("r (c p) -> (r c) p", p=128), [], [Rst], "st")
                        b, br = bank()
                        mm(b[:, 0:72], stage[0:72, :], ident_f[0:72, 0:72], True, True, [Rst, Rc], [br])
                        cp("dve", pre[:, :, 0:3], b[:, 0:72].rearrange("p (r c) -> p c r", r=3, c=24), [br], Rpre)
                else:
                    cp("pool", pre[:, :, 0:3], pre[:, :, prev_n:prev_n + 3], Rpre, Rpre)

                yield
                rmsnorm_T(x_, Rx, n, gfm, Rg, hn, Rhn, hnT, RhnT, small, Rsm, junk, Rjunk)

                yield
                for g4 in range(6):
                    b, br = bank()
                    bv = b.rearrange("p (a b) -> p a b", a=4, b=128)
                    for j in range(4):
                        c = g4 * 4 + j
                        for kc in range(8):
                            mm(bv[:, j, :n], w_in[:, kc, c * 128:(c + 1) * 128], hnT[:, kc, :n], kc == 0, kc == 7,
                               [Rw, RhnT], [br])
                    cp(ev2(), pre[:, g4 * 4:(g4 + 1) * 4, 3:3 + n], bv[:, :, :n], [br], [Rpre[g4]])
                    if lastt and not DBG_A:
                        cp("dve", nbT.rearrange("p r c -> p c r")[:, g4 * 4:(g4 + 1) * 4, :], bv[:, :, n - 3:n], [br], [RnbT])
                    yield
                if lastt and not DBG_A:
                    b, br = bank()
                    mm(b[0:72, 0:128], nbT.rearrange("p r c -> p (r c)"), ident_f, True, True, [RnbT, Rc], [br])
                    cp("dve", stage[0:72, :], b[0:72, 0:128], [br], [Rst])
                    dma("sp", o_gdn_conv[seq].rearrange("r (c p) -> (r c) p", p=128), stage[0:72, :], [Rst], [], "nb3" + seq)
                for s in range(2):
                    b, br = bank()
                    for kc in range(8):
                        mm(b[:n, :], hnT[:, kc, :n], w_in[:, kc, 3072 + s * 512:3072 + (s + 1) * 512], kc == 0, kc == 7,
                           [Rw, RhnT], [br])
                    act(zs[:n, s * 512:(s + 1) * 512], b[:n, :], AF.Silu, [br], [Rzs])
                    yield
                b, br = bank()
                for kc in range(8):
                    mm(b[:n, 0:16], hnT[:, kc, :n], w_in[:, kc, 4096:4112], kc == 0, kc == 7, [Rw, RhnT], [br])
                act(gs[:n, 0:8], b[:n, 8:16], AF.Sigmoid, [br], [Rgs])
                ts("dve", gs[:n, 8:16], gs[:n, 0:8], -1.0, ALU.mult, [Rgs], [Rgs])
                tt("dve", gs[:n, 16:24], b[:n, 0:8], prm[:n, 8:16], ALU.add, [br, Rprm], [Rgs])
                yield
                stt(gs[:n, 24:32], gs[:n, 16:24], -1.0, gs[:n, 16:24], ALU.mult, ALU.max, [Rgs], [Rgs])
                act(gs[:n, 24:32], gs[:n, 24:32], AF.Exp, [Rgs], [Rgs], scale=-1.0)
                act(gs[:n, 24:32], gs[:n, 24:32], AF.Ln, [Rgs, Rc], [Rgs], bias=eps_t[:n, 1:2])
                stt(gs[:n, 16:24], gs[:n, 16:24], 0.0, gs[:n, 24:32], ALU.max, ALU.add, [Rgs], [Rgs])
                tt("dve", gs[:n, 16:24], gs[:n, 16:24], prm[:n, 0:8], ALU.mult, [Rgs, Rprm], [Rgs])

                yield
                b, br = bank()
                mm(b[:n, 0:8], imask_u[:n, :n], gs[:n, 16:24], True, True, [Rc, Rgs], [br])
                mm(b[:, 8:16], ones_f[:n, :], gs[:n, 16:24], True, True, [Rc, Rgs], [br])
                cp("dve", gs[:n, 32:40], b[:n, 0:8], [br], [Rgs])
                act(gs[:n, 40:48], b[:n, 0:8], AF.Exp, [br], [Rgs])
                cp("dve", gs[:, 48:56], b[:, 8:16], [br], [Rgs])
                act(gs[:, 56:64], b[:, 8:16], AF.Exp, [br], [Rgs])
                tt("dve", gs[:n, 64:72], gs[:n, 48:56], gs[:n, 32:40], ALU.subtract, [Rgs], [Rgs])
                act(gs[:n, 64:72], gs[:n, 64:72], AF.Exp, [Rgs], [Rgs])


                yield

            def mid(ti):
                seq, row0, n, (kind, k) = tiles[ti]
                slot = ti % 2
                x_ = xt[slot]
                Rx = Rxt[slot]
                first = (ti == 0) or (kind == "samp")
                lastt = (ti == len(tiles) - 2) or (kind == "samp")
                prev_n = tiles[ti - 1][2] if ti > 0 else None
                zs, Rzs = zss[ti % 2], Rzss[ti % 2]
                nf = int(math.log2(min(n, 64)))

                def cd_gen():
                    for half in range(2):
                        for g4 in (half, 2 + half, 4 + half):
                            b, br = bank()
                            bv = b.rearrange("p (a b) -> p a b", a=4, b=128)
                            for j in range(4):
                                c = g4 * 4 + j
                                for tap in range(4):
                                    mm(bv[:, j, :n], diag[:, tap * 24 + c, :], pre[:, c, tap:tap + n], tap == 0, tap == 3,
                                       [Rdiag, Rpre[g4]], [br])
                            sl = g4 % 2
                            act(tmpc[sl][:, :, :n], bv[:, :, :n], AF.Silu, [br], [Rtmpc[sl]])
                            if g4 < 4:
                                tt("pool", sq[0][:, :, :n], tmpc[sl][:, :, :n], tmpc[sl][:, :, :n], ALU.mult,
                                   [Rtmpc[sl]], [Rsq[0]])
                                b2, b2r = bank()
                                b2v = b2.rearrange("p (a b) -> p a b", a=4, b=128)
                                if n == 128:
                                    mm(b2, ones_b, sq[0].rearrange("p a b -> p (a b)"), True, True, [Rc, Rsq[0]], [b2r])
                                else:
                                    for j in range(4):
                                        mm(b2v[:, j, :n], ones_b, sq[0][:, j, :n], True, True, [Rc, Rsq[0]], [b2r])
                                act(tmp2[sl][:, :, :n], b2v[:, :, :n], AF.Ln, [b2r, Rc], [Rtmp2[sl]], bias=eps_t[:, 0:1])
                                act(tmp2[sl][:, :, :n], tmp2[sl][:, :, :n], AF.Exp, [Rtmp2[sl]], [Rtmp2[sl]], scale=-0.5)
                                if g4 < 2:
                                    stt(qkn[:, g4 * 4:(g4 + 1) * 4, :n], tmpc[sl][:, :, :n], 128.0 ** -0.5, tmp2[sl][:, :, :n],
                                        ALU.mult, ALU.mult, [Rtmpc[sl], Rtmp2[sl]], [Rqkn[g4]])
                                else:
                                    tt("pool", qkn[:, g4 * 4:(g4 + 1) * 4, :n], tmpc[sl][:, :, :n], tmp2[sl][:, :, :n], ALU.mult,
                                       [Rtmpc[sl], Rtmp2[sl]], [Rqkn[g4]])
                            else:
                                cp("pool", vT[:, half * 4:(half + 1) * 4, :n], tmpc[sl][:, :, :n], [Rtmpc[sl]], [RvT[half]])
                            yield None
                        hs_ = slice(half * 4, half * 4 + 4)
                        b, br = bank()
                        bb = bankb(b).rearrange("p (a b) -> p a b", a=8, b=128)
                        for hh in range(4):
                            tr(bb[:n, hh, :], qkn[:, 8 + half * 4 + hh, :n], ident_b, [Rqkn[2 + half], Rc], [br])
                        bV2, bV2r = bank()
                        bbv = bankb(bV2).rearrange("p (a b) -> p a b", a=8, b=128)
                        for hh in range(4):
                            tr(bbv[:n, hh, :], vT[:, half * 4 + hh, :n], ident_b, [RvT[half], Rc], [bV2r])
                        tt("dve", kG[:n, hs_, :], bb[:n, 0:4, :], gs[:n, 40 + half * 4:44 + half * 4].unsqueeze(2).to_broadcast([n, 4, 128]),
                           ALU.mult, [br, Rgs], [Rktok[half]])
                        tt("dve", kd[:n, hs_, :], bb[:n, 0:4, :], gs[:n, 64 + half * 4:68 + half * 4].unsqueeze(2).to_broadcast([n, 4, 128]),
                           ALU.mult, [br, Rgs], [Rktok[half]])
                        cp("act", vtok[:n, hs_, :], bbv[:n, 0:4, :], [bV2r], [Rktok[half]])
                        yield half

                def f_gen(p, S_):
                    h0 = p * 2
                    half = p // 2
                    PR, RPR, PT, RPT = S_["PR"], S_["RPR"], S_["PT"], S_["RPT"]
                    B0, RB0, B0T, RB0T = S_["B0"], S_["RB0"], S_["B0T"], S_["RB0T"]
                    tmpA, RtmpA, OffT, ROff, tmpE, RtmpE = S_["tmpA"], S_["RtmpA"], S_["OffT"], S_["ROff"], S_["tmpE"], S_["RtmpE"]
                    Rq, Rk_ = Rqkn[half], Rqkn[2 + half]
                    bG, bGr = bank()
                    vG = bG.rearrange("p (g c) -> p g c", g=4, c=128)
                    for hh in range(2):
                        h = h0 + hh
                        mm(vG[:, hh, :n], gs[:n, 16 + h:17 + h].to_broadcast([n, 128]), imask_u[:n, :n], True, True,
                           [Rgs, Rc], [bGr])
                    for hh in range(2):
                        h = h0 + hh
                        stt(ET[:n, h, :n], vG[:n, hh, :n], gs[:n, 32 + h:33 + h], negmask[:n, :n], ALU.subtract, ALU.add,
                            [bGr, Rgs, Rc], [RET[p]])
                    act(ET[:n, h0:h0 + 2, :n], ET[:n, h0:h0 + 2, :n], AF.Exp, [RET[p]], [RET[p]])
                    act(tmpE[:, :, :n], vG[:, 0:2, :n], AF.Exp, [bGr], [RtmpE])
                    tt("pool", qgT[:, h0:h0 + 2, :n], qkn[:, h0:h0 + 2, :n], tmpE[:, :, :n], ALU.mult, [Rq, RtmpE], [RqgT[p]])
                    yield
                    bK, bKr = bank()
                    vK = bK.rearrange("p (g c) -> p g c", g=4, c=128)
                    for hh in range(2):
                        h = h0 + hh
                        mm(vK[:n, hh, :n], qkn[:, 8 + h, :n], qkn[:, 8 + h, :n], True, True, [Rk_], [bKr])
                        mm(vK[:n, 2 + hh, :n], qkn[:, 8 + h, :n], qkn[:, h, :n], True, True, [Rk_, Rq], [bKr])
                    for hh in range(2):
                        h = h0 + hh
                        stt(tmpA[:n, hh, :n], vK[:n, hh, :n], gs[:n, 8 + h:9 + h], ET[:n, h, :n], ALU.mult, ALU.mult,
                            [bKr, Rgs, RET[p]], [RtmpA])
                    tt("dve", qkT[:n, h0:h0 + 2, :n], vK[:n, 2:4, :n], ET[:n, h0:h0 + 2, :n], ALU.mult, [bKr, RET[p]], [RqkT[p]])
                    tt("pool", B0[:n, :, :n], tmpA[:n, :, :n], bdmask[:n, :n].unsqueeze(1).to_broadcast([n, 2, n]), ALU.mult,
                       [RtmpA, Rc], [RB0])
                    if n == 128:
                        tt("pool", OffT[:n, :, :n], tmpA[:n, :, :n], offmask[:n, :n].unsqueeze(1).to_broadcast([n, 2, n]), ALU.mult,
                           [RtmpA, Rc], [ROff])
                    yield
                    bT, bTr = bank()
                    vT_ = bT.rearrange("p (g c) -> p g c", g=4, c=128)
                    for hh in range(2):
                        mm(vT_[:n, hh, :n], B0[:n, hh, :n], ident_f[:n, :n], True, True, [RB0, Rc], [bTr])
                    cp("act", B0T[:n, :, :n], vT_[:n, 0:2, :n], [bTr], [RB0T])
                    yield
                    res = []
                    for _ in neumann_gen(B0[:n, :, :n], B0T[:n, :, :n], RB0, RB0T, n, nf, PR, RPR, PT, RPT, 2, res):
                        yield
                    Rfin, RRfin = res[0]
                    if n == 128:
                        Dm, RDm = PT[0], RPT[0]
                        T1, RT1 = PT[1], RPT[1]
                        T1T, RT1T = tmpA, RtmpA
                        b1, b1r = bank()
                        v1 = b1.rearrange("p (g c) -> p g c", g=4, c=128)
                        for hh in range(2):
                            mm(v1[:, hh, :], Rfin[:, hh, :], ident_f, True, True, [RRfin, Rc], [b1r])
                        cp("act", Dm, v1[:, 0:2, :], [b1r], [RDm])
                        yield
                        b2, b2r = bank()
                        v2 = b2.rearrange("p (g c) -> p g c", g=4, c=128)
                        for hh in range(2):
                            mm(v2[:, hh, :], Dm[:, hh, :], OffT[:, hh, :], True, True, [RDm, ROff], [b2r])
                        cp("dve", T1, v2[:, 0:2, :], [b2r], [RT1])
                        yield
                        b3, b3r = bank()
                        v3 = b3.rearrange("p (g c) -> p g c", g=4, c=128)
                        for hh in range(2):
                            mm(v3[:, hh, :], T1[:, hh, :], ident_f, True, True, [RT1, Rc], [b3r])
                        cp("act", T1T, v3[:, 0:2, :], [b3r], [RT1T])
                        yield
                        b4, b4r = bank()
                        v4 = b4.rearrange("p (g c) -> p g c", g=4, c=128)
                        for hh in range(2):
                            mm(v4[:, hh, :], T1T[:, hh, :], Rfin[:, hh, :], True, True, [RT1T, RRfin], [b4r])
                        tt("dve", MinvT[:, h0:h0 + 2, :], v4[:, 0:2, :], Rfin, ALU.add, [b4r, RRfin], [RMinv[p]])
                    else:
                        cp("dve", MinvT[:n, h0:h0 + 2, :n], Rfin, [RRfin], [RMinv[p]])
                    yield
                    bW, bWr = bank()
                    vW = bW.rearrange("p (g c) -> p g c", g=4, c=128)
                    for hh in range(2):
                        h = h0 + hh
                        mm(vW[:, hh, :n], kG[:n, h, :], MinvT[:n, h, :n], True, True, [Rktok[half], RMinv[p]], [bWr])
                    ts("dve", nsolw[:, h0:h0 + 2, :n], vW[:, 0:2, :n], -1.0, ALU.mult, [bWr], [Rnsolw[p]])
                    yield
                    bV, bVr = bank()
                    vV = bV.rearrange("p (g c) -> p g c", g=4, c=128)
                    for hh in range(2):
                        h = h0 + hh
                        mm(vV[:n, hh, :], MinvT[:n, h, :n], vtok[:n, h, :], True, False, [RMinv[p], Rktok[half]], [bVr])
                        mm(vV[:n, hh, :], nsolw[:, h, :n], Sb[:, h, :], False, True, [Rnsolw[p], RSb[p]], [bVr])
                    tt("dve", vnew[:n, h0:h0 + 2, :], vV[:n, 0:2, :],
                       gs[:n, h0:h0 + 2].unsqueeze(2).to_broadcast([n, 2, 128]), ALU.mult, [bVr, Rgs], [Rvnew[p]])
                    yield
                    drain_ext()
                    bO, bOr = bank()
                    vO = bO.rearrange("p (g c) -> p g c", g=4, c=128)
                    for hh in range(2):
                        h = h0 + hh
                        mm(vO[:n, hh, :], qgT[:, h, :n], Sb[:, h, :], True, False, [RqgT[p], RSb[p]], [bOr])
                        mm(vO[:n, hh, :], qkT[:n, h, :n], vnew[:n, h, :], False, True, [RqkT[p], Rvnew[p]], [bOr])
                    cp("act", o[:n, h0 * 128:(h0 + 2) * 128], bO[:n, 0:256], [bOr], [Ro[p]])
                    bS, bSr = bank()
                    vS = bS.rearrange("p (g c) -> p g c", g=4, c=128)
                    for hh in range(2):
                        h = h0 + hh
                        mm(vS[:, hh, :], kd[:n, h, :], vnew[:n, h, :], True, True, [Rktok[half], Rvnew[p]], [bSr])
                    for hh in range(2):
                        h = h0 + hh
                        stt(S[:, h, :], S[:, h, :], gs[:, 56 + h:57 + h], vS[:, hh, :], ALU.mult, ALU.add,
                            [RS[p], Rgs, bSr], [RS[p]])
                    cp("pool", Sb[:, h0:h0 + 2, :], S[:, h0:h0 + 2, :], [RS[p]], [RSb[p]])

                cg = cd_gen()
                ready = set()
                next_p = 0
                active = []
                cd_alive = True
                while cd_alive or active or next_p < 4:
                    if cd_alive:
                        try:
                            r = next(cg)
                            if r is not None:
                                ready.add(r)
                        except StopIteration:
                            cd_alive = False
                    while next_p < 4 and (next_p // 2) in ready and len(active) < NACT and (not DBG_B or not cd_alive):
                        active.append(f_gen(next_p, SETS[next_p % NSETS]))
                        next_p += 1
                    for g in list(active):
                        try:
                            next(g)
                        except StopIteration:
                            active.remove(g)
                    step_ext()
                drain_ext()


                if lastt:
                    dma("sp", o_gdn_state[seq].rearrange("h d e -> d h e"), S, RS, [], "Sst" + seq)

            def tail(ti):
                seq, row0, n, (kind, k) = tiles[ti]
                slot = ti % 2
                x_ = xt[slot]
                Rx = Rxt[slot]
                first = (ti == 0) or (kind == "samp")
                lastt = (ti == len(tiles) - 2) or (kind == "samp")
                prev_n = tiles[ti - 1][2] if ti > 0 else None
                zs, Rzs = zss[ti % 2], Rzss[ti % 2]
                for h in range(8):
                    act(junk[:n, h * 128:(h + 1) * 128], o[:n, h * 128:(h + 1) * 128], AF.Square, Ro, [Rjunk, Rgs2],
                        accum_out=gs[:n, 72 + h:73 + h])
                yield
                act(gs[:n, 80:88], gs[:n, 72:80], AF.Ln, [Rgs2, Rc], [Rgs2], bias=eps_t[:n, 0:1], scale=1.0 / 128)
                act(gs[:n, 80:88], gs[:n, 80:88], AF.Exp, [Rgs2], [Rgs2], scale=-0.5)
                ov = o.rearrange("p (a b) -> p a b", a=8, b=128)
                tt("dve", ov[:n], ov[:n], gs[:n, 80:88].unsqueeze(2).to_broadcast([n, 8, 128]), ALU.mult, Ro + [Rgs2], Ro)
                yield
                tt("pool", ov[:n], ov[:n], onb[:n, :].unsqueeze(1).to_broadcast([n, 8, 128]), ALU.mult, Ro + [Ronb], Ro)
                yield
                tt("dve", yg[:n, :], o[:n, :], zs[:n, :], ALU.mult, Ro + [Rzs], [Ryg])
                yield
                b, br = bank()
                bb = bankb(b).rearrange("p (a b) -> p a b", a=8, b=128)
                for kc in range(8):
                    tr(bb[:, kc, :n], yg[:n, kc * 128:(kc + 1) * 128], ident_b[:n, :n], [Ryg, Rc], [br])
                cp("act", ygT[:, :, :n], bb[:, :, :n], [br], [RygT])
                yield
                for s in range(2):
                    b, br = bank()
                    for kc in range(8):
                        mm(b[:n, :], ygT[:, kc, :n], w_out[:, kc, s * 512:(s + 1) * 512], kc == 0, kc == 7, [RygT, Rwo], [br])
                    tt("dve", x_[:n, s * 512:(s + 1) * 512], b[:n, :], x_[:n, s * 512:(s + 1) * 512], ALU.add, [br, Rx], [Rx])
                    yield
                dma("sp", hs[0][row0:row0 + n, :], x_[:n, :], [Rx], [], f"xst{slot}")

                yield

            def run_il(gens):
                gens = [g for g in gens if g is not None]
                while gens:
                    for g in list(gens):
                        try:
                            next(g)
                        except StopIteration:
                            gens.remove(g)

            load_x(0)
            if len(tiles) > 1:
                load_x(1)
            run_il([head(0)])
            for ti in range(len(tiles)):
                mid(ti)
                drain_ext()
                tg = tail(ti)
                hg = head(ti + 1) if ti + 1 < len(tiles) else None
                nxt_load = (lambda t2=ti + 2: load_x(t2)) if ti + 2 < len(tiles) else None
                tail_alive = True
                while hg is not None:
                    try:
                        next(hg)
                    except StopIteration:
                        hg = None
                    if tail_alive:
                        try:
                            next(tg)
                        except StopIteration:
                            tail_alive = False
                if tail_alive:
                    ext["g"], ext["done"] = tg, nxt_load
                    if ti + 1 >= len(tiles):
                        drain_ext()
                elif nxt_load is not None:
                    nxt_load()
            drain_ext()
            P.barrier()

        def phase_ffn(layer, src, dst, final):
            A.off = persist_off
            Rwg, Rwu, Rwd = Reg("wg"), Reg("wu"), Reg("wd")
            wg = A.alloc([128, 8, DFF], BF16)
            wu = A.alloc([128, 8, DFF], BF16)
            wd = A.alloc([128, NFC, D], BF16)
            load_w(wg, ffn_w_gate[layer], 8, "wg", Rwg)
            load_w(wu, ffn_w_up[layer], 8, "wu", Rwu)
            load_w(wd, ffn_w_down[layer], NFC, "wd", Rwd)
            stage = A.alloc([128, 128], F32)
            Rst = Reg("stage")
            gfm = A.alloc([128, 8], F32)
            Rg = Reg("gfm")
            load_featmajor(gfm, norm_ffn[layer], 8, stage, Rst, Rg, "st")
            if final:
                gfin = A.alloc([128, D], F32)
                Rgfin = Reg("gfin")
                dma("sp", gfin, norm_final.partition_broadcast(128), [], [Rgfin], "gfin")
            NSL = 6
            xt = [A.alloc([128, D], F32) for _ in range(NSL)]
            Rxt = [Reg(f"xt{i}") for i in range(NSL)]
            junk = A.alloc([128, D], BF16)
            Rjunk = Reg("junk")
            small = A.alloc([128, 8], F32)
            Rsm = Reg("small")
            hn = A.alloc([128, D], BF16)
            Rhn = Reg("hn")
            hnT = A.alloc([128, 8, 512], BF16)
            RhnT = Reg("hnT")
            sg = [A.alloc([128, 512], F32) for _ in range(2)]
            Rsg = [Reg("sg0"), Reg("sg1")]
            hT = A.alloc([128, NFC, 512], BF16)
            RhT = Reg("hT")

            macros = []
            cur, tot = [], 0
            for ti, tl in enumerate(tiles):
                if tot + tl[2] > 512:
                    macros.append(cur)
                    cur, tot = [], 0
                cur.append(ti)
                tot += tl[2]
            if cur:
                macros.append(cur)

            def load_x(ti):
                seq, row0, n, _ = tiles[ti]
                slot = ti % NSL
                dma("sp", xt[slot][:n, :], src[row0:row0 + n, :], [], [Rxt[slot]], f"fx{slot}")

            NPRE = 5
            for ti in range(min(NPRE, len(tiles))):
                load_x(ti)
            nloaded = min(NPRE, len(tiles))
            for mac in macros:
                offs = []
                NT = 0
                for ti in mac:
                    offs.append(NT)
                    NT += tiles[ti][2]
                for j, ti in enumerate(mac):
                    seq, row0, n, (kind, k) = tiles[ti]
                    slot = ti % NSL
                    rmsnorm_T(xt[slot], Rxt[slot], n, gfm, Rg, hn, Rhn, hnT[:, :, offs[j]:offs[j] + n], RhnT,
                              small, Rsm, junk, Rjunk)
                for fc in range(NFC):
                    bg, bgr = bank()
                    bu, bur = bank()
                    for kc in range(8):
                        mm(bg[:, :NT], wg[:, kc, fc * 128:(fc + 1) * 128], hnT[:, kc, :NT], kc == 0, kc == 7, [Rwg, RhnT], [bgr])
                    for kc in range(8):
                        mm(bu[:, :NT], wu[:, kc, fc * 128:(fc + 1) * 128], hnT[:, kc, :NT], kc == 0, kc == 7, [Rwu, RhnT], [bur])
                    sl = fc % 2
                    act(sg[sl][:, :NT], bg[:, :NT], AF.Silu, [bgr], [Rsg[sl]])
                    tt("dve", hT[:, fc, :NT], bu[:, :NT], sg[sl][:, :NT], ALU.mult, [bur, Rsg[sl]], [RhT])
                for j, ti in enumerate(mac):
                    seq, row0, n, (kind, k) = tiles[ti]
                    slot = ti % NSL
                    x_ = xt[slot]
                    Rx = Rxt[slot]
                    o0 = offs[j]
                    for s in range(2):
                        b, br = bank()
                        for fc in range(NFC):
                            mm(b[:n, :], hT[:, fc, o0:o0 + n], wd[:, fc, s * 512:(s + 1) * 512], fc == 0, fc == NFC - 1,
                               [RhT, Rwd], [br])
                        tt("dve", x_[:n, s * 512:(s + 1) * 512], b[:n, :], x_[:n, s * 512:(s + 1) * 512], ALU.add, [br, Rx], [Rx])
                    if not final:
                        dma("sp", dst[row0:row0 + n, :], x_[:n, :], [Rx], [], f"fs{slot}")
                    else:
                        act(junk[:n, :], x_[:n, :], AF.Square, [Rx], [Rjunk, Rsm], accum_out=small[:n, 4:5])
                        act(small[:n, 5:6], small[:n, 4:5], AF.Ln, [Rsm, Rc], [Rsm], bias=eps_t[:n, 0:1], scale=1.0 / D)
                        act(small[:n, 6:7], small[:n, 5:6], AF.Exp, [Rsm], [Rsm], scale=-0.5)
                        stt(x_[:n, :], x_[:n, :], small[:n, 6:7], gfin[:n, :], ALU.mult, ALU.mult, [Rx, Rsm, Rgfin], [Rx])
                        if kind == "x":
                            dma("sp", y_prompt[k * 128:(k + 1) * 128, :], x_[:n, :], [Rx], [], f"fs{slot}")
                        elif kind == "samp":
                            dma("sp", y_sample, x_[:n, :], [Rx], [], f"fs{slot}")
                    if nloaded < len(tiles):
                        load_x(nloaded)
                        nloaded += 1
            P.barrier()

        def phase_rwkv(src, dst):
            phase_rwkv_impl(src, dst)

        def phase_rwkv_impl(src, dst):
            A.off = persist_off
            Rwr, Rwk, Rwv, Rwo = Reg("w_r"), Reg("w_k"), Reg("w_v"), Reg("w_o")
            w_r = A.alloc([128, 8, D], BF16)
            w_k = A.alloc([128, 8, D], BF16)
            w_v = A.alloc([128, 8, D], BF16)
            w_o = A.alloc([128, 8, D], BF16)
            w1 = A.alloc([128, 8, 64], BF16)
            a1 = A.alloc([128, 8, 64], BF16)
            g1 = A.alloc([128, 8, 160], BF16)
            w2 = A.alloc([128, D], BF16)
            a2 = A.alloc([128, D], BF16)
            g2 = A.alloc([128, 2, D], BF16)
            Rlo = Reg("lora_w")
            load_w(w_r, rwkv_w_r, 8, "w_r", Rwr)
            load_w(w_k, rwkv_w_k, 8, "w_k", Rwk)
            load_w(w_v, rwkv_w_v, 8, "w_v", Rwv)
            load_w(w_o, rwkv_w_o, 8, "w_o", Rwo)
            load_w(w1, rwkv_w1, 8, "lo", Rlo)
            load_w(a1, rwkv_a1, 8, "lo", Rlo)
            load_w(g1, rwkv_g1, 8, "lo", Rlo)
            dma("pool", w2[0:64, :], rwkv_w2, [], [Rlo], "lo", max_dma_last_dim=4096)
            dma("pool", a2[0:64, :], rwkv_a2, [], [Rlo], "lo", max_dma_last_dim=4096)
            dma("pool", g2[:, 0, :], rwkv_g2[0:128, :], [], [Rlo], "lo", max_dma_last_dim=4096)
            dma("pool", g2[0:32, 1, :], rwkv_g2[128:160, :], [], [Rlo], "lo", max_dma_last_dim=4096)
            stage = A.alloc([128, 128], F32)
            Rst = Reg("stage")
            prm = A.alloc([128, 112], F32)
            Rprm = Reg("prm")
            dma("sp", stage[0:48, :], rwkv_mu.rearrange("m (c p) -> (m c) p", p=128), [], [Rst], "st")
            for i, vec in enumerate([rwkv_w0, rwkv_a0, rwkv_k_k, rwkv_k_a, rwkv_r_k, norm_mix[1]]):
                dma("sp", stage[48 + 8 * i:56 + 8 * i, :], vec.rearrange("(c p) -> c p", p=128), [], [Rst], "st")
            b, br = bank()
            tr(b[:, 0:96], stage[0:96, :], ident_f[0:96, 0:96], [Rst, Rc], [br])
            cp("dve", prm[:, 0:96], b[:, 0:96], [br], [Rprm])
            MU, W0, A0, KK, KA, RK, GM, OMKA = 0, 48, 56, 64, 72, 80, 88, 96
            ts("dve", prm[:, OMKA:OMKA + 8], prm[:, KA:KA + 8], -1.0, ALU.mult, [Rprm], [Rprm], s2=1.0, op1=ALU.add)
            lnw = A.alloc([128, D], F32)
            lnb = A.alloc([128, D], F32)
            Rln = Reg("ln")
            dma("sp", lnw, rwkv_ln_w.partition_broadcast(128), [], [Rln], "lnw")
            dma("sp", lnb, rwkv_ln_b.partition_broadcast(128), [], [Rln], "lnb")
            Rk2 = Reg("consts2")
            blk64 = A.alloc([128, 128], BF16)
            hm = A.alloc([128, 2], F32)
            headind = A.alloc([128, 2], BF16)
            nhmrow = A.alloc([128, 2, 128], BF16)
            hmrow = A.alloc([128, 2, 128], BF16)
            memset("pool", blk64, 0.0, [Rk2])
            memset("pool", blk64[0:64, 0:64], 1.0, [Rk2])
            memset("pool", blk64[64:128, 64:128], 1.0, [Rk2])
            memset("pool", hm, 0.0, [Rk2])
            memset("pool", hm[0:64, 0:1], 1.0, [Rk2])
            memset("pool", hm[64:128, 1:2], 1.0, [Rk2])
            cp("pool", headind, hm, [Rk2], [Rk2])
            memset("pool", hmrow, 0.0, [Rk2])
            memset("pool", hmrow[:, 0, 0:64], 1.0, [Rk2])
            memset("pool", hmrow[:, 1, 64:128], 1.0, [Rk2])
            memset("pool", nhmrow, 0.0, [Rk2])
            memset("pool", nhmrow[:, 0, 0:64], -1.0, [Rk2])
            memset("pool", nhmrow[:, 1, 64:128], -1.0, [Rk2])

            xt = [A.alloc([128, D], F32) for _ in range(2)]
            Rxt = [Reg("xt0"), Reg("xt1")]
            junk = A.alloc([128, D], BF16)
            Rjunk = Reg("junk")
            small = A.alloc([128, 8], F32)
            Rsm = Reg("small")
            hn = A.alloc([128, D], BF16)
            Rhn = Reg("hn")
            hnTe = A.alloc([128, 8, 130], F32)
            RhnT = Reg("hnTe")
            xm = A.alloc([128, 6, 8, 128], BF16)
            Rxm = [Reg(f"xm{m}") for m in range(6)]
            NTMP = 14
            tpblk = A.alloc([128, 2 * NTMP * 128], F32)
            tp = [[tpblk[:, (s_ * NTMP + i_) * 128:(s_ * NTMP + i_ + 1) * 128] for i_ in range(NTMP)] for s_ in range(2)]
            Rtp = [[Reg(f"tp{s}_{i}") for i in range(NTMP)] for s in range(2)]
            xx = tpblk[:, 0:1024].rearrange("p (a b) -> p a b", a=8, b=128)
            Rxx = Rtp[0][0:8]
            sqb = [A.alloc([128, 128], BF16) for _ in range(2)]
            Rsqb = [Reg("sqb0"), Reg("sqb1")]
            bt = A.alloc([128, 8, 128], BF16)
            kt = A.alloc([128, 8, 128], BF16)
            rkr = A.alloc([128, 8, 128], BF16)
            Rbt = [Reg(f"bt{i}") for i in range(8)]
            Rkt = [Reg(f"kt{i}") for i in range(8)]
            Rrkr = Reg("rkr")
            krp = A.alloc([128, 8, 4, 128], BF16)
            Rkrp = [Reg(f"krp{i}") for i in range(8)]
            nbtok = A.alloc([128, 8, 2, 128], BF16)
            ktok = A.alloc([128, 8, 2, 128], BF16)
            Rtok = [Reg(f"tok{i}") for i in range(8)]
            GL = A.alloc([128, 8], F32)
            RGL = [Reg(f"GL{i}") for i in range(8)]
            lo1 = A.alloc([128, 2, 128], BF16)
            lo3 = A.alloc([128, 2, 128], BF16)
            Rlo1, Rlo3 = Reg("lo1"), Reg("lo3")
            vfs = [A.alloc([128, D], F32) for _ in range(2)]
            Rvfs = [Reg("vf0"), Reg("vf1")]
            vb = A.alloc([128, D], BF16)
            Rv = Reg("v")
            gates = [A.alloc([128, D], BF16) for _ in range(2)]
            Rgates = [Reg("gate0"), Reg("gate1")]
            NSETS = 3
            SETS = []
            for si in range(NSETS):
                PRs = [A.alloc([128, 2, 2, 128], BF16) for _ in range(2)]
                RPRs = [Reg(f"PR{si}a"), Reg(f"PR{si}b")]
                PTs = [A.alloc([128, 2, 128], BF16) for _ in range(2)]
                RPTs = [Reg(f"PT{si}a"), Reg(f"PT{si}b")]
                B0s = A.alloc([128, 2, 128], BF16)
                B0Ts = A.alloc([128, 2, 128], BF16)
                SETS.append((PRs, RPRs, PTs, RPTs, B0s, Reg(f"B0_{si}"), B0Ts, Reg(f"B0T_{si}"),
                             A.alloc([128, 2, 128], BF16), A.alloc([128, 2, 128], BF16), A.alloc([128, 2, 128], BF16),
                             Reg(f"mats{si}"), A.alloc([128, 2, 128], BF16), Reg(f"Minv{si}"),
                             A.alloc([128, 2, 64], BF16), A.alloc([128, 2, 64], BF16), Reg(f"RHS{si}"), Reg(f"U{si}")))
            yv = A.alloc([128, D], F32)
            Ry = Reg("y")
            ysq = A.alloc([128, D], F32)
            Rysq = Reg("ysq")
            st16 = A.alloc([128, 96], F32)
            Rst16 = Reg("st16")
            Pst = A.alloc([128, 8, 64], F32)
            Pb = A.alloc([128, 8, 64], BF16)
            PG = A.alloc([128, 2, 64], F32)
            RP = [Reg(f"P{i}") for i in range(8)]
            RPb = [Reg(f"Pb{i}") for i in range(8)]
            RPG = [Reg("PG0"), Reg("PG1")]
            Snat = tpblk[:, NTMP * 128:NTMP * 128 + 1024].rearrange("p (a b) -> p a b", a=16, b=64)
            RSnat = Rtp[1][0:8]
            yg = A.alloc([128, D], BF16)
            Ryg = Reg("yg")
            ygT = A.alloc([128, 8, 128], BF16)
            RygT = Reg("ygT")
            hnT = hnTe[:, :, 1:129]

            def load_x(ti):
                seq, row0, n, _ = tiles[ti]
                slot = ti % 2
                dma("sp", xt[slot][:n, :], src[row0:row0 + n, :], [], [Rxt[slot]], f"xt{slot}")

            ext = {"g": None, "done": None}

            def step_ext():
                if ext["g"] is not None:
                    try:
                        next(ext["g"])
                    except StopIteration:
                        ext["g"] = None
                        if ext["done"] is not None:
                            ext["done"]()
                            ext["done"] = None

            def drain_ext():
                while ext["g"] is not None:
                    step_ext()

            def head(ti):
                seq, row0, n, (kind, k) = tiles[ti]
                slot = ti % 2
                x_ = xt[slot]
                Rx = Rxt[slot]
                first = (ti == 0) or (kind == "samp")
                lastt = (ti == len(tiles) - 2) or (kind == "samp")
                prev_n = tiles[ti - 1][2] if ti > 0 else None
                gate, Rgate = gates[ti % 2], Rgates[ti % 2]
                vf, Rvf = vfs[ti % 2], Rvfs[ti % 2]
                XR, XW, XK, XV, XA, XG = 0, 1, 2, 3, 4, 5
                if first and seq == "p":
                    memset("pool", Pst, 0.0, RP)
                    memset("pool", Pb, 0.0, RPb)
                    memset("pool", hnTe[:, :, 0:1], 0.0, [RhnT])
                elif first:
                    dma("sp", Snat[0:64, :, :], state_rwkv.rearrange("h i j -> i h j"), [], RSnat, "Sld")
                    for half in range(2):
                        b, br = bank()
                        bv = b.rearrange("p (a b) -> p a b", a=8, b=64)
                        for j in range(4):
                            oc = half * 4 + j
                            tr(bv[:, j, :], Snat[0:64, 2 * oc:2 * oc + 2, :].rearrange("p a b -> p (a b)"),
                               ident_f[0:64, 0:64], RSnat + [Rc], [br])
                        cp("dve", Pst[:, half * 4:(half + 1) * 4, :], bv[:, 0:4, :], [br], RP)
                    cp("pool", Pb, Pst, RP, RPb)
                    dma("sp", stage[0:8, :], cache_rwkv_shift.rearrange("o (c p) -> (o c) p", p=128), [], [Rst], "st")
                    b, br = bank()
                    tr(b[:, 0:8], stage[0:8, :], ident_f[0:8, 0:8], [Rst, Rc], [br])
                    cp("dve", hnTe[:, :, 0:1], b[:, 0:8].unsqueeze(2), [br], [RhnT])
                else:
                    cp("pool", hnTe[:, :, 0:1], hnTe[:, :, prev_n:prev_n + 1], [RhnT], [RhnT])

                yield
                rmsnorm_T(x_, Rx, n, prm[:, GM:GM + 8], Rprm, hn, Rhn, hnT, RhnT, small, Rsm, junk, Rjunk)
                if lastt:
                    b, br = bank()
                    tr(b[0:8, 0:128], hnTe[:, :, n:n + 1].rearrange("p a b -> p (a b)"), ident_f, [RhnT, Rc], [br])
                    cp("dve", stage[0:8, :], b[0:8, 0:128], [br], [Rst])
                    dma("sp", o_rwkv_shift[seq].rearrange("o (c p) -> (o c) p", p=128), stage[0:8, :], [Rst], [], "shst" + seq)
                yield
                tt("pool", xx[:, :, :n], hnTe[:, :, 0:n], hnTe[:, :, 1:n + 1], ALU.subtract, [RhnT], Rxx)
                for m in range(6):
                    eng = "dve" if m % 2 == 0 else "pool"
                    tt(eng, xm[:, m, :, :n], xx[:, :, :n],
                       prm[:, MU + 8 * m:MU + 8 * m + 8].unsqueeze(2).to_broadcast([128, 8, n]), ALU.mult,
                       Rxx + [Rprm], [Rxm[m]])
                    tt(eng, xm[:, m, :, :n], xm[:, m, :, :n], hnTe[:, :, 1:n + 1], ALU.add, [Rxm[m], RhnT], [Rxm[m]])
                    if m % 2 == 1:
                        yield
                XR, XW, XK, XV, XA, XG = 0, 1, 2, 3, 4, 5
                b, br = bank()
                for kc in range(8):
                    mm(b[0:64, 0:n], w1[:, kc, :], xm[:, XW, kc, :n], kc == 0, kc == 7, [Rlo, Rxm[XW]], [br])
                for kc in range(8):
                    mm(b[0:64, 128:128 + n], a1[:, kc, :], xm[:, XA, kc, :n], kc == 0, kc == 7, [Rlo, Rxm[XA]], [br])
                act(lo1[0:64, 0, :n], b[0:64, 0:n], AF.Tanh, [br], [Rlo1])
                cp("dve", lo1[0:64, 1, :n], b[0:64, 128:128 + n], [br], [Rlo1])
                yield
                b, br = bank()
                for kc in range(8):
                    mm(b[:, 0:n], g1[:, kc, 0:128], xm[:, XG, kc, :n], kc == 0, kc == 7, [Rlo, Rxm[XG]], [br])
                for kc in range(8):
                    mm(b[0:32, 128:128 + n], g1[:, kc, 128:160], xm[:, XG, kc, :n], kc == 0, kc == 7, [Rlo, Rxm[XG]], [br])
                act(lo3[:, 0, :n], b[:, 0:n], AF.Sigmoid, [br], [Rlo3])
                act(lo3[0:32, 1, :n], b[0:32, 128:128 + n], AF.Sigmoid, [br], [Rlo3])
                yield
                for s in range(2):
                    b, br = bank()
                    mm(b[:n, :], lo3[:, 0, :n], g2[:, 0, s * 512:(s + 1) * 512], True, False, [Rlo3, Rlo], [br])
                    mm(b[:n, :], lo3[0:32, 1, :n], g2[0:32, 1, s * 512:(s + 1) * 512], False, True, [Rlo3, Rlo], [br])
                    cp("act", gate[:n, s * 512:(s + 1) * 512], b[:n, :], [br], [Rgate])
                    yield
                for s in range(2):
                    b, br = bank()
                    for kc in range(8):
                        mm(b[:n, :], xm[:, XV, kc, :n], w_v[:, kc, s * 512:(s + 1) * 512], kc == 0, kc == 7, [Rxm[XV], Rwv], [br])
                    cp("act", vf[:n, s * 512:(s + 1) * 512], b[:n, :], [br], [Rvf])
                    cp("dve", vb[:n, s * 512:(s + 1) * 512], b[:n, :], [br], [Rv])
                    yield


                yield

            def mid(ti):
                seq, row0, n, (kind, k) = tiles[ti]
                slot = ti % 2
                x_ = xt[slot]
                Rx = Rxt[slot]
                first = (ti == 0) or (kind == "samp")
                lastt = (ti == len(tiles) - 2) or (kind == "samp")
                prev_n = tiles[ti - 1][2] if ti > 0 else None
                gate, Rgate = gates[ti % 2], Rgates[ti % 2]
                vf, Rvf = vfs[ti % 2], Rvfs[ti % 2]
                XR, XW, XK, XV, XA, XG = 0, 1, 2, 3, 4, 5
                nf = int(math.log2(n))
                by = [(psf[6][:], psr[6]), (psf[7][:], psr[7])]
                LD, C_, AT, KX, RT_, KKn, TKA, K2, TMP, E1, E2, E3, RR, KR = range(14)

                def prep_gen():
                    for pair in range(4):
                        ocs = (2 * pair, 2 * pair + 1)
                        T2 = {oc: tp[oc % 2] for oc in ocs}
                        R2 = {oc: Rtp[oc % 2] for oc in ocs}
                        bl2 = {}
                        for oc in ocs:
                            T_, RT = T2[oc], R2[oc]
                            osl = slice(oc * 128, (oc + 1) * 128)
                            br_, brr = bank()
                            bk_, bkr = bank()
                            bl_, blr = bank()
                            bl2[oc] = (bl_, blr)
                            for kc in range(8):
                                mm(br_[:, 0:n], w_r[:, kc, osl], xm[:, XR, kc, :n], kc == 0, kc == 7, [Rwr, Rxm[XR]], [brr])
                            for kc in range(8):
                                mm(bk_[:, 0:n], w_k[:, kc, osl], xm[:, XK, kc, :n], kc == 0, kc == 7, [Rwk, Rxm[XK]], [bkr])
                            mm(bl_[:, 0:n], w2[0:64, osl], lo1[0:64, 0, :n], True, True, [Rlo, Rlo1], [blr])
                            mm(bl_[:, 128:128 + n], a2[0:64, osl], lo1[0:64, 1, :n], True, True, [Rlo, Rlo1], [blr])
                            cp("act", T_[RR][:, :n], br_[:, 0:n], [brr], [RT[RR]])
                            cp("act", T_[KR][:, :n], bk_[:, 0:n], [bkr], [RT[KR]])
                            act(T_[KX][:, :n], bk_[:, 0:n], AF.Copy, [bkr, Rprm], [RT[KX]], scale=prm[:, KK + oc:KK + oc + 1])
                            act(T_[LD][:, :n], bl_[:, 0:n], AF.Sigmoid, [blr, Rprm], [RT[LD]], bias=prm[:, W0 + oc:W0 + oc + 1])
                            act(T_[AT][:, :n], bl_[:, 128:128 + n], AF.Sigmoid, [blr, Rprm], [RT[AT]], bias=prm[:, A0 + oc:A0 + oc + 1])
                        steps = []
                        for oc in ocs:
                            T_, RT = T2[oc], R2[oc]
                            sl = oc % 2
                            L = []
                            L.append(lambda T_=T_, RT=RT: ts("pool", T_[LD][:, :n], T_[LD][:, :n], -math.exp(-0.5), ALU.mult, [RT[LD]], [RT[LD]]))
                            L.append(lambda T_=T_, RT=RT: P.op("dve", lambda e, o_=T_[C_][:, :n], d0=ones_f[:, :n], d1=T_[LD][:, :n]:
                                     e.tensor_tensor_scan(o_, d0, d1, 0.0, ALU.mult, ALU.add), [Rc, RT[LD]], [RT[C_]]))
                            L.append(lambda T_=T_, RT=RT, sl=sl: tt("pool", sqb[sl][:, :n], T_[KX][:, :n], T_[KX][:, :n], ALU.mult, [RT[KX]], [Rsqb[sl]]))

                            def ssq(T_=T_, RT=RT, sl=sl):
                                bs_, bsr = bank()
                                mm(bs_[:, 0:n], blk64, sqb[sl][:, :n], True, True, [Rk2, Rsqb[sl]], [bsr])
                                act(T_[RT_][:, :n], bs_[:, 0:n], AF.Ln, [bsr, Rc], [RT[RT_]], bias=eps_t[:, 0:1])
                            L.append(ssq)
                            L.append(lambda T_=T_, RT=RT, oc=oc: ts("dve", T_[TKA][:, :n], T_[AT][:, :n], prm[:, KA + oc:KA + oc + 1], ALU.mult, [RT[AT], Rprm], [RT[TKA]],
                                     s2=prm[:, OMKA + oc:OMKA + oc + 1], op1=ALU.add))
                            L.append(lambda T_=T_, RT=RT: act(T_[RT_][:, :n], T_[RT_][:, :n], AF.Exp, [RT[RT_]], [RT[RT_]], scale=-0.5))
                            L.append(lambda T_=T_, RT=RT: tt("pool", T_[TMP][:, :n], T_[C_][:, :n], T_[LD][:, :n], ALU.subtract, [RT[C_], RT[LD]], [RT[TMP]]))
                            L.append(lambda T_=T_, RT=RT: act(T_[E1][:, :n], T_[TMP][:, :n], AF.Exp, [RT[TMP]], [RT[E1]]))
                            L.append(lambda T_=T_, RT=RT: tt("dve", T_[K2][:, :n], T_[KR][:, :n], T_[TKA][:, :n], ALU.mult, [RT[KR], RT[TKA]], [RT[K2]]))
                            L.append(lambda T_=T_, RT=RT: act(T_[E2][:, :n], T_[C_][:, :n], AF.Exp, [RT[C_]], [RT[E2]], scale=-1.0))
                            L.append(lambda T_=T_, RT=RT: tt("pool", T_[KKn][:, :n], T_[KX][:, :n], T_[RT_][:, :n], ALU.mult, [RT[KX], RT[RT_]], [RT[KKn]]))
                            L.append(lambda T_=T_, RT=RT: act(T_[E3][:, :n], T_[C_][:, :n], AF.Exp, [RT[C_]], [RT[E3]]))
                            L.append(lambda T_=T_, RT=RT, oc=oc: tt("pool", kt[:, oc, :n], T_[K2][:, :n], T_[E2][:, :n], ALU.mult, [RT[K2], RT[E2]], [Rkt[oc]]))
                            for h2 in range(2):
                                L.append(lambda T_=T_, RT=RT, oc=oc, h2=h2: stt(krp[:, oc, 2 * h2 + 0, :n], T_[KKn][:, :n], hm[:, h2:h2 + 1], T_[E1][:, :n], ALU.mult, ALU.mult,
                                         [RT[KKn], Rk2, RT[E1]], [Rkrp[oc]]))
                            L.append(lambda T_=T_, RT=RT: tt("pool", T_[TMP][:, :n], T_[KKn][:, :n], T_[AT][:, :n], ALU.mult, [RT[KKn], RT[AT]], [RT[TMP]]))
                            for h2 in range(2):
                                L.append(lambda T_=T_, RT=RT, oc=oc, h2=h2: stt(krp[:, oc, 2 * h2 + 1, :n], T_[RR][:, :n], hm[:, h2:h2 + 1], T_[E3][:, :n], ALU.mult, ALU.mult,
                                         [RT[RR], Rk2, RT[E3]], [Rkrp[oc]]))
                            L.append(lambda T_=T_, RT=RT, oc=oc: tt("pool", bt[:, oc, :n], T_[TMP][:, :n], T_[E2][:, :n], ALU.mult, [RT[TMP], RT[E2]], [Rbt[oc]]))
                            L.append(lambda T_=T_, RT=RT, oc=oc: cp("pool", GL[:, oc:oc + 1], T_[E3][:, n - 1:n], [RT[E3]], [RGL[oc]]))
                            L.append(lambda T_=T_, RT=RT, oc=oc: stt(rkr[:, oc, :n], T_[RR][:, :n], prm[:, RK + oc:RK + oc + 1], T_[K2][:, :n], ALU.mult, ALU.mult,
                                     [RT[RR], Rprm, RT[K2]], [Rrkr]))
                            steps.append(L)
                        for i in range(len(steps[0])):
                            for L in steps:
                                L[i]()
                            if i % 6 == 5:
                                yield None
                        yield ocs[0]
                        yield ocs[1]

                def g_gen(oc, S_):
                    (PRs, RPRs, PTs, RPTs, B0s, RB0s, B0Ts, RB0Ts, BrTn, AkT, BkT, Rmats, MinvTs, RMinvs, RHSb, Ub, RRHS, RU) = S_
                    b, br = bank()
                    bb = bankb(b).rearrange("p (a b) -> p a b", a=8, b=128)
                    tr(bb[:n, 0, :], bt[:, oc, :n], ident_b, [Rbt[oc], Rc], [br])
                    tr(bb[:n, 1, :], kt[:, oc, :n], ident_b, [Rkt[oc], Rc], [br])
                    tt("dve", nbtok[:n, oc], bb[:n, 0:1, :].to_broadcast([n, 2, 128]), nhmrow[:n], ALU.mult, [br, Rk2], [Rtok[oc]])
                    tt("dve", ktok[:n, oc], bb[:n, 1:2, :].to_broadcast([n, 2, 128]), hmrow[:n], ALU.mult, [br, Rk2], [Rtok[oc]])
                    yield
                    bm1, bm1r = bank()
                    bm2, bm2r = bank()
                    bm3, bm3r = bank()
                    vm1 = bm1.rearrange("p (h t c) -> p h t c", h=2, t=2, c=128)
                    vm2 = bm2.rearrange("p (h t c) -> p h t c", h=2, t=2, c=128)
                    vm3 = bm3.rearrange("p (g c) -> p g c", g=4, c=128)
                    for h2 in range(2):
                        if n == 128:
                            mm(vm1[:n, h2, :, :n], bt[:, oc, :n], krp[:, oc, 2 * h2:2 * h2 + 2, :n], True, True, [Rbt[oc], Rkrp[oc]], [bm1r])
                            mm(vm2[:n, h2, :, :n], kt[:, oc, :n], krp[:, oc, 2 * h2:2 * h2 + 2, :n], True, True, [Rkt[oc], Rkrp[oc]], [bm2r])
                        else:
                            for t_ in range(2):
                                mm(vm1[:n, h2, t_, :n], bt[:, oc, :n], krp[:, oc, 2 * h2 + t_, :n], True, True, [Rbt[oc], Rkrp[oc]], [bm1r])
                                mm(vm2[:n, h2, t_, :n], kt[:, oc, :n], krp[:, oc, 2 * h2 + t_, :n], True, True, [Rkt[oc], Rkrp[oc]], [bm2r])
                        mm(vm3[:n, h2, :n], krp[:, oc, 2 * h2, :n], bt[:, oc, :n], True, True, [Rkrp[oc], Rbt[oc]], [bm3r])
                    stt(B0s[:n, :, :n], vm1[:n, :, 0, :n], -1.0, smask_u[:n, :n].unsqueeze(1).to_broadcast([n, 2, n]),
                        ALU.mult, ALU.mult, [bm1r, Rc], [RB0s])
                    stt(BrTn[:n, :, :n], vm1[:n, :, 1, :n], -1.0, imask_u[:n, :n].unsqueeze(1).to_broadcast([n, 2, n]),
                        ALU.mult, ALU.mult, [bm1r, Rc], [Rmats])
                    tt("dve", AkT[:n, :, :n], vm2[:n, :, 0, :n], smask_u[:n, :n].unsqueeze(1).to_broadcast([n, 2, n]),
                       ALU.mult, [bm2r, Rc], [Rmats])
                    tt("dve", BkT[:n, :, :n], vm2[:n, :, 1, :n], imask_u[:n, :n].unsqueeze(1).to_broadcast([n, 2, n]),
                       ALU.mult, [bm2r, Rc], [Rmats])
                    stt(B0Ts[:n, :, :n], vm3[:n, 0:2, :n], -1.0, smask_l[:n, :n].unsqueeze(1).to_broadcast([n, 2, n]),
                        ALU.mult, ALU.mult, [bm3r, Rc], [RB0Ts])
                    yield
                    res = []
                    for _ in neumann_gen(B0s[:n, :, :n], B0Ts[:n, :, :n], RB0s, RB0Ts, n, nf, PRs, RPRs, PTs, RPTs, 2, res, psum_acc=True):
                        yield
                    Rfin, RRfin = res[0]
                    cp("act", MinvTs[:n, :, :n], Rfin, [RRfin], [RMinvs])
                    bR, bRr = bank()
                    vR = bR[:, 0:128].rearrange("p (g c) -> p g c", g=2, c=64)
                    for h2 in range(2):
                        hd = 2 * oc + h2
                        mm(vR[:n, h2, :], krp[:, oc, 2 * h2, :n], Pb[:, oc, :], True, False, [Rkrp[oc], RPb[oc]], [bRr])
                        mm(vR[:n, h2, :], AkT[:n, h2, :n], vb[:n, hd * 64:(hd + 1) * 64], False, True, [Rmats, Rv], [bRr])
                    cp("dve", RHSb[:n], vR[:n], [bRr], [RRHS])
                    yield
                    bU, bUr = bank()
                    vU = bU[:, 0:128].rearrange("p (g c) -> p g c", g=2, c=64)
                    for h2 in range(2):
                        mm(vU[:n, h2, :], MinvTs[:n, h2, :n], RHSb[:n, h2, :], True, True, [RMinvs, RRHS], [bUr])
                    cp("act", Ub[:n], vU[:n], [bUr], [RU])
                    yield
                    drain_ext()
                    bY, bYr = bank()
                    vY = bY[:, 0:128].rearrange("p (g c) -> p g c", g=2, c=64)
                    for h2 in range(2):
                        hd = 2 * oc + h2
                        mm(vY[:n, h2, :], krp[:, oc, 2 * h2 + 1, :n], Pb[:, oc, :], True, False, [Rkrp[oc], RPb[oc]], [bYr])
                        mm(vY[:n, h2, :], BrTn[:n, h2, :n], Ub[:n, h2, :], False, False, [Rmats, RU], [bYr])
                        mm(vY[:n, h2, :], BkT[:n, h2, :n], vb[:n, hd * 64:(hd + 1) * 64], False, True, [Rmats, Rv], [bYr])
                    cp("act", yv[:n, oc * 128:(oc + 1) * 128], bY[:n, 0:128], [bYr], [Ry])
                    bP, bPr = bank()
                    for h2 in range(2):
                        hd = 2 * oc + h2
                        mm(bP[:, 0:64], nbtok[:n, oc, h2, :], Ub[:n, h2, :], h2 == 0, False, [Rtok[oc], RU], [bPr])
                        mm(bP[:, 0:64], ktok[:n, oc, h2, :], vb[:n, hd * 64:(hd + 1) * 64], False, h2 == 1, [Rtok[oc], Rv], [bPr])
                    ts("pool", PG[:, oc % 2, :], Pst[:, oc, :], GL[:, oc:oc + 1], ALU.mult, [RP[oc], RGL[oc]], [RPG[oc % 2]])
                    stt(Pst[:, oc, :], bP[:, 0:64], GL[:, oc:oc + 1], PG[:, oc % 2, :], ALU.mult, ALU.add,
                        [bPr, RGL[oc], RPG[oc % 2]], [RP[oc]])
                    cp("pool", Pb[:, oc, :], Pst[:, oc, :], [RP[oc]], [RPb[oc]])

                pg = prep_gen()
                ready = set()
                next_g = 0
                active = []
                prep_alive = True
                while prep_alive or active or next_g < 8:
                    if prep_alive:
                        try:
                            r = next(pg)
                            if r is not None:
                                ready.add(r)
                        except StopIteration:
                            prep_alive = False
                    while next_g < 8 and next_g in ready and len(active) < NSETS:
                        active.append(g_gen(next_g, SETS[next_g % NSETS]))
                        next_g += 1
                    for g in list(active):
                        try:
                            next(g)
                        except StopIteration:
                            active.remove(g)
                    step_ext()
                drain_ext()

                b, br = bank()
                for oc in range(8):
                    mm(b[:n, 2 * oc:2 * oc + 2], rkr[:, oc, :n], headind, True, True, [Rrkr, Rk2], [br])
                cp("dve", st16[:n, 32:48], b[:n, 0:16], [br], [Rst16])


                if lastt:
                    for half in range(2):
                        b, br = bank()
                        bv = b.rearrange("p (a b) -> p a b", a=4, b=128)
                        for j in range(4):
                            oc = half * 4 + j
                            tr(bv[0:64, j, :], Pst[:, oc, :], ident_f, RP + [Rc], [br])
                        cp("dve", Snat[0:64, half * 8:(half + 1) * 8, :].rearrange("p a b -> p (a b)"),
                           b[0:64, :], [br], RSnat)
                    dma("sp", o_rwkv_state[seq].rearrange("h i j -> i h j"), Snat[0:64, :, :], RSnat, [], "Sst" + seq)

            def tail(ti):
                seq, row0, n, (kind, k) = tiles[ti]
                slot = ti % 2
                x_ = xt[slot]
                Rx = Rxt[slot]
                first = (ti == 0) or (kind == "samp")
                lastt = (ti == len(tiles) - 2) or (kind == "samp")
                prev_n = tiles[ti - 1][2] if ti > 0 else None
                gate, Rgate = gates[ti % 2], Rgates[ti % 2]
                vf, Rvf = vfs[ti % 2], Rvfs[ti % 2]
                XR, XW, XK, XV, XA, XG = 0, 1, 2, 3, 4, 5
                y3 = yv.rearrange("p (a b) -> p a b", a=16, b=64)
                q3 = ysq.rearrange("p (a b) -> p a b", a=16, b=64)
                v3 = vf.rearrange("p (a b) -> p a b", a=16, b=64)
                P.op("dve", lambda e, o_=st16[:n, 0:16], i_=y3[:n]: e.tensor_reduce(o_, i_, mybir.AxisListType.X, ALU.add),
                     [Ry], [Rst16])
                tt("pool", ysq[:n, :], yv[:n, :], yv[:n, :], ALU.mult, [Ry], [Rysq])
                P.op("dve", lambda e, o_=st16[:n, 16:32], i_=q3[:n]: e.tensor_reduce(o_, i_, mybir.AxisListType.X, ALU.add),
                     [Rysq], [Rst16])
                yield
                ts("dve", st16[:n, 0:16], st16[:n, 0:16], 1.0 / 64, ALU.mult, [Rst16], [Rst16])
                tt("dve", st16[:n, 48:64], st16[:n, 0:16], st16[:n, 0:16], ALU.mult, [Rst16], [Rst16])
                stt(st16[:n, 16:32], st16[:n, 16:32], 1.0 / 64, st16[:n, 48:64], ALU.mult, ALU.subtract, [Rst16], [Rst16])
                act(st16[:n, 16:32], st16[:n, 16:32], AF.Ln, [Rst16, Rc], [Rst16], bias=eps_t[:n, 2:3])
                act(st16[:n, 16:32], st16[:n, 16:32], AF.Exp, [Rst16], [Rst16], scale=-0.5)
                yield
                tt("dve", y3[:n], y3[:n], st16[:n, 0:16].unsqueeze(2).to_broadcast([n, 16, 64]), ALU.subtract, [Ry, Rst16], [Ry])
                tt("pool", y3[:n], y3[:n], st16[:n, 16:32].unsqueeze(2).to_broadcast([n, 16, 64]), ALU.mult, [Ry, Rst16], [Ry])
                yield
                tt("dve", yv[:n, :], yv[:n, :], lnw[:n, :], ALU.mult, [Ry, Rln], [Ry])
                tt("pool", yv[:n, :], yv[:n, :], lnb[:n, :], ALU.add, [Ry, Rln], [Ry])
                yield
                tt("dve", q3[:n], v3[:n], st16[:n, 32:48].unsqueeze(2).to_broadcast([n, 16, 64]), ALU.mult, [Rvf, Rst16], [Rysq])
                tt("pool", yv[:n, :], yv[:n, :], ysq[:n, :], ALU.add, [Ry, Rysq], [Ry])
                yield
                tt("dve", yg[:n, :], yv[:n, :], gate[:n, :], ALU.mult, [Ry, Rgate], [Ryg])
                b, br = bank()
                bb = bankb(b).rearrange("p (a b) -> p a b", a=8, b=128)
                for kc in range(8):
                    tr(bb[:, kc, :n], yg[:n, kc * 128:(kc + 1) * 128], ident_b[:n, :n], [Ryg, Rc], [br])
                cp("act", ygT[:, :, :n], bb[:, :, :n], [br], [RygT])
                yield
                for s in range(2):
                    b, br = bank()
                    for kc in range(8):
                        mm(b[:n, :], ygT[:, kc, :n], w_o[:, kc, s * 512:(s + 1) * 512], kc == 0, kc == 7, [RygT, Rwo], [br])
                    tt("dve", x_[:n, s * 512:(s + 1) * 512], b[:n, :], x_[:n, s * 512:(s + 1) * 512], ALU.add, [br, Rx], [Rx])
                    yield
                dma("sp", dst[row0:row0 + n, :], x_[:n, :], [Rx], [], f"xst{slot}")

                yield

            def run_il(gens):
                gens = [g for g in gens if g is not None]
                while gens:
                    for g in list(gens):
                        try:
                            next(g)
                        except StopIteration:
                            gens.remove(g)

            load_x(0)
            if len(tiles) > 1:
                load_x(1)
            run_il([head(0)])
            for ti in range(len(tiles)):
                mid(ti)
                drain_ext()
                tg = tail(ti)
                hg = head(ti + 1) if ti + 1 < len(tiles) else None
                nxt_load = (lambda t2=ti + 2: load_x(t2)) if ti + 2 < len(tiles) else None
                tail_alive = True
                while hg is not None:
                    try:
                        next(hg)
                    except StopIteration:
                        hg = None
                    if tail_alive:
                        try:
                            next(tg)
                        except StopIteration:
                            tail_alive = False
                if tail_alive:
                    ext["g"], ext["done"] = tg, nxt_load
                    if ti + 1 >= len(tiles):
                        drain_ext()
                elif nxt_load is not None:
                    nxt_load()
            drain_ext()
            P.barrier()

        if 1 in phases:
            phase_gdn()
        if 2 in phases:
            phase_ffn(0, hs[0], hs[1], False)
        if 3 in phases:
            phase_rwkv(hs[1], hs[2])
        if 4 in phases:
            phase_ffn(1, hs[2], None, True)

        with nc.Block() as block:
            @block.tensor
            def _(e):
                P.replay("pe", e)

            @block.scalar
            def _(e):
                P.replay("act", e)

            @block.vector
            def _(e):
                P.replay("dve", e)

            @block.gpsimd
            def _(e):
                P.replay("pool", e)

            @block.sync
            def _(e):
                P.replay("sp", e)
    return nc


_NC = None

IN_NAMES_PER_CORE = {
    "x_prompt": lambda a, i: a[i],
    "x_sample": lambda a, i: a[i],
    "cache_gdn_conv": lambda a, i: a[0, i],
    "state_gdn": lambda a, i: a[0, i],
    "cache_rwkv_shift": lambda a, i: a[0, i],
    "state_rwkv": lambda a, i: a[0, i],
}
SQUEEZE0 = ["gdn_w_in", "gdn_conv_w", "gdn_a_log", "gdn_dt_bias", "gdn_o_norm", "gdn_w_out", "rwkv_mu", "rwkv_w0",
            "rwkv_w1", "rwkv_w2", "rwkv_a0", "rwkv_a1", "rwkv_a2", "rwkv_g1", "rwkv_g2", "rwkv_k_k", "rwkv_k_a",
            "rwkv_w_r", "rwkv_w_k", "rwkv_w_v", "rwkv_w_o", "rwkv_ln_w", "rwkv_ln_b"]


def kernel(**inputs):
    global _NC
    if _NC is None:
        _NC = build_program()
    nc = _NC
    f = lambda a: np.ascontiguousarray(np.asarray(a, dtype=np.float32))
    shared = {}
    for k in ["meta_tokens", "norm_mix", "norm_ffn", "norm_final", "ffn_w_gate", "ffn_w_up", "ffn_w_down"]:
        shared[k] = f(inputs[k])
    for k in SQUEEZE0:
        shared[k] = f(np.asarray(inputs[k])[0])
    shared["rwkv_r_k"] = f(np.asarray(inputs["rwkv_r_k"])[0].reshape(D))
    in_maps = []
    for i in range(8):
        m = dict(shared)
        for k, fn in IN_NAMES_PER_CORE.items():
            m[k] = f(fn(np.asarray(inputs[k]), i))
        in_maps.append(m)
    res = run_bass_kernel_spmd(nc, in_maps, core_ids=list(range(8)))
    R = res.results

    def st(name, lead=None):
        a = np.stack([np.asarray(R[i][name], dtype=np.float32) for i in range(8)], axis=0)
        return a if lead is None else a[None]

    return (st("y_prompt"), st("y_sample"),
            st("p_gdn_conv", 1), st("p_gdn_state", 1), st("p_rwkv_shift", 1), st("p_rwkv_state", 1),
            st("s_gdn_conv", 1), st("s_gdn_state", 1), st("s_rwkv_shift", 1), st("s_rwkv_state", 1))
```

```python
import math
from contextlib import ExitStack

import numpy as np
import concourse.bass as bass
import concourse.mybir as mybir
from concourse.bass_utils import run_bass_kernel_spmd

F32 = mybir.dt.float32
BF16 = mybir.dt.bfloat16
AF = mybir.ActivationFunctionType
ALU = mybir.AluOpType

D = 1024
SEQ = 8192
NMETA = 16
TP = SEQ + NMETA
DEC = 64
NROWS = TP + DEC
DFF = 2816
NFC = DFF // 128
GIN = 4112
EPS = 1e-6
GN_EPS = 64e-5
NEG = -1.0e5

ENGS = ("pe", "act", "dve", "pool", "sp")
DBG_A = False
DBG_B = False


class Reg:
    __slots__ = ("name", "w", "r", "psum")

    def __init__(self, name, psum=False):
        self.name = name
        self.w = None
        self.r = []
        self.psum = psum


class Prog:
    def __init__(self, sems):
        self.free_sems = list(sems)
        self.q = {e: [] for e in ENGS}
        self.cnt = {e: 0 for e in ENGS}
        self.esem = {e: self.free_sems.pop() for e in ENGS}
        self.waited = {e: {} for e in ENGS}
        self.dsem = {}

    def _waits(self, eng, deps):
        best = {}
        for (sk, sem, val) in deps:
            if sk == "pe" and eng == "pe":
                continue
            if self.waited[eng].get(sk, 0) >= val:
                continue
            if best.get(sk, (None, 0))[1] < val:
                best[sk] = (sem, val)
        out = []
        for sk, (sem, val) in best.items():
            self.waited[eng][sk] = val
            out.append((sem, val))
        return out

    def op(self, eng, fn, reads=(), writes=(), dma=None, skip_waw=False):
        deps = []
        for b in list(reads) + ([] if skip_waw else list(writes)):
            if b.w is not None:
                deps.append(b.w)
        for b in writes:
            deps.extend(b.r)
        for b in reads:
            if b.psum:
                deps.extend(ev_ for ev_ in b.r if ev_[0] != eng)
        waits = self._waits(eng, deps)
        if dma is None:
            self.cnt[eng] += 1
            ev = (eng, self.esem[eng], self.cnt[eng])
            inc = (self.esem[eng], 1)
        else:
            if dma not in self.dsem:
                self.dsem[dma] = [self.free_sems.pop(), 0]
            ent = self.dsem[dma]
            ent[1] += 16
            ev = ("d:" + dma, ent[0], ent[1])
            inc = (ent[0], 16)
        self.q[eng].append((waits, fn, inc))
        for b in writes:
            b.w = ev
            b.r = []
        for b in reads:
            if b not in writes:
                b.r.append(ev)
        return ev

    def barrier(self):
        evs = [(e, self.esem[e], self.cnt[e]) for e in ENGS if self.cnt[e] > 0]
        evs += [("d:" + k, v[0], v[1]) for k, v in self.dsem.items() if v[1] > 0]
        for e in ENGS:
            ws = []
            for (sk, sem, val) in evs:
                if sk == e:
                    continue
                if self.waited[e].get(sk, 0) >= val:
                    continue
                self.waited[e][sk] = val
                ws.append((sem, val))
            if ws:
                self.q[e].append((ws, None, None))

    def replay(self, eng, e):
        for (waits, fn, inc) in self.q[eng]:
            for (sem, val) in waits:
                e.wait_ge(sem, val)
            if fn is not None:
                ins = fn(e)
                ins.then_inc(inc[0], inc[1])


class Arena:
    def __init__(self, ap, nwords):
        self.ap = ap
        self.n = nwords
        self.off = 0

    def alloc(self, shape, dtype):
        free = 1
        for s in shape[1:]:
            free *= s
        words = free if dtype == F32 else (free + 1) // 2
        words = (words + 7) // 8 * 8
        assert self.off + words <= self.n, f"arena overflow {self.off}+{words}>{self.n}"
        v = self.ap[:, self.off:self.off + words]
        self.off += words
        if dtype != F32:
            v = v.bitcast(dtype)
        v = v[:, 0:free]
        if len(shape) == 3:
            v = v.rearrange("p (a b) -> p a b", a=shape[1], b=shape[2])
        elif len(shape) == 4:
            v = v.rearrange("p (a b c) -> p a b c", a=shape[1], b=shape[2], c=shape[3])
        return v


def build_program(SEQ=SEQ, phases=(1, 2, 3, 4)):
    TP = SEQ + NMETA
    NROWS = TP + DEC
    nc = bass.Bass("TRN2", target_bir_lowering=False)

    def din(name, shape):
        return nc.dram_tensor(name, list(shape), F32, kind="ExternalInput").ap()

    def dout(name, shape):
        return nc.dram_tensor(name, list(shape), F32, kind="ExternalOutput").ap()

    x_prompt = din("x_prompt", [SEQ, D])
    x_sample = din("x_sample", [DEC, D])
    cache_gdn_conv = din("cache_gdn_conv", [3, 3072])
    state_gdn = din("state_gdn", [8, 128, 128])
    cache_rwkv_shift = din("cache_rwkv_shift", [1, D])
    state_rwkv = din("state_rwkv", [16, 64, 64])
    meta_tokens = din("meta_tokens", [NMETA, D])
    norm_mix = din("norm_mix", [2, D])
    norm_ffn = din("norm_ffn", [2, D])
    norm_final = din("norm_final", [D])
    gdn_w_in = din("gdn_w_in", [D, GIN])
    gdn_conv_w = din("gdn_conv_w", [4, 3072])
    gdn_a_log = din("gdn_a_log", [8])
    gdn_dt_bias = din("gdn_dt_bias", [8])
    gdn_o_norm = din("gdn_o_norm", [128])
    gdn_w_out = din("gdn_w_out", [D, D])
    rwkv_mu = din("rwkv_mu", [6, D])
    rwkv_w0 = din("rwkv_w0", [D])
    rwkv_w1 = din("rwkv_w1", [D, 64])
    rwkv_w2 = din("rwkv_w2", [64, D])
    rwkv_a0 = din("rwkv_a0", [D])
    rwkv_a1 = din("rwkv_a1", [D, 64])
    rwkv_a2 = din("rwkv_a2", [64, D])
    rwkv_g1 = din("rwkv_g1", [D, 160])
    rwkv_g2 = din("rwkv_g2", [160, D])
    rwkv_k_k = din("rwkv_k_k", [D])
    rwkv_k_a = din("rwkv_k_a", [D])
    rwkv_r_k = din("rwkv_r_k", [D])
    rwkv_w_r = din("rwkv_w_r", [D, D])
    rwkv_w_k = din("rwkv_w_k", [D, D])
    rwkv_w_v = din("rwkv_w_v", [D, D])
    rwkv_w_o = din("rwkv_w_o", [D, D])
    rwkv_ln_w = din("rwkv_ln_w", [D])
    rwkv_ln_b = din("rwkv_ln_b", [D])
    ffn_w_gate = din("ffn_w_gate", [2, D, DFF])
    ffn_w_up = din("ffn_w_up", [2, D, DFF])
    ffn_w_down = din("ffn_w_down", [2, DFF, D])

    y_prompt = dout("y_prompt", [SEQ, D])
    y_sample = dout("y_sample", [DEC, D])
    o_gdn_conv = {"p": dout("p_gdn_conv", [3, 3072]), "s": dout("s_gdn_conv", [3, 3072])}
    o_gdn_state = {"p": dout("p_gdn_state", [8, 128, 128]), "s": dout("s_gdn_state", [8, 128, 128])}
    o_rwkv_shift = {"p": dout("p_rwkv_shift", [1, D]), "s": dout("s_rwkv_shift", [1, D])}
    o_rwkv_state = {"p": dout("p_rwkv_state", [16, 64, 64]), "s": dout("s_rwkv_state", [16, 64, 64])}

    hs = [nc.dram_tensor(f"hscr{i}", [NROWS, D], F32, kind="Internal").ap() for i in range(3)]

    tiles = [("p", 0, NMETA, ("meta", 0))]
    for k in range(SEQ // 128):
        tiles.append(("p", NMETA + 128 * k, 128, ("x", k)))
    tiles.append(("s", TP, DEC, ("samp", 0)))

    with ExitStack() as es:
        sems = [es.enter_context(nc.semaphore(f"sm{i}")) for i in range(100)]
        P = Prog(sems)
        NW = 53100
        arena_t = es.enter_context(nc.sbuf_tensor("arena", [128, NW], F32))
        A = Arena(arena_t[:], NW)
        psf = [es.enter_context(nc.psum_tensor(f"ps{i}", [128, 512], F32)) for i in range(8)]
        psr = [Reg(f"ps{i}", psum=True) for i in range(8)]
        pcount = [0]

        def bank():
            i = pcount[0] % 8
            pcount[0] += 1
            return psf[i][:], psr[i]

        def bankb(b):
            return b.bitcast(BF16)

        rr = [0]

        def ev2():
            rr[0] += 1
            return "act" if rr[0] % 2 else "dve"

        def mm(out, lhsT, rhs, start, stop, reads, writes):
            P.op("pe", lambda e: e.matmul(out, lhsT, rhs, start=start, stop=stop), reads, writes)

        def tr(out, in_, ident, reads, writes):
            P.op("pe", lambda e: e.transpose(out, in_, ident), reads, writes)

        def act(out, in_, func, reads, writes, bias=None, scale=None, accum_out=None):
            kw = {}
            if bias is not None:
                kw["bias"] = bias
            if scale is not None:
                kw["scale"] = scale
            if accum_out is not None:
                kw["accum_out"] = accum_out
            P.op("act", lambda e: e.activation(out, in_, func, **kw), reads, writes)

        def tt(eng, out, in0, in1, op, reads, writes):
            P.op(eng, lambda e: e.tensor_tensor(out, in0, in1, op), reads, writes)

        def ts(eng, out, in0, s1, op0, reads, writes, s2=None, op1=None):
            if op1 is None:
                P.op(eng, lambda e: e.tensor_scalar(out, in0, s1, None, op0), reads, writes)
            else:
                P.op(eng, lambda e: e.tensor_scalar(out, in0, s1, s2, op0, op1), reads, writes)

        def stt(out, in0, scalar, in1, op0, op1, reads, writes):
            P.op("dve", lambda e: e.scalar_tensor_tensor(out, in0, scalar, in1, op0, op1), reads, writes)

        def cp(eng, out, in_, reads, writes):
            if eng == "act":
                P.op("act", lambda e: e.copy(out, in_), reads, writes)
            else:
                P.op(eng, lambda e: e.tensor_copy(out, in_), reads, writes)

        def recip(out, in_, reads, writes):
            P.op("dve", lambda e: e.reciprocal(out, in_), reads, writes)

        def memset(eng, ap, val, writes):
            P.op(eng, lambda e: e.memset(ap, val), (), writes)

        def dma(eng, out, in_, reads, writes, key, skip_waw=False, **kw):
            P.op(eng, lambda e: e.dma_start(out=out, in_=in_, **kw), reads, writes, dma=key, skip_waw=skip_waw)

        Rc = Reg("consts")
        ident_f = A.alloc([128, 128], F32)
        ident_b = A.alloc([128, 128], BF16)
        ones_f = A.alloc([128, 128], F32)
        ones_b = A.alloc([128, 128], BF16)
        zeros_f = A.alloc([128, 128], F32)
        imask_u = A.alloc([128, 128], F32)
        smask_u = A.alloc([128, 128], F32)
        smask_l = A.alloc([128, 128], F32)
        bdmask = A.alloc([128, 128], F32)
        offmask = A.alloc([128, 128], F32)
        negmask = A.alloc([128, 128], F32)
        eps_t = A.alloc([128, 4], F32)
        memset("pool", ones_f, 1.0, [Rc])
        memset("pool", zeros_f, 0.0, [Rc])
        memset("pool", eps_t[:, 0:1], EPS, [Rc])
        memset("pool", eps_t[:, 1:2], 1.0, [Rc])
        memset("pool", eps_t[:, 2:3], GN_EPS, [Rc])
        memset("pool", eps_t[:, 3:4], 0.0, [Rc])

        def asel(out, in_, cmp_op, fill, base, cm, step):
            P.op("pool", lambda e: e.affine_select(out, in_, [[step, 128]], cmp_op, fill,
                                                   base=base, channel_multiplier=cm), [Rc], [Rc])

        asel(imask_u, ones_f, ALU.is_ge, 0.0, 0, -1, 1)
        asel(smask_u, ones_f, ALU.is_gt, 0.0, 0, -1, 1)
        asel(smask_l, ones_f, ALU.is_gt, 0.0, 0, 1, -1)
        asel(ident_f, ones_f, ALU.is_equal, 0.0, 0, -1, 1)
        asel(negmask, zeros_f, ALU.is_ge, NEG, 0, -1, 1)
        cp("pool", bdmask, smask_u, [Rc], [Rc])
        memset("pool", bdmask[0:64, 64:128], 0.0, [Rc])
        memset("pool", offmask, 0.0, [Rc])
        memset("pool", offmask[0:64, 64:128], 1.0, [Rc])
        cp("pool", ident_b, ident_f, [Rc], [Rc])
        cp("pool", ones_b, ones_f, [Rc], [Rc])
        persist_off = A.off

        def load_w(dst, src, K, key, reg):
            for kc in range(K):
                dma("pool", dst[:, kc, :], src[kc * 128:(kc + 1) * 128, :], [], [reg], key,
                    skip_waw=(kc > 0), max_dma_last_dim=4096)

        def load_featmajor(dst, src_vec, C, stage, stage_reg, dst_reg, key):
            dma("sp", stage[0:C, 0:128], src_vec.rearrange("(c p) -> c p", p=128), [], [stage_reg], key)
            b, br = bank()
            tr(b[:, 0:C], stage[0:C, 0:128], ident_f[0:C, 0:C], [stage_reg, Rc], [br])
            cp("dve", dst, b[:, 0:C], [br], [dst_reg])

        def rmsnorm_T(xt, Rx, n, gfm, Rg, hn, Rhn, hnT, RhnT, small, Rsm, junk, Rjunk):
            act(junk[:n, :], xt[:n, :], AF.Square, [Rx], [Rjunk, Rsm], accum_out=small[:n, 0:1])
            act(small[:n, 1:2], small[:n, 0:1], AF.Ln, [Rsm, Rc], [Rsm], bias=eps_t[:n, 0:1], scale=1.0 / D)
            act(small[:n, 2:3], small[:n, 1:2], AF.Exp, [Rsm], [Rsm], scale=-0.5)
            ts("dve", hn[:n, :], xt[:n, :], small[:n, 2:3], ALU.mult, [Rx, Rsm], [Rhn])
            b, br = bank()
            bb = bankb(b).rearrange("p (a b) -> p a b", a=8, b=128)
            for kc in range(8):
                tr(bb[:, kc, :n], hn[:n, kc * 128:(kc + 1) * 128], ident_b[:n, :n], [Rhn, Rc], [br])
            for kc in range(8):
                eng = ev2()
                if eng == "act":
                    act(hnT[:, kc, :n], bb[:, kc, :n], AF.Copy, [br, Rg], [RhnT], scale=gfm[:, kc:kc + 1])
                else:
                    ts("dve", hnT[:, kc, :n], bb[:, kc, :n], gfm[:, kc:kc + 1], ALU.mult, [br, Rg], [RhnT])

        def neumann_gen(Bsrc, BTsrc, RB, RBT, n, nf, PR, RPR, PT, RPT, G, res, psum_acc=False):
            cur = 0
            tt("pool", PR[0][:n, :, 1, :n], Bsrc, ident_f[:n, :n].unsqueeze(1).to_broadcast([n, G, n]), ALU.add,
               [RB, Rc], [RPR[0]])
            Pk = Bsrc
            PTk = BTsrc
            RPk, RPTk = RB, RBT
            Rk = PR[0][:n, :, 1, :n]
            RRk = RPR[0]
            for k in range(0, nf):
                last = (k == nf - 1)
                if k == 0:
                    if nf == 1:
                        break
                    nxt = 1 - cur
                    b1, b1r = bank()
                    b2, b2r = bank()
                    v1 = b1.rearrange("p (g c) -> p g c", g=4, c=128)
                    v2 = b2.rearrange("p (g c) -> p g c", g=4, c=128)
                    for hh in range(G):
                        mm(v1[:n, hh, :n], PTk[:, hh, :], Pk[:, hh, :], True, True, [RPk, RPTk], [b1r])
                        mm(v2[:n, hh, :n], Pk[:, hh, :], PTk[:, hh, :], True, True, [RPk, RPTk], [b2r])
                    cp("act", PR[nxt][:n, :, 0, :n], v1[:n, 0:G, :n], [b1r], [RPR[nxt]])
                    cp("dve", PT[nxt][:n, :, :n], v2[:n, 0:G, :n], [b2r], [RPT[nxt]])
                    cp("pool", PR[nxt][:n, :, 1, :n], Rk, [RRk], [RPR[nxt]])
                    cur = nxt
                    Pk = PR[cur][:n, :, 0, :n]
                    PTk = PT[cur][:n, :, :n]
                    RPk, RPTk = RPR[cur], RPT[cur]
                    Rk = PR[cur][:n, :, 1, :n]
                    RRk = RPR[cur]
                    yield
                    continue
                nxt = 1 - cur
                if not last:
                    ba, bar = bank()
                    bb_, bbr = bank()
                    bc, bcr = bank()
                    va = ba.rearrange("p (g t c) -> p g t c", g=2, t=2, c=128)
                    vb = bb_.rearrange("p (g t c) -> p g t c", g=2, t=2, c=128)
                    vc = bc.rearrange("p (g c) -> p g c", g=4, c=128)
                    for hh in range(G):
                        tgt, tgr = (va, bar) if hh < 2 else (vb, bbr)
                        if n == 128:
                            mm(tgt[:n, hh % 2, :, :n], PTk[:, hh, :], PR[cur][:n, hh, :, :n], True, not psum_acc,
                               [RPk, RPTk, RRk], [tgr])
                        else:
                            for t_ in range(2):
                                mm(tgt[:n, hh % 2, t_, :n], PTk[:, hh, :], PR[cur][:n, hh, t_, :n], True,
                                   not (psum_acc and t_ == 1), [RPk, RPTk, RRk], [tgr])
                        if psum_acc:
                            mm(tgt[:n, hh % 2, 1, :n], ident_b[:n, :n], PR[cur][:n, hh, 1, :n], False, True,
                               [Rc, RRk], [tgr])
                        mm(vc[:n, hh, :n], Pk[:, hh, :], PTk[:, hh, :], True, True, [RPk, RPTk], [bcr])
                    for (vv, vr, h0) in ((va, bar, 0), (vb, bbr, 2)):
                        gcount = min(2, G - h0)
                        if gcount <= 0:
                            continue
                        cp("act", PR[nxt][:n, h0:h0 + gcount, 0, :n], vv[:n, 0:gcount, 0, :n], [vr], [RPR[nxt]])
                        if psum_acc:
                            cp("act", PR[nxt][:n, h0:h0 + gcount, 1, :n], vv[:n, 0:gcount, 1, :n], [vr], [RPR[nxt]])
                        else:
                            tt("dve", PR[nxt][:n, h0:h0 + gcount, 1, :n], vv[:n, 0:gcount, 1, :n],
                               PR[cur][:n, h0:h0 + gcount, 1, :n], ALU.add, [vr, RPR[cur]], [RPR[nxt]])
                    cp("act", PT[nxt][:n, :, :n], vc[:n, 0:G, :n], [bcr], [RPT[nxt]])
                else:
                    ba, bar = bank()
                    va = ba.rearrange("p (g c) -> p g c", g=4, c=128)
                    for hh in range(G):
                        mm(va[:n, hh, :n], PTk[:, hh, :], PR[cur][:n, hh, 1, :n], True, not psum_acc, [RPTk, RRk], [bar])
                        if psum_acc:
                            mm(va[:n, hh, :n], ident_b[:n, :n], PR[cur][:n, hh, 1, :n], False, True, [Rc, RRk], [bar])
                    if psum_acc:
                        cp("act", PR[nxt][:n, :, 1, :n], va[:n, 0:G, :n], [bar], [RPR[nxt]])
                    else:
                        tt("dve", PR[nxt][:n, :, 1, :n], va[:n, 0:G, :n], PR[cur][:n, :, 1, :n], ALU.add,
                           [bar, RPR[cur]], [RPR[nxt]])
                cur = nxt
                Pk = PR[cur][:n, :, 0, :n]
                PTk = PT[cur][:n, :, :n]
                RPk, RPTk = RPR[cur], RPT[cur]
                Rk = PR[cur][:n, :, 1, :n]
                RRk = RPR[cur]
                yield
            res.append((Rk, RRk))

        def neumann(Bsrc, BTsrc, RB, RBT, n, nf, PR, RPR, PT, RPT, G):
            res = []
            for _ in neumann_gen(Bsrc, BTsrc, RB, RBT, n, nf, PR, RPR, PT, RPT, G, res):
                pass
            return res[0]

        def phase_gdn():
            A.off = persist_off
            Rw = Reg("w_in")
            Rwo = Reg("w_out")
            w_in = A.alloc([128, 8, GIN], BF16)
            w_out = A.alloc([128, 8, D], BF16)
            diag = A.alloc([128, 96, 128], BF16)
            Rdiag = Reg("diag")
            load_w(w_in, gdn_w_in, 8, "w_in", Rw)
            load_w(w_out, gdn_w_out, 8, "w_out", Rwo)

            stage = A.alloc([128, 128], F32)
            Rst = Reg("stage")
            gfm = A.alloc([128, 8], F32)
            Rg = Reg("gfm")
            load_featmajor(gfm, norm_mix[0], 8, stage, Rst, Rg, "st")
            cw = A.alloc([128, 96], F32)
            Rcw = Reg("cw")
            dma("sp", stage[0:96, :], gdn_conv_w.rearrange("t (c p) -> (t c) p", p=128), [], [Rst], "st")
            b, br = bank()
            tr(b[:, 0:96], stage[0:96, :], ident_f[0:96, 0:96], [Rst, Rc], [br])
            cp("dve", cw, b[:, 0:96], [br], [Rcw])
            for idx in range(96):
                ts("pool" if idx % 2 else "dve", diag[:, idx, :], ident_f, cw[:, idx:idx + 1],
                   ALU.mult, [Rc, Rcw], [Rdiag])
            prm = A.alloc([128, 32], F32)
            Rprm = Reg("prm")
            dma("sp", prm[:, 0:8], gdn_a_log.partition_broadcast(128), [], [Rprm], "prm")
            dma("sp", prm[:, 8:16], gdn_dt_bias.partition_broadcast(128), [], [Rprm], "prm2")
            act(prm[:, 0:8], prm[:, 0:8], AF.Exp, [Rprm], [Rprm])
            ts("dve", prm[:, 0:8], prm[:, 0:8], -1.0, ALU.mult, [Rprm], [Rprm])
            onb = A.alloc([128, 128], F32)
            Ronb = Reg("onb")
            dma("sp", onb, gdn_o_norm.partition_broadcast(128), [], [Ronb], "onb")

            xt = [A.alloc([128, D], F32) for _ in range(2)]
            Rxt = [Reg("xt0"), Reg("xt1")]
            junk = A.alloc([128, D], BF16)
            Rjunk = Reg("junk")
            small = A.alloc([128, 8], F32)
            Rsm = Reg("small")
            hn = A.alloc([128, D], BF16)
            Rhn = Reg("hn")
            hnT = A.alloc([128, 8, 128], BF16)
            RhnT = Reg("hnT")
            yg = A.alloc([128, D], BF16)
            Ryg = Reg("yg")
            ygT = A.alloc([128, 8, 128], BF16)
            RygT = Reg("ygT")
            pre = A.alloc([128, 24, 132], BF16)
            Rpre = [Reg(f"pre{i}") for i in range(6)]
            nbT = A.alloc([128, 3, 24], F32)
            RnbT = Reg("nbT")
            tmpc = [A.alloc([128, 4, 128], F32) for _ in range(2)]
            Rtmpc = [Reg("tmpc0"), Reg("tmpc1")]
            sq = [A.alloc([128, 4, 128], BF16)]
            Rsq = [Reg("sq0")]
            tmp2 = [A.alloc([128, 4, 128], F32) for _ in range(2)]
            Rtmp2 = [Reg("tmp20"), Reg("tmp21")]
            qkn = A.alloc([128, 16, 128], BF16)
            Rqkn = [Reg(f"qkn{i}") for i in range(4)]
            vT = A.alloc([128, 8, 128], BF16)
            RvT = [Reg("vT0"), Reg("vT1")]
            kG = A.alloc([128, 8, 128], BF16)
            kd = A.alloc([128, 8, 128], BF16)
            vtok = A.alloc([128, 8, 128], BF16)
            Rktok = [Reg("ktok0"), Reg("ktok1")]
            zss = [A.alloc([128, D], BF16) for _ in range(2)]
            Rzss = [Reg("zs0"), Reg("zs1")]
            gs = A.alloc([128, 96], F32)
            Rgs = Reg("gs")
            Rgs2 = Reg("gs2")
            ET = A.alloc([128, 8, 128], F32)
            RET = [Reg(f"ET{i}") for i in range(4)]
            qgT = A.alloc([128, 8, 128], BF16)
            RqgT = [Reg(f"qgT{i}") for i in range(4)]
            qkT = A.alloc([128, 8, 128], BF16)
            RqkT = [Reg(f"qkT{i}") for i in range(4)]
            NSETS = 2
            NACT = NSETS
            SETS = []
            for si in range(NSETS):
                PRs = [A.alloc([128, 2, 2, 128], F32) for _ in range(2)]
                PTs = [A.alloc([128, 2, 128], F32) for _ in range(2)]
                SETS.append(dict(PR=PRs, RPR=[Reg(f"gPR{si}a"), Reg(f"gPR{si}b")], PT=PTs,
                                 RPT=[Reg(f"gPT{si}a"), Reg(f"gPT{si}b")],
                                 B0=A.alloc([128, 2, 128], F32), RB0=Reg(f"gB0{si}"),
                                 B0T=A.alloc([128, 2, 128], F32), RB0T=Reg(f"gB0T{si}"),
                                 tmpA=A.alloc([128, 2, 128], F32), RtmpA=Reg(f"gtmpA{si}"),
                                 OffT=A.alloc([128, 2, 128], F32), ROff=Reg(f"gOff{si}"),
                                 tmpE=A.alloc([128, 2, 128], F32), RtmpE=Reg(f"gtmpE{si}")))
            MinvT = A.alloc([128, 8, 128], BF16)
            RMinv = [Reg(f"Minv{i}") for i in range(4)]
            nsolw = A.alloc([128, 8, 128], BF16)
            Rnsolw = [Reg(f"nsolw{i}") for i in range(4)]
            vnew = A.alloc([128, 8, 128], BF16)
            Rvnew = [Reg(f"vnew{i}") for i in range(4)]
            S = A.alloc([128, 8, 128], F32)
            RS = [Reg(f"S{i}") for i in range(4)]
            Sb = A.alloc([128, 8, 128], BF16)
            RSb = [Reg(f"Sb{i}") for i in range(4)]
            o = A.alloc([128, D], F32)
            Ro = [Reg(f"o{i}") for i in range(4)]

            def load_x(ti):
                seq, row0, n, (kind, k) = tiles[ti]
                slot = ti % 2
                if kind == "meta":
                    src = meta_tokens
                elif kind == "x":
                    src = x_prompt[k * 128:(k + 1) * 128, :]
                else:
                    src = x_sample
                dma("sp", xt[slot][:n, :], src, [], [Rxt[slot]], f"xt{slot}")

            ext = {"g": None, "done": None}

            def step_ext():
                if ext["g"] is not None:
                    try:
                        next(ext["g"])
                    except StopIteration:
                        ext["g"] = None
                        if ext["done"] is not None:
                            ext["done"]()
                            ext["done"] = None

            def drain_ext():
                while ext["g"] is not None:
                    step_ext()

            def head(ti):
                seq, row0, n, (kind, k) = tiles[ti]
                slot = ti % 2
                x_ = xt[slot]
                Rx = Rxt[slot]
                first = (ti == 0) or (kind == "samp")
                lastt = (ti == len(tiles) - 2) or (kind == "samp")
                prev_n = tiles[ti - 1][2] if ti > 0 else None
                zs, Rzs = zss[ti % 2], Rzss[ti % 2]
                if first and seq == "p":
                    memset("pool", S, 0.0, RS)
                    memset("pool", Sb, 0.0, RSb)
                    memset("pool", pre[:, :, 0:3], 0.0, Rpre)
                elif first:
                    dma("sp", S, state_gdn.rearrange("h d e -> d h e"), [], RS, "Sld")
                    cp("pool", Sb, S, RS, RSb)
                    if DBG_B:
                        memset("pool", pre[:, :, 0:3], 0.0, Rpre)
                    else:
                        dma("sp", stage[0:72, :], cache_gdn_conv.rearrange("r (c p) -> (r c) p", p=128), [], [Rst], "st")
                        b, br = bank()
                        mm(b[:, 0:72], stage[0:72, :], ident_f[0:72, 0:72], True, True, [Rst, Rc], [br])
                        cp("dve", pre[:, :, 0:3], b[:, 0:72].rearrange("p (r c) -> p c r", r=3, c=24), [br], Rpre)
                else:
                    cp("pool", pre[:, :, 0:3], pre[:, :, prev_n:prev_n + 3], Rpre, Rpre)

                yield
                rmsnorm_T(x_, Rx, n, gfm, Rg, hn, Rhn, hnT, RhnT, small, Rsm, junk, Rjunk)

                yield
                for g4 in range(6):
                    b, br = bank()
                    bv = b.rearrange("p (a b) -> p a b", a=4, b=128)
                    for j in range(4):
                        c = g4 * 4 + j
                        for kc in range(8):
                            mm(bv[:, j, :n], w_in[:, kc, c * 128:(c + 1) * 128], hnT[:, kc, :n], kc == 0, kc == 7,
                               [Rw, RhnT], [br])
                    cp(ev2(), pre[:, g4 * 4:(g4 + 1) * 4, 3:3 + n], bv[:, :, :n], [br], [Rpre[g4]])
                    if lastt and not DBG_A:
                        cp("dve", nbT.rearrange("p r c -> p c r")[:, g4 * 4:(g4 + 1) * 4, :], bv[:, :, n - 3:n], [br], [RnbT])
                    yield
                if lastt and not DBG_A:
                    b, br = bank()
                    mm(b[0:72, 0:128], nbT.rearrange("p r c -> p (r c)"), ident_f, True, True, [RnbT, Rc], [br])
                    cp("dve", stage[0:72, :], b[0:72, 0:128], [br], [Rst])
                    dma("sp", o_gdn_conv[seq].rearrange("r (c p) -> (r c) p", p=128), stage[0:72, :], [Rst], [], "nb3" + seq)
                for s in range(2):
                    b, br = bank()
                    for kc in range(8):
                        mm(b[:n, :], hnT[:, kc, :n], w_in[:, kc, 3072 + s * 512:3072 + (s + 1) * 512], kc == 0, kc == 7,
                           [Rw, RhnT], [br])
                    act(zs[:n, s * 512:(s + 1) * 512], b[:n, :], AF.Silu, [br], [Rzs])
                    yield
                b, br = bank()
                for kc in range(8):
                    mm(b[:n, 0:16], hnT[:, kc, :n], w_in[:, kc, 4096:4112], kc == 0, kc == 7, [Rw, RhnT], [br])
                act(gs[:n, 0:8], b[:n, 8:16], AF.Sigmoid, [br], [Rgs])
                ts("dve", gs[:n, 8:16], gs[:n, 0:8], -1.0, ALU.mult, [Rgs], [Rgs])
                tt("dve", gs[:n, 16:24], b[:n, 0:8], prm[:n, 8:16], ALU.add, [br, Rprm], [Rgs])
                yield
                stt(gs[:n, 24:32], gs[:n, 16:24], -1.0, gs[:n, 16:24], ALU.mult, ALU.max, [Rgs], [Rgs])
                act(gs[:n, 24:32], gs[:n, 24:32], AF.Exp, [Rgs], [Rgs], scale=-1.0)
                act(gs[:n, 24:32], gs[:n, 24:32], AF.Ln, [Rgs, Rc], [Rgs], bias=eps_t[:n, 1:2])
                stt(gs[:n, 16:24], gs[:n, 16:24], 0.0, gs[:n, 24:32], ALU.max, ALU.add, [Rgs], [Rgs])
                tt("dve", gs[:n, 16:24], gs[:n, 16:24], prm[:n, 0:8], ALU.mult, [Rgs, Rprm], [Rgs])

                yield
                b, br = bank()
                mm(b[:n, 0:8], imask_u[:n, :n], gs[:n, 16:24], True, True, [Rc, Rgs], [br])
                mm(b[:, 8:16], ones_f[:n, :], gs[:n, 16:24], True, True, [Rc, Rgs], [br])
                cp("dve", gs[:n, 32:40], b[:n, 0:8], [br], [Rgs])
                act(gs[:n, 40:48], b[:n, 0:8], AF.Exp, [br], [Rgs])
                cp("dve", gs[:, 48:56], b[:, 8:16], [br], [Rgs])
                act(gs[:, 56:64], b[:, 8:16], AF.Exp, [br], [Rgs])
                tt("dve", gs[:n, 64:72], gs[:n, 48:56], gs[:n, 32:40], ALU.subtract, [Rgs], [Rgs])
                act(gs[:n, 64:72], gs[:n, 64:72], AF.Exp, [Rgs], [Rgs])


                yield

            def mid(ti):
                seq, row0, n, (kind, k) = tiles[ti]
                slot = ti % 2
                x_ = xt[slot]
                Rx = Rxt[slot]
                first = (ti == 0) or (kind == "samp")
                lastt = (ti == len(tiles) - 2) or (kind == "samp")
                prev_n = tiles[ti - 1][2] if ti > 0 else None
                zs, Rzs = zss[ti % 2], Rzss[ti % 2]
                nf = int(math.log2(min(n, 64)))

                def cd_gen():
                    for half in range(2):
                        bvs = []
                        for g4 in (half, 2 + half, 4 + half):
                            b, br = bank()
                            bv = b.rearrange("p (a b) -> p a b", a=4, b=128)
                            for j in range(4):
                                c = g4 * 4 + j
                                for tap in range(4):
                                    mm(bv[:, j, :n], diag[:, tap * 24 + c, :], pre[:, c, tap:tap + n], tap == 0, tap == 3,
                                       [Rdiag, Rpre[g4]], [br])
                            bvs.append((bv, br))
                        act(tmpc[0][:, :, :n], bvs[0][0][:, :, :n], AF.Silu, [bvs[0][1]], [Rtmpc[0]])
                        act(tmpc[1][:, :, :n], bvs[1][0][:, :, :n], AF.Silu, [bvs[1][1]], [Rtmpc[1]])
                        act(vT[:, half * 4:(half + 1) * 4, :n], bvs[2][0][:, :, :n], AF.Silu, [bvs[2][1]], [RvT[half]])
                        yield None
                        for qk in range(2):
                            g4 = half + 2 * qk
                            tt("pool", sq[0][:, :, :n], tmpc[qk][:, :, :n], tmpc[qk][:, :, :n], ALU.mult,
                               [Rtmpc[qk]], [Rsq[0]])
                            b2, b2r = bank()
                            b2v = b2.rearrange("p (a b) -> p a b", a=4, b=128)
                            if n == 128:
                                mm(b2, ones_b, sq[0].rearrange("p a b -> p (a b)"), True, True, [Rc, Rsq[0]], [b2r])
                            else:
                                for j in range(4):
                                    mm(b2v[:, j, :n], ones_b, sq[0][:, j, :n], True, True, [Rc, Rsq[0]], [b2r])
                            act(tmp2[qk][:, :, :n], b2v[:, :, :n], AF.Ln, [b2r, Rc], [Rtmp2[qk]], bias=eps_t[:, 0:1])
                            act(tmp2[qk][:, :, :n], tmp2[qk][:, :, :n], AF.Exp, [Rtmp2[qk]], [Rtmp2[qk]], scale=-0.5)
                            if qk == 0:
                                stt(qkn[:, g4 * 4:(g4 + 1) * 4, :n], tmpc[0][:, :, :n], 128.0 ** -0.5, tmp2[0][:, :, :n],
                                    ALU.mult, ALU.mult, [Rtmpc[0], Rtmp2[0]], [Rqkn[g4]])
                            else:
                                tt("pool", qkn[:, g4 * 4:(g4 + 1) * 4, :n], tmpc[1][:, :, :n], tmp2[1][:, :, :n], ALU.mult,
                                   [Rtmpc[1], Rtmp2[1]], [Rqkn[g4]])
                            yield None
                        hs_ = slice(half * 4, half * 4 + 4)
                        b, br = bank()
                        bb = bankb(b).rearrange("p (a b) -> p a b", a=8, b=128)
                        for hh in range(4):
                            tr(bb[:n, hh, :], qkn[:, 8 + half * 4 + hh, :n], ident_b, [Rqkn[2 + half], Rc], [br])
                        bV2, bV2r = bank()
                        bbv = bankb(bV2).rearrange("p (a b) -> p a b", a=8, b=128)
                        for hh in range(4):
                            tr(bbv[:n, hh, :], vT[:, half * 4 + hh, :n], ident_b, [RvT[half], Rc], [bV2r])
                        tt("dve", kG[:n, hs_, :], bb[:n, 0:4, :], gs[:n, 40 + half * 4:44 + half * 4].unsqueeze(2).to_broadcast([n, 4, 128]),
                           ALU.mult, [br, Rgs], [Rktok[half]])
                        tt("dve", kd[:n, hs_, :], bb[:n, 0:4, :], gs[:n, 64 + half * 4:68 + half * 4].unsqueeze(2).to_broadcast([n, 4, 128]),
                           ALU.mult, [br, Rgs], [Rktok[half]])
                        cp("act", vtok[:n, hs_, :], bbv[:n, 0:4, :], [bV2r], [Rktok[half]])
                        yield half

                def f_gen(p, S_):
                    h0 = p * 2
                    half = p // 2
                    PR, RPR, PT, RPT = S_["PR"], S_["RPR"], S_["PT"], S_["RPT"]
                    B0, RB0, B0T, RB0T = S_["B0"], S_["RB0"], S_["B0T"], S_["RB0T"]
                    tmpA, RtmpA, OffT, ROff, tmpE, RtmpE = S_["tmpA"], S_["RtmpA"], S_["OffT"], S_["ROff"], S_["tmpE"], S_["RtmpE"]
                    Rq, Rk_ = Rqkn[half], Rqkn[2 + half]
                    bG, bGr = bank()
                    vG = bG.rearrange("p (g c) -> p g c", g=4, c=128)
                    for hh in range(2):
                        h = h0 + hh
                        mm(vG[:, hh, :n], gs[:n, 16 + h:17 + h].to_broadcast([n, 128]), imask_u[:n, :n], True, True,
                           [Rgs, Rc], [bGr])
                    for hh in range(2):
                        h = h0 + hh
                        stt(ET[:n, h, :n], vG[:n, hh, :n], gs[:n, 32 + h:33 + h], negmask[:n, :n], ALU.subtract, ALU.add,
                            [bGr, Rgs, Rc], [RET[p]])
                    act(ET[:n, h0:h0 + 2, :n], ET[:n, h0:h0 + 2, :n], AF.Exp, [RET[p]], [RET[p]])
                    act(tmpE[:, :, :n], vG[:, 0:2, :n], AF.Exp, [bGr], [RtmpE])
                    tt("pool", qgT[:, h0:h0 + 2, :n], qkn[:, h0:h0 + 2, :n], tmpE[:, :, :n], ALU.mult, [Rq, RtmpE], [RqgT[p]])
                    yield
                    bK, bKr = bank()
                    vK = bK.rearrange("p (g c) -> p g c", g=4, c=128)
                    for hh in range(2):
                        h = h0 + hh
                        mm(vK[:n, hh, :n], qkn[:, 8 + h, :n], qkn[:, 8 + h, :n], True, True, [Rk_], [bKr])
                        mm(vK[:n, 2 + hh, :n], qkn[:, 8 + h, :n], qkn[:, h, :n], True, True, [Rk_, Rq], [bKr])
                    for hh in range(2):
                        h = h0 + hh
                        stt(tmpA[:n, hh, :n], vK[:n, hh, :n], gs[:n, 8 + h:9 + h], ET[:n, h, :n], ALU.mult, ALU.mult,
                            [bKr, Rgs, RET[p]], [RtmpA])
                    tt("dve", qkT[:n, h0:h0 + 2, :n], vK[:n, 2:4, :n], ET[:n, h0:h0 + 2, :n], ALU.mult, [bKr, RET[p]], [RqkT[p]])
                    tt("pool", B0[:n, :, :n], tmpA[:n, :, :n], bdmask[:n, :n].unsqueeze(1).to_broadcast([n, 2, n]), ALU.mult,
                       [RtmpA, Rc], [RB0])
                    if n == 128:
                        tt("pool", OffT[:n, :, :n], tmpA[:n, :, :n], offmask[:n, :n].unsqueeze(1).to_broadcast([n, 2, n]), ALU.mult,
                           [RtmpA, Rc], [ROff])
                    yield
                    bT, bTr = bank()
                    vT_ = bT.rearrange("p (g c) -> p g c", g=4, c=128)
                    for hh in range(2):
                        mm(vT_[:n, hh, :n], B0[:n, hh, :n], ident_f[:n, :n], True, True, [RB0, Rc], [bTr])
                    cp("act", B0T[:n, :, :n], vT_[:n, 0:2, :n], [bTr], [RB0T])
                    yield
                    res = []
                    for _ in neumann_gen(B0[:n, :, :n], B0T[:n, :, :n], RB0, RB0T, n, nf, PR, RPR, PT, RPT, 2, res):
                        yield
                    Rfin, RRfin = res[0]
                    if n == 128:
                        Dm, RDm = PT[0], RPT[0]
                        T1, RT1 = PT[1], RPT[1]
                        T1T, RT1T = tmpA, RtmpA
                        b1, b1r = bank()
                        v1 = b1.rearrange("p (g c) -> p g c", g=4, c=128)
                        for hh in range(2):
                            mm(v1[:, hh, :], Rfin[:, hh, :], ident_f, True, True, [RRfin, Rc], [b1r])
                        cp("act", Dm, v1[:, 0:2, :], [b1r], [RDm])
                        yield
                        b2, b2r = bank()
                        v2 = b2.rearrange("p (g c) -> p g c", g=4, c=128)
                        for hh in range(2):
                            mm(v2[:, hh, :], Dm[:, hh, :], OffT[:, hh, :], True, True, [RDm, ROff], [b2r])
                        cp("dve", T1, v2[:, 0:2, :], [b2r], [RT1])
                        yield
                        b3, b3r = bank()
                        v3 = b3.rearrange("p (g c) -> p g c", g=4, c=128)
                        for hh in range(2):
                            mm(v3[:, hh, :], T1[:, hh, :], ident_f, True, True, [RT1, Rc], [b3r])
                        cp("act", T1T, v3[:, 0:2, :], [b3r], [RT1T])
                        yield
                        b4, b4r = bank()
                        v4 = b4.rearrange("p (g c) -> p g c", g=4, c=128)
                        for hh in range(2):
                            mm(v4[:, hh, :], T1T[:, hh, :], Rfin[:, hh, :], True, True, [RT1T, RRfin], [b4r])
                        tt("dve", MinvT[:, h0:h0 + 2, :], v4[:, 0:2, :], Rfin, ALU.add, [b4r, RRfin], [RMinv[p]])
                    else:
                        cp("dve", MinvT[:n, h0:h0 + 2, :n], Rfin, [RRfin], [RMinv[p]])
                    yield
                    bW, bWr = bank()
                    vW = bW.rearrange("p (g c) -> p g c", g=4, c=128)
                    for hh in range(2):
                        h = h0 + hh
                        mm(vW[:, hh, :n], kG[:n, h, :], MinvT[:n, h, :n], True, True, [Rktok[half], RMinv[p]], [bWr])
                    ts("dve", nsolw[:, h0:h0 + 2, :n], vW[:, 0:2, :n], -1.0, ALU.mult, [bWr], [Rnsolw[p]])
                    yield
                    bV, bVr = bank()
                    vV = bV.rearrange("p (g c) -> p g c", g=4, c=128)
                    for hh in range(2):
                        h = h0 + hh
                        mm(vV[:n, hh, :], MinvT[:n, h, :n], vtok[:n, h, :], True, False, [RMinv[p], Rktok[half]], [bVr])
                        mm(vV[:n, hh, :], nsolw[:, h, :n], Sb[:, h, :], False, True, [Rnsolw[p], RSb[p]], [bVr])
                    tt("dve", vnew[:n, h0:h0 + 2, :], vV[:n, 0:2, :],
                       gs[:n, h0:h0 + 2].unsqueeze(2).to_broadcast([n, 2, 128]), ALU.mult, [bVr, Rgs], [Rvnew[p]])
                    yield
                    drain_ext()
                    bO, bOr = bank()
                    vO = bO.rearrange("p (g c) -> p g c", g=4, c=128)
                    for hh in range(2):
                        h = h0 + hh
                        mm(vO[:n, hh, :], qgT[:, h, :n], Sb[:, h, :], True, False, [RqgT[p], RSb[p]], [bOr])
                        mm(vO[:n, hh, :], qkT[:n, h, :n], vnew[:n, h, :], False, True, [RqkT[p], Rvnew[p]], [bOr])
                    cp("act", o[:n, h0 * 128:(h0 + 2) * 128], bO[:n, 0:256], [bOr], [Ro[p]])
                    bS, bSr = bank()
                    vS = bS.rearrange("p (g c) -> p g c", g=4, c=128)
                    for hh in range(2):
                        h = h0 + hh
                        mm(vS[:, hh, :], kd[:n, h, :], vnew[:n, h, :], True, True, [Rktok[half], Rvnew[p]], [bSr])
                    for hh in range(2):
                        h = h0 + hh
                        stt(S[:, h, :], S[:, h, :], gs[:, 56 + h:57 + h], vS[:, hh, :], ALU.mult, ALU.add,
                            [RS[p], Rgs, bSr], [RS[p]])
                    cp("pool", Sb[:, h0:h0 + 2, :], S[:, h0:h0 + 2, :], [RS[p]], [RSb[p]])

                cg = cd_gen()
                ready = set()
                next_p = 0
                active = []
                cd_alive = True
                while cd_alive or active or next_p < 4:
                    if cd_alive:
                        try:
                            r = next(cg)
                            if r is not None:
                                ready.add(r)
                        except StopIteration:
                            cd_alive = False
                    while next_p < 4 and (next_p // 2) in ready and len(active) < NACT and (not DBG_B or not cd_alive):
                        active.append(f_gen(next_p, SETS[next_p % NSETS]))
                        next_p += 1
                    for g in list(active):
                        try:
                            next(g)
                        except StopIteration:
                            active.remove(g)
                    step_ext()
                drain_ext()


                if lastt:
                    dma("sp", o_gdn_state[seq].rearrange("h d e -> d h e"), S, RS, [], "Sst" + seq)

            def tail(ti):
                seq, row0, n, (kind, k) = tiles[ti]
                slot = ti % 2
                x_ = xt[slot]
                Rx = Rxt[slot]
                first = (ti == 0) or (kind == "samp")
                lastt = (ti == len(tiles) - 2) or (kind == "samp")
                prev_n = tiles[ti - 1][2] if ti > 0 else None
                zs, Rzs = zss[ti % 2], Rzss[ti % 2]
                for h in range(8):
                    act(junk[:n, h * 128:(h + 1) * 128], o[:n, h * 128:(h + 1) * 128], AF.Square, Ro, [Rjunk, Rgs2],
                        accum_out=gs[:n, 72 + h:73 + h])
                yield
                act(gs[:n, 80:88], gs[:n, 72:80], AF.Ln, [Rgs2, Rc], [Rgs2], bias=eps_t[:n, 0:1], scale=1.0 / 128)
                act(gs[:n, 80:88], gs[:n, 80:88], AF.Exp, [Rgs2], [Rgs2], scale=-0.5)
                ov = o.rearrange("p (a b) -> p a b", a=8, b=128)
                tt("dve", ov[:n], ov[:n], gs[:n, 80:88].unsqueeze(2).to_broadcast([n, 8, 128]), ALU.mult, Ro + [Rgs2], Ro)
                yield
                tt("pool", ov[:n], ov[:n], onb[:n, :].unsqueeze(1).to_broadcast([n, 8, 128]), ALU.mult, Ro + [Ronb], Ro)
                yield
                tt("dve", yg[:n, :], o[:n, :], zs[:n, :], ALU.mult, Ro + [Rzs], [Ryg])
                yield
                b, br = bank()
                bb = bankb(b).rearrange("p (a b) -> p a b", a=8, b=128)
                for kc in range(8):
                    tr(bb[:, kc, :n], yg[:n, kc * 128:(kc + 1) * 128], ident_b[:n, :n], [Ryg, Rc], [br])
                cp("act", ygT[:, :, :n], bb[:, :, :n], [br], [RygT])
                yield
                for s in range(2):
                    b, br = bank()
                    for kc in range(8):
                        mm(b[:n, :], ygT[:, kc, :n], w_out[:, kc, s * 512:(s + 1) * 512], kc == 0, kc == 7, [RygT, Rwo], [br])
                    tt("dve", x_[:n, s * 512:(s + 1) * 512], b[:n, :], x_[:n, s * 512:(s + 1) * 512], ALU.add, [br, Rx], [Rx])
                    yield
                dma("sp", hs[0][row0:row0 + n, :], x_[:n, :], [Rx], [], f"xst{slot}")

                yield

            def run_il(gens):
                gens = [g for g in gens if g is not None]
                while gens:
                    for g in list(gens):
                        try:
                            next(g)
                        except StopIteration:
                            gens.remove(g)

            load_x(0)
            if len(tiles) > 1:
                load_x(1)
            run_il([head(0)])
            for ti in range(len(tiles)):
                mid(ti)
                drain_ext()
                tg = tail(ti)
                hg = head(ti + 1) if ti + 1 < len(tiles) else None
                nxt_load = (lambda t2=ti + 2: load_x(t2)) if ti + 2 < len(tiles) else None
                tail_alive = True
                while hg is not None:
                    try:
                        next(hg)
                    except StopIteration:
                        hg = None
                    if tail_alive:
                        try:
                            next(tg)
                        except StopIteration:
                            tail_alive = False
                if tail_alive:
                    ext["g"], ext["done"] = tg, nxt_load
                    if ti + 1 >= len(tiles):
                        drain_ext()
                elif nxt_load is not None:
                    nxt_load()
            drain_ext()
            P.barrier()

        def phase_ffn(layer, src, dst, final):
            A.off = persist_off
            Rwg, Rwu, Rwd = Reg("wg"), Reg("wu"), Reg("wd")
            wg = A.alloc([128, 8, DFF], BF16)
            wu = A.alloc([128, 8, DFF], BF16)
            wd = A.alloc([128, NFC, D], BF16)
            load_w(wg, ffn_w_gate[layer], 8, "wg", Rwg)
            load_w(wu, ffn_w_up[layer], 8, "wu", Rwu)
            load_w(wd, ffn_w_down[layer], NFC, "wd", Rwd)
            stage = A.alloc([128, 128], F32)
            Rst = Reg("stage")
            gfm = A.alloc([128, 8], F32)
            Rg = Reg("gfm")
            load_featmajor(gfm, norm_ffn[layer], 8, stage, Rst, Rg, "st")
            if final:
                gfin = A.alloc([128, D], F32)
                Rgfin = Reg("gfin")
                dma("sp", gfin, norm_final.partition_broadcast(128), [], [Rgfin], "gfin")
            NSL = 6
            xt = [A.alloc([128, D], F32) for _ in range(NSL)]
            Rxt = [Reg(f"xt{i}") for i in range(NSL)]
            junk = A.alloc([128, D], BF16)
            Rjunk = Reg("junk")
            small = A.alloc([128, 8], F32)
            Rsm = Reg("small")
            hn = A.alloc([128, D], BF16)
            Rhn = Reg("hn")
            hnT = A.alloc([128, 8, 512], BF16)
            RhnT = Reg("hnT")
            sg = [A.alloc([128, 512], F32) for _ in range(2)]
            Rsg = [Reg("sg0"), Reg("sg1")]
            hT = A.alloc([128, NFC, 512], BF16)
            RhT = Reg("hT")

            macros = []
            cur, tot = [], 0
            for ti, tl in enumerate(tiles):
                if tot + tl[2] > 512:
                    macros.append(cur)
                    cur, tot = [], 0
                cur.append(ti)
                tot += tl[2]
            if cur:
                macros.append(cur)

            def load_x(ti):
                seq, row0, n, _ = tiles[ti]
                slot = ti % NSL
                dma("sp", xt[slot][:n, :], src[row0:row0 + n, :], [], [Rxt[slot]], f"fx{slot}")

            NPRE = 5
            for ti in range(min(NPRE, len(tiles))):
                load_x(ti)
            nloaded = min(NPRE, len(tiles))
            for mac in macros:
                offs = []
                NT = 0
                for ti in mac:
                    offs.append(NT)
                    NT += tiles[ti][2]
                for j, ti in enumerate(mac):
                    seq, row0, n, (kind, k) = tiles[ti]
                    slot = ti % NSL
                    rmsnorm_T(xt[slot], Rxt[slot], n, gfm, Rg, hn, Rhn, hnT[:, :, offs[j]:offs[j] + n], RhnT,
                              small, Rsm, junk, Rjunk)
                for fc in range(NFC):
                    bg, bgr = bank()
                    bu, bur = bank()
                    for kc in range(8):
                        mm(bg[:, :NT], wg[:, kc, fc * 128:(fc + 1) * 128], hnT[:, kc, :NT], kc == 0, kc == 7, [Rwg, RhnT], [bgr])
                    for kc in range(8):
                        mm(bu[:, :NT], wu[:, kc, fc * 128:(fc + 1) * 128], hnT[:, kc, :NT], kc == 0, kc == 7, [Rwu, RhnT], [bur])
                    sl = fc % 2
                    act(sg[sl][:, :NT], bg[:, :NT], AF.Silu, [bgr], [Rsg[sl]])
                    tt("dve", hT[:, fc, :NT], bu[:, :NT], sg[sl][:, :NT], ALU.mult, [bur, Rsg[sl]], [RhT])
                for j, ti in enumerate(mac):
                    seq, row0, n, (kind, k) = tiles[ti]
                    slot = ti % NSL
                    x_ = xt[slot]
                    Rx = Rxt[slot]
                    o0 = offs[j]
                    for s in range(2):
                        b, br = bank()
                        for fc in range(NFC):
                            mm(b[:n, :], hT[:, fc, o0:o0 + n], wd[:, fc, s * 512:(s + 1) * 512], fc == 0, fc == NFC - 1,
                               [RhT, Rwd], [br])
                        tt("dve", x_[:n, s * 512:(s + 1) * 512], b[:n, :], x_[:n, s * 512:(s + 1) * 512], ALU.add, [br, Rx], [Rx])
                    if not final:
                        dma("sp", dst[row0:row0 + n, :], x_[:n, :], [Rx], [], f"fs{slot}")
                    else:
                        act(junk[:n, :], x_[:n, :], AF.Square, [Rx], [Rjunk, Rsm], accum_out=small[:n, 4:5])
                        act(small[:n, 5:6], small[:n, 4:5], AF.Ln, [Rsm, Rc], [Rsm], bias=eps_t[:n, 0:1], scale=1.0 / D)
                        act(small[:n, 6:7], small[:n, 5:6], AF.Exp, [Rsm], [Rsm], scale=-0.5)
                        stt(x_[:n, :], x_[:n, :], small[:n, 6:7], gfin[:n, :], ALU.mult, ALU.mult, [Rx, Rsm, Rgfin], [Rx])
                        if kind == "x":
                            dma("sp", y_prompt[k * 128:(k + 1) * 128, :], x_[:n, :], [Rx], [], f"fs{slot}")
                        elif kind == "samp":
                            dma("sp", y_sample, x_[:n, :], [Rx], [], f"fs{slot}")
                    if nloaded < len(tiles):
                        load_x(nloaded)
                        nloaded += 1
            P.barrier()

        def phase_rwkv(src, dst):
            phase_rwkv_impl(src, dst)

        def phase_rwkv_impl(src, dst):
            A.off = persist_off
            Rwr, Rwk, Rwv, Rwo = Reg("w_r"), Reg("w_k"), Reg("w_v"), Reg("w_o")
            w_r = A.alloc([128, 8, D], BF16)
            w_k = A.alloc([128, 8, D], BF16)
            w_v = A.alloc([128, 8, D], BF16)
            w_o = A.alloc([128, 8, D], BF16)
            w1 = A.alloc([128, 8, 64], BF16)
            a1 = A.alloc([128, 8, 64], BF16)
            g1 = A.alloc([128, 8, 160], BF16)
            w2 = A.alloc([128, D], BF16)
            a2 = A.alloc([128, D], BF16)
            g2 = A.alloc([128, 2, D], BF16)
            Rlo = Reg("lora_w")
            load_w(w_r, rwkv_w_r, 8, "w_r", Rwr)
            load_w(w_k, rwkv_w_k, 8, "w_k", Rwk)
            load_w(w_v, rwkv_w_v, 8, "w_v", Rwv)
            load_w(w_o, rwkv_w_o, 8, "w_o", Rwo)
            load_w(w1, rwkv_w1, 8, "lo", Rlo)
            load_w(a1, rwkv_a1, 8, "lo", Rlo)
            load_w(g1, rwkv_g1, 8, "lo", Rlo)
            dma("pool", w2[0:64, :], rwkv_w2, [], [Rlo], "lo", max_dma_last_dim=4096)
            dma("pool", a2[0:64, :], rwkv_a2, [], [Rlo], "lo", max_dma_last_dim=4096)
            dma("pool", g2[:, 0, :], rwkv_g2[0:128, :], [], [Rlo], "lo", max_dma_last_dim=4096)
            dma("pool", g2[0:32, 1, :], rwkv_g2[128:160, :], [], [Rlo], "lo", max_dma_last_dim=4096)
            stage = A.alloc([128, 128], F32)
            Rst = Reg("stage")
            prm = A.alloc([128, 112], F32)
            Rprm = Reg("prm")
            dma("sp", stage[0:48, :], rwkv_mu.rearrange("m (c p) -> (m c) p", p=128), [], [Rst], "st")
            for i, vec in enumerate([rwkv_w0, rwkv_a0, rwkv_k_k, rwkv_k_a, rwkv_r_k, norm_mix[1]]):
                dma("sp", stage[48 + 8 * i:56 + 8 * i, :], vec.rearrange("(c p) -> c p", p=128), [], [Rst], "st")
            b, br = bank()
            tr(b[:, 0:96], stage[0:96, :], ident_f[0:96, 0:96], [Rst, Rc], [br])
            cp("dve", prm[:, 0:96], b[:, 0:96], [br], [Rprm])
            MU, W0, A0, KK, KA, RK, GM, OMKA = 0, 48, 56, 64, 72, 80, 88, 96
            ts("dve", prm[:, OMKA:OMKA + 8], prm[:, KA:KA + 8], -1.0, ALU.mult, [Rprm], [Rprm], s2=1.0, op1=ALU.add)
            lnw = A.alloc([128, D], F32)
            lnb = A.alloc([128, D], F32)
            Rln = Reg("ln")
            dma("sp", lnw, rwkv_ln_w.partition_broadcast(128), [], [Rln], "lnw")
            dma("sp", lnb, rwkv_ln_b.partition_broadcast(128), [], [Rln], "lnb")
            Rk2 = Reg("consts2")
            blk64 = A.alloc([128, 128], BF16)
            hm = A.alloc([128, 2], F32)
            headind = A.alloc([128, 2], BF16)
            nhmrow = A.alloc([128, 2, 128], BF16)
            hmrow = A.alloc([128, 2, 128], BF16)
            memset("pool", blk64, 0.0, [Rk2])
            memset("pool", blk64[0:64, 0:64], 1.0, [Rk2])
            memset("pool", blk64[64:128, 64:128], 1.0, [Rk2])
            memset("pool", hm, 0.0, [Rk2])
            memset("pool", hm[0:64, 0:1], 1.0, [Rk2])
            memset("pool", hm[64:128, 1:2], 1.0, [Rk2])
            cp("pool", headind, hm, [Rk2], [Rk2])
            memset("pool", hmrow, 0.0, [Rk2])
            memset("pool", hmrow[:, 0, 0:64], 1.0, [Rk2])
            memset("pool", hmrow[:, 1, 64:128], 1.0, [Rk2])
            memset("pool", nhmrow, 0.0, [Rk2])
            memset("pool", nhmrow[:, 0, 0:64], -1.0, [Rk2])
            memset("pool", nhmrow[:, 1, 64:128], -1.0, [Rk2])

            xt = [A.alloc([128, D], F32) for _ in range(2)]
            Rxt = [Reg("xt0"), Reg("xt1")]
            junk = A.alloc([128, D], BF16)
            Rjunk = Reg("junk")
            small = A.alloc([128, 8], F32)
            Rsm = Reg("small")
            hn = A.alloc([128, D], BF16)
            Rhn = Reg("hn")
            hnTe = A.alloc([128, 8, 130], F32)
            RhnT = Reg("hnTe")
            xm = A.alloc([128, 6, 8, 128], BF16)
            Rxm = [Reg(f"xm{m}") for m in range(6)]
            NTMP = 14
            tpblk = A.alloc([128, 2 * NTMP * 128], F32)
            tp = [[tpblk[:, (s_ * NTMP + i_) * 128:(s_ * NTMP + i_ + 1) * 128] for i_ in range(NTMP)] for s_ in range(2)]
            Rtp = [[Reg(f"tp{s}_{i}") for i in range(NTMP)] for s in range(2)]
            xx = tpblk[:, 0:1024].rearrange("p (a b) -> p a b", a=8, b=128)
            Rxx = Rtp[0][0:8]
            sqb = [A.alloc([128, 128], BF16) for _ in range(2)]
            Rsqb = [Reg("sqb0"), Reg("sqb1")]
            bt = A.alloc([128, 8, 128], BF16)
            kt = A.alloc([128, 8, 128], BF16)
            rkr = A.alloc([128, 8, 128], BF16)
            Rbt = [Reg(f"bt{i}") for i in range(8)]
            Rkt = [Reg(f"kt{i}") for i in range(8)]
            Rrkr = Reg("rkr")
            krp = A.alloc([128, 8, 4, 128], BF16)
            Rkrp = [Reg(f"krp{i}") for i in range(8)]
            nbtok = A.alloc([128, 8, 2, 128], BF16)
            ktok = A.alloc([128, 8, 2, 128], BF16)
            Rtok = [Reg(f"tok{i}") for i in range(8)]
            GL = A.alloc([128, 8], F32)
            RGL = [Reg(f"GL{i}") for i in range(8)]
            lo1 = A.alloc([128, 2, 128], BF16)
            lo3 = A.alloc([128, 2, 128], BF16)
            Rlo1, Rlo3 = Reg("lo1"), Reg("lo3")
            vfs = [A.alloc([128, D], F32) for _ in range(2)]
            Rvfs = [Reg("vf0"), Reg("vf1")]
            vb = A.alloc([128, D], BF16)
            Rv = Reg("v")
            gates = [A.alloc([128, D], BF16) for _ in range(2)]
            Rgates = [Reg("gate0"), Reg("gate1")]
            NSETS = 3
            SETS = []
            for si in range(NSETS):
                PRs = [A.alloc([128, 2, 2, 128], BF16) for _ in range(2)]
                RPRs = [Reg(f"PR{si}a"), Reg(f"PR{si}b")]
                PTs = [A.alloc([128, 2, 128], BF16) for _ in range(2)]
                RPTs = [Reg(f"PT{si}a"), Reg(f"PT{si}b")]
                B0s = A.alloc([128, 2, 128], BF16)
                B0Ts = A.alloc([128, 2, 128], BF16)
                SETS.append((PRs, RPRs, PTs, RPTs, B0s, Reg(f"B0_{si}"), B0Ts, Reg(f"B0T_{si}"),
                             A.alloc([128, 2, 128], BF16), A.alloc([128, 2, 128], BF16), A.alloc([128, 2, 128], BF16),
                             Reg(f"mats{si}"), A.alloc([128, 2, 128], BF16), Reg(f"Minv{si}"),
                             A.alloc([128, 2, 64], BF16), A.alloc([128, 2, 64], BF16), Reg(f"RHS{si}"), Reg(f"U{si}")))
            yv = A.alloc([128, D], F32)
            Ry = Reg("y")
            ysq = A.alloc([128, D], F32)
            Rysq = Reg("ysq")
            st16 = A.alloc([128, 96], F32)
            Rst16 = Reg("st16")
            Pst = A.alloc([128, 8, 64], F32)
            Pb = A.alloc([128, 8, 64], BF16)
            PG = A.alloc([128, 2, 64], F32)
            RP = [Reg(f"P{i}") for i in range(8)]
            RPb = [Reg(f"Pb{i}") for i in range(8)]
            RPG = [Reg("PG0"), Reg("PG1")]
            Snat = tpblk[:, NTMP * 128:NTMP * 128 + 1024].rearrange("p (a b) -> p a b", a=16, b=64)
            RSnat = Rtp[1][0:8]
            yg = A.alloc([128, D], BF16)
            Ryg = Reg("yg")
            ygT = A.alloc([128, 8, 128], BF16)
            RygT = Reg("ygT")
            hnT = hnTe[:, :, 1:129]

            def load_x(ti):
                seq, row0, n, _ = tiles[ti]
                slot = ti % 2
                dma("sp", xt[slot][:n, :], src[row0:row0 + n, :], [], [Rxt[slot]], f"xt{slot}")

            ext = {"g": None, "done": None}

            def step_ext():
                if ext["g"] is not None:
                    try:
                        next(ext["g"])
                    except StopIteration:
                        ext["g"] = None
                        if ext["done"] is not None:
                            ext["done"]()
                            ext["done"] = None

            def drain_ext():
                while ext["g"] is not None:
                    step_ext()

            def head(ti):
                seq, row0, n, (kind, k) = tiles[ti]
                slot = ti % 2
                x_ = xt[slot]
                Rx = Rxt[slot]
                first = (ti == 0) or (kind == "samp")
                lastt = (ti == len(tiles) - 2) or (kind == "samp")
                prev_n = tiles[ti - 1][2] if ti > 0 else None
                gate, Rgate = gates[ti % 2], Rgates[ti % 2]
                vf, Rvf = vfs[ti % 2], Rvfs[ti % 2]
                XR, XW, XK, XV, XA, XG = 0, 1, 2, 3, 4, 5
                if first and seq == "p":
                    memset("pool", Pst, 0.0, RP)
                    memset("pool", Pb, 0.0, RPb)
                    memset("pool", hnTe[:, :, 0:1], 0.0, [RhnT])
                elif first:
                    dma("sp", Snat[0:64, :, :], state_rwkv.rearrange("h i j -> i h j"), [], RSnat, "Sld")
                    for half in range(2):
                        b, br = bank()
                        bv = b.rearrange("p (a b) -> p a b", a=8, b=64)
                        for j in range(4):
                            oc = half * 4 + j
                            tr(bv[:, j, :], Snat[0:64, 2 * oc:2 * oc + 2, :].rearrange("p a b -> p (a b)"),
                               ident_f[0:64, 0:64], RSnat + [Rc], [br])
                        cp("dve", Pst[:, half * 4:(half + 1) * 4, :], bv[:, 0:4, :], [br], RP)
                    cp("pool", Pb, Pst, RP, RPb)
                    dma("sp", stage[0:8, :], cache_rwkv_shift.rearrange("o (c p) -> (o c) p", p=128), [], [Rst], "st")
                    b, br = bank()
                    tr(b[:, 0:8], stage[0:8, :], ident_f[0:8, 0:8], [Rst, Rc], [br])
                    cp("dve", hnTe[:, :, 0:1], b[:, 0:8].unsqueeze(2), [br], [RhnT])
                else:
                    cp("pool", hnTe[:, :, 0:1], hnTe[:, :, prev_n:prev_n + 1], [RhnT], [RhnT])

                yield
                rmsnorm_T(x_, Rx, n, prm[:, GM:GM + 8], Rprm, hn, Rhn, hnT, RhnT, small, Rsm, junk, Rjunk)
                if lastt:
                    b, br = bank()
                    tr(b[0:8, 0:128], hnTe[:, :, n:n + 1].rearrange("p a b -> p (a b)"), ident_f, [RhnT, Rc], [br])
                    cp("dve", stage[0:8, :], b[0:8, 0:128], [br], [Rst])
                    dma("sp", o_rwkv_shift[seq].rearrange("o (c p) -> (o c) p", p=128), stage[0:8, :], [Rst], [], "shst" + seq)
                yield
                tt("pool", xx[:, :, :n], hnTe[:, :, 0:n], hnTe[:, :, 1:n + 1], ALU.subtract, [RhnT], Rxx)
                for m in range(6):
                    eng = "dve" if m % 2 == 0 else "pool"
                    tt(eng, xm[:, m, :, :n], xx[:, :, :n],
                       prm[:, MU + 8 * m:MU + 8 * m + 8].unsqueeze(2).to_broadcast([128, 8, n]), ALU.mult,
                       Rxx + [Rprm], [Rxm[m]])
                    tt(eng, xm[:, m, :, :n], xm[:, m, :, :n], hnTe[:, :, 1:n + 1], ALU.add, [Rxm[m], RhnT], [Rxm[m]])
                    if m % 2 == 1:
                        yield
                XR, XW, XK, XV, XA, XG = 0, 1, 2, 3, 4, 5
                b, br = bank()
                for kc in range(8):
                    mm(b[0:64, 0:n], w1[:, kc, :], xm[:, XW, kc, :n], kc == 0, kc == 7, [Rlo, Rxm[XW]], [br])
                for kc in range(8):
                    mm(b[0:64, 128:128 + n], a1[:, kc, :], xm[:, XA, kc, :n], kc == 0, kc == 7, [Rlo, Rxm[XA]], [br])
                act(lo1[0:64, 0, :n], b[0:64, 0:n], AF.Tanh, [br], [Rlo1])
                cp("dve", lo1[0:64, 1, :n], b[0:64, 128:128 + n], [br], [Rlo1])
                yield
                b, br = bank()
                for kc in range(8):
                    mm(b[:, 0:n], g1[:, kc, 0:128], xm[:, XG, kc, :n], kc == 0, kc == 7, [Rlo, Rxm[XG]], [br])
                for kc in range(8):
                    mm(b[0:32, 128:128 + n], g1[:, kc, 128:160], xm[:, XG, kc, :n], kc == 0, kc == 7, [Rlo, Rxm[XG]], [br])
                act(lo3[:, 0, :n], b[:, 0:n], AF.Sigmoid, [br], [Rlo3])
                act(lo3[0:32, 1, :n], b[0:32, 128:128 + n], AF.Sigmoid, [br], [Rlo3])
                yield
                for s in range(2):
                    b, br = bank()
                    mm(b[:n, :], lo3[:, 0, :n], g2[:, 0, s * 512:(s + 1) * 512], True, False, [Rlo3, Rlo], [br])
                    mm(b[:n, :], lo3[0:32, 1, :n], g2[0:32, 1, s * 512:(s + 1) * 512], False, True, [Rlo3, Rlo], [br])
                    cp("act", gate[:n, s * 512:(s + 1) * 512], b[:n, :], [br], [Rgate])
                    yield
                for s in range(2):
                    b, br = bank()
                    for kc in range(8):
                        mm(b[:n, :], xm[:, XV, kc, :n], w_v[:, kc, s * 512:(s + 1) * 512], kc == 0, kc == 7, [Rxm[XV], Rwv], [br])
                    cp("act", vf[:n, s * 512:(s + 1) * 512], b[:n, :], [br], [Rvf])
                    cp("dve", vb[:n, s * 512:(s + 1) * 512], b[:n, :], [br], [Rv])
                    yield


                yield

            def mid(ti):
                seq, row0, n, (kind, k) = tiles[ti]
                slot = ti % 2
                x_ = xt[slot]
                Rx = Rxt[slot]
                first = (ti == 0) or (kind == "samp")
                lastt = (ti == len(tiles) - 2) or (kind == "samp")
                prev_n = tiles[ti - 1][2] if ti > 0 else None
                gate, Rgate = gates[ti % 2], Rgates[ti % 2]
                vf, Rvf = vfs[ti % 2], Rvfs[ti % 2]
                XR, XW, XK, XV, XA, XG = 0, 1, 2, 3, 4, 5
                nf = int(math.log2(n))
                by = [(psf[6][:], psr[6]), (psf[7][:], psr[7])]
                LD, C_, AT, KX, RT_, KKn, TKA, K2, TMP, E1, E2, E3, RR, KR = range(14)

                def prep_gen():
                    for pair in range(4):
                        ocs = (2 * pair, 2 * pair + 1)
                        T2 = {oc: tp[oc % 2] for oc in ocs}
                        R2 = {oc: Rtp[oc % 2] for oc in ocs}
                        bl2 = {}
                        for oc in ocs:
                            T_, RT = T2[oc], R2[oc]
                            osl = slice(oc * 128, (oc + 1) * 128)
                            br_, brr = bank()
                            bk_, bkr = bank()
                            bl_, blr = bank()
                            bl2[oc] = (bl_, blr)
                            for kc in range(8):
                                mm(br_[:, 0:n], w_r[:, kc, osl], xm[:, XR, kc, :n], kc == 0, kc == 7, [Rwr, Rxm[XR]], [brr])
                            for kc in range(8):
                                mm(bk_[:, 0:n], w_k[:, kc, osl], xm[:, XK, kc, :n], kc == 0, kc == 7, [Rwk, Rxm[XK]], [bkr])
                            mm(bl_[:, 0:n], w2[0:64, osl], lo1[0:64, 0, :n], True, True, [Rlo, Rlo1], [blr])
                            mm(bl_[:, 128:128 + n], a2[0:64, osl], lo1[0:64, 1, :n], True, True, [Rlo, Rlo1], [blr])
                            cp("act", T_[RR][:, :n], br_[:, 0:n], [brr], [RT[RR]])
                            cp("act", T_[KR][:, :n], bk_[:, 0:n], [bkr], [RT[KR]])
                            act(T_[KX][:, :n], bk_[:, 0:n], AF.Copy, [bkr, Rprm], [RT[KX]], scale=prm[:, KK + oc:KK + oc + 1])
                            act(T_[LD][:, :n], bl_[:, 0:n], AF.Sigmoid, [blr, Rprm], [RT[LD]], bias=prm[:, W0 + oc:W0 + oc + 1])
                            act(T_[AT][:, :n], bl_[:, 128:128 + n], AF.Sigmoid, [blr, Rprm], [RT[AT]], bias=prm[:, A0 + oc:A0 + oc + 1])
                        steps = []
                        for oc in ocs:
                            T_, RT = T2[oc], R2[oc]
                            sl = oc % 2
                            L = []
                            L.append(lambda T_=T_, RT=RT: ts("pool", T_[LD][:, :n], T_[LD][:, :n], -math.exp(-0.5), ALU.mult, [RT[LD]], [RT[LD]]))
                            L.append(lambda T_=T_, RT=RT: P.op("dve", lambda e, o_=T_[C_][:, :n], d0=ones_f[:, :n], d1=T_[LD][:, :n]:
                                     e.tensor_tensor_scan(o_, d0, d1, 0.0, ALU.mult, ALU.add), [Rc, RT[LD]], [RT[C_]]))
                            L.append(lambda T_=T_, RT=RT, sl=sl: tt("pool", sqb[sl][:, :n], T_[KX][:, :n], T_[KX][:, :n], ALU.mult, [RT[KX]], [Rsqb[sl]]))

                            def ssq(T_=T_, RT=RT, sl=sl):
                                bs_, bsr = bank()
                                mm(bs_[:, 0:n], blk64, sqb[sl][:, :n], True, True, [Rk2, Rsqb[sl]], [bsr])
                                act(T_[RT_][:, :n], bs_[:, 0:n], AF.Ln, [bsr, Rc], [RT[RT_]], bias=eps_t[:, 0:1])
                            L.append(ssq)
                            L.append(lambda T_=T_, RT=RT, oc=oc: ts("dve", T_[TKA][:, :n], T_[AT][:, :n], prm[:, KA + oc:KA + oc + 1], ALU.mult, [RT[AT], Rprm], [RT[TKA]],
                                     s2=prm[:, OMKA + oc:OMKA + oc + 1], op1=ALU.add))
                            L.append(lambda T_=T_, RT=RT: act(T_[RT_][:, :n], T_[RT_][:, :n], AF.Exp, [RT[RT_]], [RT[RT_]], scale=-0.5))
                            L.append(lambda T_=T_, RT=RT: tt("pool", T_[TMP][:, :n], T_[C_][:, :n], T_[LD][:, :n], ALU.subtract, [RT[C_], RT[LD]], [RT[TMP]]))
                            L.append(lambda T_=T_, RT=RT: act(T_[E1][:, :n], T_[TMP][:, :n], AF.Exp, [RT[TMP]], [RT[E1]]))
                            L.append(lambda T_=T_, RT=RT: tt("dve", T_[K2][:, :n], T_[KR][:, :n], T_[TKA][:, :n], ALU.mult, [RT[KR], RT[TKA]], [RT[K2]]))
                            L.append(lambda T_=T_, RT=RT: act(T_[E2][:, :n], T_[C_][:, :n], AF.Exp, [RT[C_]], [RT[E2]], scale=-1.0))
                            L.append(lambda T_=T_, RT=RT: tt("pool", T_[KKn][:, :n], T_[KX][:, :n], T_[RT_][:, :n], ALU.mult, [RT[KX], RT[RT_]], [RT[KKn]]))
                            L.append(lambda T_=T_, RT=RT: act(T_[E3][:, :n], T_[C_][:, :n], AF.Exp, [RT[C_]], [RT[E3]]))
                            L.append(lambda T_=T_, RT=RT, oc=oc: tt("pool", kt[:, oc, :n], T_[K2][:, :n], T_[E2][:, :n], ALU.mult, [RT[K2], RT[E2]], [Rkt[oc]]))
                            for h2 in range(2):
                                L.append(lambda T_=T_, RT=RT, oc=oc, h2=h2: stt(krp[:, oc, 2 * h2 + 0, :n], T_[KKn][:, :n], hm[:, h2:h2 + 1], T_[E1][:, :n], ALU.mult, ALU.mult,
                                         [RT[KKn], Rk2, RT[E1]], [Rkrp[oc]]))
                            L.append(lambda T_=T_, RT=RT: tt("pool", T_[TMP][:, :n], T_[KKn][:, :n], T_[AT][:, :n], ALU.mult, [RT[KKn], RT[AT]], [RT[TMP]]))
                            for h2 in range(2):
                                L.append(lambda T_=T_, RT=RT, oc=oc, h2=h2: stt(krp[:, oc, 2 * h2 + 1, :n], T_[RR][:, :n], hm[:, h2:h2 + 1], T_[E3][:, :n], ALU.mult, ALU.mult,
                                         [RT[RR], Rk2, RT[E3]], [Rkrp[oc]]))
                            L.append(lambda T_=T_, RT=RT, oc=oc: tt("pool", bt[:, oc, :n], T_[TMP][:, :n], T_[E2][:, :n], ALU.mult, [RT[TMP], RT[E2]], [Rbt[oc]]))
                            L.append(lambda T_=T_, RT=RT, oc=oc: cp("pool", GL[:, oc:oc + 1], T_[E3][:, n - 1:n], [RT[E3]], [RGL[oc]]))
                            L.append(lambda T_=T_, RT=RT, oc=oc: stt(rkr[:, oc, :n], T_[RR][:, :n], prm[:, RK + oc:RK + oc + 1], T_[K2][:, :n], ALU.mult, ALU.mult,
                                     [RT[RR], Rprm, RT[K2]], [Rrkr]))
                            steps.append(L)
                        for i in range(len(steps[0])):
                            for L in steps:
                                L[i]()
                            if i % 6 == 5:
                                yield None
                        yield ocs[0]
                        yield ocs[1]

                def g_gen(oc, S_):
                    (PRs, RPRs, PTs, RPTs, B0s, RB0s, B0Ts, RB0Ts, BrTn, AkT, BkT, Rmats, MinvTs, RMinvs, RHSb, Ub, RRHS, RU) = S_
                    b, br = bank()
                    bb = bankb(b).rearrange("p (a b) -> p a b", a=8, b=128)
                    tr(bb[:n, 0, :], bt[:, oc, :n], ident_b, [Rbt[oc], Rc], [br])
                    tr(bb[:n, 1, :], kt[:, oc, :n], ident_b, [Rkt[oc], Rc], [br])
                    tt("dve", nbtok[:n, oc], bb[:n, 0:1, :].to_broadcast([n, 2, 128]), nhmrow[:n], ALU.mult, [br, Rk2], [Rtok[oc]])
                    tt("dve", ktok[:n, oc], bb[:n, 1:2, :].to_broadcast([n, 2, 128]), hmrow[:n], ALU.mult, [br, Rk2], [Rtok[oc]])
                    yield
                    bm1, bm1r = bank()
                    bm2, bm2r = bank()
                    bm3, bm3r = bank()
                    vm1 = bm1.rearrange("p (h t c) -> p h t c", h=2, t=2, c=128)
                    vm2 = bm2.rearrange("p (h t c) -> p h t c", h=2, t=2, c=128)
                    vm3 = bm3.rearrange("p (g c) -> p g c", g=4, c=128)
                    for h2 in range(2):
                        if n == 128:
                            mm(vm1[:n, h2, :, :n], bt[:, oc, :n], krp[:, oc, 2 * h2:2 * h2 + 2, :n], True, True, [Rbt[oc], Rkrp[oc]], [bm1r])
                            mm(vm2[:n, h2, :, :n], kt[:, oc, :n], krp[:, oc, 2 * h2:2 * h2 + 2, :n], True, True, [Rkt[oc], Rkrp[oc]], [bm2r])
                        else:
                            for t_ in range(2):
                                mm(vm1[:n, h2, t_, :n], bt[:, oc, :n], krp[:, oc, 2 * h2 + t_, :n], True, True, [Rbt[oc], Rkrp[oc]], [bm1r])
                                mm(vm2[:n, h2, t_, :n], kt[:, oc, :n], krp[:, oc, 2 * h2 + t_, :n], True, True, [Rkt[oc], Rkrp[oc]], [bm2r])
                        mm(vm3[:n, h2, :n], krp[:, oc, 2 * h2, :n], bt[:, oc, :n], True, True, [Rkrp[oc], Rbt[oc]], [bm3r])
                    stt(B0s[:n, :, :n], vm1[:n, :, 0, :n], -1.0, smask_u[:n, :n].unsqueeze(1).to_broadcast([n, 2, n]),
                        ALU.mult, ALU.mult, [bm1r, Rc], [RB0s])
                    stt(BrTn[:n, :, :n], vm1[:n, :, 1, :n], -1.0, imask_u[:n, :n].unsqueeze(1).to_broadcast([n, 2, n]),
                        ALU.mult, ALU.mult, [bm1r, Rc], [Rmats])
                    tt("dve", AkT[:n, :, :n], vm2[:n, :, 0, :n], smask_u[:n, :n].unsqueeze(1).to_broadcast([n, 2, n]),
                       ALU.mult, [bm2r, Rc], [Rmats])
                    tt("dve", BkT[:n, :, :n], vm2[:n, :, 1, :n], imask_u[:n, :n].unsqueeze(1).to_broadcast([n, 2, n]),
                       ALU.mult, [bm2r, Rc], [Rmats])
                    stt(B0Ts[:n, :, :n], vm3[:n, 0:2, :n], -1.0, smask_l[:n, :n].unsqueeze(1).to_broadcast([n, 2, n]),
                        ALU.mult, ALU.mult, [bm3r, Rc], [RB0Ts])
                    yield
                    res = []
                    for _ in neumann_gen(B0s[:n, :, :n], B0Ts[:n, :, :n], RB0s, RB0Ts, n, nf, PRs, RPRs, PTs, RPTs, 2, res, psum_acc=True):
                        yield
                    Rfin, RRfin = res[0]
                    cp("act", MinvTs[:n, :, :n], Rfin, [RRfin], [RMinvs])
                    bR, bRr = bank()
                    vR = bR[:, 0:128].rearrange("p (g c) -> p g c", g=2, c=64)
                    for h2 in range(2):
                        hd = 2 * oc + h2
                        mm(vR[:n, h2, :], krp[:, oc, 2 * h2, :n], Pb[:, oc, :], True, False, [Rkrp[oc], RPb[oc]], [bRr])
                        mm(vR[:n, h2, :], AkT[:n, h2, :n], vb[:n, hd * 64:(hd + 1) * 64], False, True, [Rmats, Rv], [bRr])
                    cp("dve", RHSb[:n], vR[:n], [bRr], [RRHS])
                    yield
                    bU, bUr = bank()
                    vU = bU[:, 0:128].rearrange("p (g c) -> p g c", g=2, c=64)
                    for h2 in range(2):
                        mm(vU[:n, h2, :], MinvTs[:n, h2, :n], RHSb[:n, h2, :], True, True, [RMinvs, RRHS], [bUr])
                    cp("act", Ub[:n], vU[:n], [bUr], [RU])
                    yield
                    drain_ext()
                    bY, bYr = bank()
                    vY = bY[:, 0:128].rearrange("p (g c) -> p g c", g=2, c=64)
                    for h2 in range(2):
                        hd = 2 * oc + h2
                        mm(vY[:n, h2, :], krp[:, oc, 2 * h2 + 1, :n], Pb[:, oc, :], True, False, [Rkrp[oc], RPb[oc]], [bYr])
                        mm(vY[:n, h2, :], BrTn[:n, h2, :n], Ub[:n, h2, :], False, False, [Rmats, RU], [bYr])
                        mm(vY[:n, h2, :], BkT[:n, h2, :n], vb[:n, hd * 64:(hd + 1) * 64], False, True, [Rmats, Rv], [bYr])
                    cp("act", yv[:n, oc * 128:(oc + 1) * 128], bY[:n, 0:128], [bYr], [Ry])
                    bP, bPr = bank()
                    for h2 in range(2):
                        hd = 2 * oc + h2
                        mm(bP[:, 0:64], nbtok[:n, oc, h2, :], Ub[:n, h2, :], h2 == 0, False, [Rtok[oc], RU], [bPr])
                        mm(bP[:, 0:64], ktok[:n, oc, h2, :], vb[:n, hd * 64:(hd + 1) * 64], False, h2 == 1, [Rtok[oc], Rv], [bPr])
                    ts("pool", PG[:, oc % 2, :], Pst[:, oc, :], GL[:, oc:oc + 1], ALU.mult, [RP[oc], RGL[oc]], [RPG[oc % 2]])
                    stt(Pst[:, oc, :], bP[:, 0:64], GL[:, oc:oc + 1], PG[:, oc % 2, :], ALU.mult, ALU.add,
                        [bPr, RGL[oc], RPG[oc % 2]], [RP[oc]])
                    cp("pool", Pb[:, oc, :], Pst[:, oc, :], [RP[oc]], [RPb[oc]])

                pg = prep_gen()
                ready = set()
                next_g = 0
                active = []
                prep_alive = True
                while prep_alive or active or next_g < 8:
                    if prep_alive:
                        try:
                            r = next(pg)
                            if r is not None:
                                ready.add(r)
                        except StopIteration:
                            prep_alive = False
                    while next_g < 8 and next_g in ready and len(active) < NSETS:
                        active.append(g_gen(next_g, SETS[next_g % NSETS]))
                        next_g += 1
                    for g in list(active):
                        try:
                            next(g)
                        except StopIteration:
                            active.remove(g)
                    step_ext()
                drain_ext()

                b, br = bank()
                for oc in range(8):
                    mm(b[:n, 2 * oc:2 * oc + 2], rkr[:, oc, :n], headind, True, True, [Rrkr, Rk2], [br])
                cp("dve", st16[:n, 32:48], b[:n, 0:16], [br], [Rst16])


                if lastt:
                    for half in range(2):
                        b, br = bank()
                        bv = b.rearrange("p (a b) -> p a b", a=4, b=128)
                        for j in range(4):
                            oc = half * 4 + j
                            tr(bv[0:64, j, :], Pst[:, oc, :], ident_f, RP + [Rc], [br])
                        cp("dve", Snat[0:64, half * 8:(half + 1) * 8, :].rearrange("p a b -> p (a b)"),
                           b[0:64, :], [br], RSnat)
                    dma("sp", o_rwkv_state[seq].rearrange("h i j -> i h j"), Snat[0:64, :, :], RSnat, [], "Sst" + seq)

            def tail(ti):
                seq, row0, n, (kind, k) = tiles[ti]
                slot = ti % 2
                x_ = xt[slot]
                Rx = Rxt[slot]
                first = (ti == 0) or (kind == "samp")
                lastt = (ti == len(tiles) - 2) or (kind == "samp")
                prev_n = tiles[ti - 1][2] if ti > 0 else None
                gate, Rgate = gates[ti % 2], Rgates[ti % 2]
                vf, Rvf = vfs[ti % 2], Rvfs[ti % 2]
                XR, XW, XK, XV, XA, XG = 0, 1, 2, 3, 4, 5
                y3 = yv.rearrange("p (a b) -> p a b", a=16, b=64)
                q3 = ysq.rearrange("p (a b) -> p a b", a=16, b=64)
                v3 = vf.rearrange("p (a b) -> p a b", a=16, b=64)
                P.op("dve", lambda e, o_=st16[:n, 0:16], i_=y3[:n]: e.tensor_reduce(o_, i_, mybir.AxisListType.X, ALU.add),
                     [Ry], [Rst16])
                tt("pool", ysq[:n, :], yv[:n, :], yv[:n, :], ALU.mult, [Ry], [Rysq])
                P.op("dve", lambda e, o_=st16[:n, 16:32], i_=q3[:n]: e.tensor_reduce(o_, i_, mybir.AxisListType.X, ALU.add),
                     [Rysq], [Rst16])
                yield
                ts("dve", st16[:n, 0:16], st16[:n, 0:16], 1.0 / 64, ALU.mult, [Rst16], [Rst16])
                tt("dve", st16[:n, 48:64], st16[:n, 0:16], st16[:n, 0:16], ALU.mult, [Rst16], [Rst16])
                stt(st16[:n, 16:32], st16[:n, 16:32], 1.0 / 64, st16[:n, 48:64], ALU.mult, ALU.subtract, [Rst16], [Rst16])
                act(st16[:n, 16:32], st16[:n, 16:32], AF.Ln, [Rst16, Rc], [Rst16], bias=eps_t[:n, 2:3])
                act(st16[:n, 16:32], st16[:n, 16:32], AF.Exp, [Rst16], [Rst16], scale=-0.5)
                yield
                tt("dve", y3[:n], y3[:n], st16[:n, 0:16].unsqueeze(2).to_broadcast([n, 16, 64]), ALU.subtract, [Ry, Rst16], [Ry])
                tt("pool", y3[:n], y3[:n], st16[:n, 16:32].unsqueeze(2).to_broadcast([n, 16, 64]), ALU.mult, [Ry, Rst16], [Ry])
                yield
                tt("dve", yv[:n, :], yv[:n, :], lnw[:n, :], ALU.mult, [Ry, Rln], [Ry])
                tt("pool", yv[:n, :], yv[:n, :], lnb[:n, :], ALU.add, [Ry, Rln], [Ry])
                yield
                tt("dve", q3[:n], v3[:n], st16[:n, 32:48].unsqueeze(2).to_broadcast([n, 16, 64]), ALU.mult, [Rvf, Rst16], [Rysq])
                tt("pool", yv[:n, :], yv[:n, :], ysq[:n, :], ALU.add, [Ry, Rysq], [Ry])
                yield
                tt("dve", yg[:n, :], yv[:n, :], gate[:n, :], ALU.mult, [Ry, Rgate], [Ryg])
                b, br = bank()
                bb = bankb(b).rearrange("p (a b) -> p a b", a=8, b=128)
                for kc in range(8):
                    tr(bb[:, kc, :n], yg[:n, kc * 128:(kc + 1) * 128], ident_b[:n, :n], [Ryg, Rc], [br])
                cp("act", ygT[:, :, :n], bb[:, :, :n], [br], [RygT])
                yield
                for s in range(2):
                    b, br = bank()
                    for kc in range(8):
                        mm(b[:n, :], ygT[:, kc, :n], w_o[:, kc, s * 512:(s + 1) * 512], kc == 0, kc == 7, [RygT, Rwo], [br])
                    tt("dve", x_[:n, s * 512:(s + 1) * 512], b[:n, :], x_[:n, s * 512:(s + 1) * 512], ALU.add, [br, Rx], [Rx])
                    yield
                dma("sp", dst[row0:row0 + n, :], x_[:n, :], [Rx], [], f"xst{slot}")

                yield

            def run_il(gens):
                gens = [g for g in gens if g is not None]
                while gens:
                    for g in list(gens):
                        try:
                            next(g)
                        except StopIteration:
                            gens.remove(g)

            load_x(0)
            if len(tiles) > 1:
                load_x(1)
            run_il([head(0)])
            for ti in range(len(tiles)):
                mid(ti)
                drain_ext()
                tg = tail(ti)
                hg = head(ti + 1) if ti + 1 < len(tiles) else None
                nxt_load = (lambda t2=ti + 2: load_x(t2)) if ti + 2 < len(tiles) else None
                tail_alive = True
                while hg is not None:
                    try:
                        next(hg)
                    except StopIteration:
                        hg = None
                    if tail_alive:
                        try:
                            next(tg)
                        except StopIteration:
                            tail_alive = False
                if tail_alive:
                    ext["g"], ext["done"] = tg, nxt_load
                    if ti + 1 >= len(tiles):
                        drain_ext()
                elif nxt_load is not None:
                    nxt_load()
            drain_ext()
            P.barrier()

        if 1 in phases:
            phase_gdn()
        if 2 in phases:
            phase_ffn(0, hs[0], hs[1], False)
        if 3 in phases:
            phase_rwkv(hs[1], hs[2])
        if 4 in phases:
            phase_ffn(1, hs[2], None, True)

        with nc.Block() as block:
            @block.tensor
            def _(e):
                P.replay("pe", e)

            @block.scalar
            def _(e):
                P.replay("act", e)

            @block.vector
            def _(e):
                P.replay("dve", e)

            @block.gpsimd
            def _(e):
                P.replay("pool", e)

            @block.sync
            def _(e):
                P.replay("sp", e)
    return nc


_NC = None

IN_NAMES_PER_CORE = {
    "x_prompt": lambda a, i: a[i],
    "x_sample": lambda a, i: a[i],
    "cache_gdn_conv": lambda a, i: a[0, i],
    "state_gdn": lambda a, i: a[0, i],
    "cache_rwkv_shift": lambda a, i: a[0, i],
    "state_rwkv": lambda a, i: a[0, i],
}
SQUEEZE0 = ["gdn_w_in", "gdn_conv_w", "gdn_a_log", "gdn_dt_bias", "gdn_o_norm", "gdn_w_out", "rwkv_mu", "rwkv_w0",
            "rwkv_w1", "rwkv_w2", "rwkv_a0", "rwkv_a1", "rwkv_a2", "rwkv_g1", "rwkv_g2", "rwkv_k_k", "rwkv_k_a",
            "rwkv_w_r", "rwkv_w_k", "rwkv_w_v", "rwkv_w_o", "rwkv_ln_w", "rwkv_ln_b"]


def kernel(**inputs):
    global _NC
    if _NC is None:
        _NC = build_program()
    nc = _NC
    f = lambda a: np.ascontiguousarray(np.asarray(a, dtype=np.float32))
    shared = {}
    for k in ["meta_tokens", "norm_mix", "norm_ffn", "norm_final", "ffn_w_gate", "ffn_w_up", "ffn_w_down"]:
        shared[k] = f(inputs[k])
    for k in SQUEEZE0:
        shared[k] = f(np.asarray(inputs[k])[0])
    shared["rwkv_r_k"] = f(np.asarray(inputs["rwkv_r_k"])[0].reshape(D))
    in_maps = []
    for i in range(8):
        m = dict(shared)
        for k, fn in IN_NAMES_PER_CORE.items():
            m[k] = f(fn(np.asarray(inputs[k]), i))
        in_maps.append(m)
    res = run_bass_kernel_spmd(nc, in_maps, core_ids=list(range(8)))
    R = res.results

    def st(name, lead=None):
        a = np.stack([np.asarray(R[i][name], dtype=np.float32) for i in range(8)], axis=0)
        return a if lead is None else a[None]

    return (st("y_prompt"), st("y_sample"),
            st("p_gdn_conv", 1), st("p_gdn_state", 1), st("p_rwkv_shift", 1), st("p_rwkv_state", 1),
            st("s_gdn_conv", 1), st("s_gdn_state", 1), st("s_rwkv_shift", 1), st("s_rwkv_state", 1))
```

```python
import math
from contextlib import ExitStack

import numpy as np
import concourse.bass as bass
import concourse.mybir as mybir
from concourse.bass_utils import run_bass_kernel_spmd

F32 = mybir.dt.float32
BF16 = mybir.dt.bfloat16
AF = mybir.ActivationFunctionType
ALU = mybir.AluOpType

D = 1024
SEQ = 8192
NMETA = 16
TP = SEQ + NMETA
DEC = 64
NROWS = TP + DEC
DFF = 2816
NFC = DFF // 128
GIN = 4112
EPS = 1e-6
GN_EPS = 64e-5
NEG = -1.0e5

ENGS = ("pe", "act", "dve", "pool", "sp")
DBG_A = False
DBG_B = False


class Reg:
    __slots__ = ("name", "w", "r", "psum")

    def __init__(self, name, psum=False):
        self.name = name
        self.w = None
        self.r = []
        self.psum = psum


class Prog:
    def __init__(self, sems):
        self.free_sems = list(sems)
        self.q = {e: [] for e in ENGS}
        self.cnt = {e: 0 for e in ENGS}
        self.esem = {e: self.free_sems.pop() for e in ENGS}
        self.waited = {e: {} for e in ENGS}
        self.dsem = {}

    def _waits(self, eng, deps):
        best = {}
        for (sk, sem, val) in deps:
            if sk == "pe" and eng == "pe":
                continue
            if self.waited[eng].get(sk, 0) >= val:
                continue
            if best.get(sk, (None, 0))[1] < val:
                best[sk] = (sem, val)
        out = []
        for sk, (sem, val) in best.items():
            self.waited[eng][sk] = val
            out.append((sem, val))
        return out

    def op(self, eng, fn, reads=(), writes=(), dma=None, skip_waw=False):
        deps = []
        for b in list(reads) + ([] if skip_waw else list(writes)):
            if b.w is not None:
                deps.append(b.w)
        for b in writes:
            deps.extend(b.r)
        for b in reads:
            if b.psum:
                deps.extend(ev_ for ev_ in b.r if ev_[0] != eng)
        waits = self._waits(eng, deps)
        if dma is None:
            self.cnt[eng] += 1
            ev = (eng, self.esem[eng], self.cnt[eng])
            inc = (self.esem[eng], 1)
        else:
            if dma not in self.dsem:
                self.dsem[dma] = [self.free_sems.pop(), 0]
            ent = self.dsem[dma]
            ent[1] += 16
            ev = ("d:" + dma, ent[0], ent[1])
            inc = (ent[0], 16)
        self.q[eng].append((waits, fn, inc))
        for b in writes:
            b.w = ev
            b.r = []
        for b in reads:
            if b not in writes:
                b.r.append(ev)
        return ev

    def barrier(self):
        evs = [(e, self.esem[e], self.cnt[e]) for e in ENGS if self.cnt[e] > 0]
        evs += [("d:" + k, v[0], v[1]) for k, v in self.dsem.items() if v[1] > 0]
        for e in ENGS:
            ws = []
            for (sk, sem, val) in evs:
                if sk == e:
                    continue
                if self.waited[e].get(sk, 0) >= val:
                    continue
                self.waited[e][sk] = val
                ws.append((sem, val))
            if ws:
                self.q[e].append((ws, None, None))

    def replay(self, eng, e):
        for (waits, fn, inc) in self.q[eng]:
            for (sem, val) in waits:
                e.wait_ge(sem, val)
            if fn is not None:
                ins = fn(e)
                ins.then_inc(inc[0], inc[1])


class Arena:
    def __init__(self, ap, nwords):
        self.ap = ap
        self.n = nwords
        self.off = 0

    def alloc(self, shape, dtype):
        free = 1
        for s in shape[1:]:
            free *= s
        words = free if dtype == F32 else (free + 1) // 2
        words = (words + 7) // 8 * 8
        assert self.off + words <= self.n, f"arena overflow {self.off}+{words}>{self.n}"
        v = self.ap[:, self.off:self.off + words]
        self.off += words
        if dtype != F32:
            v = v.bitcast(dtype)
        v = v[:, 0:free]
        if len(shape) == 3:
            v = v.rearrange("p (a b) -> p a b", a=shape[1], b=shape[2])
        elif len(shape) == 4:
            v = v.rearrange("p (a b c) -> p a b c", a=shape[1], b=shape[2], c=shape[3])
        return v


def build_program(SEQ=SEQ, phases=(1, 2, 3, 4)):
    TP = SEQ + NMETA
    NROWS = TP + DEC
    nc = bass.Bass("TRN2", target_bir_lowering=False)

    def din(name, shape):
        return nc.dram_tensor(name, list(shape), F32, kind="ExternalInput").ap()

    def dout(name, shape):
        return nc.dram_tensor(name, list(shape), F32, kind="ExternalOutput").ap()

    x_prompt = din("x_prompt", [SEQ, D])
    x_sample = din("x_sample", [DEC, D])
    cache_gdn_conv = din("cache_gdn_conv", [3, 3072])
    state_gdn = din("state_gdn", [8, 128, 128])
    cache_rwkv_shift = din("cache_rwkv_shift", [1, D])
    state_rwkv = din("state_rwkv", [16, 64, 64])
    meta_tokens = din("meta_tokens", [NMETA, D])
    norm_mix = din("norm_mix", [2, D])
    norm_ffn = din("norm_ffn", [2, D])
    norm_final = din("norm_final", [D])
    gdn_w_in = din("gdn_w_in", [D, GIN])
    gdn_conv_w = din("gdn_conv_w", [4, 3072])
    gdn_a_log = din("gdn_a_log", [8])
    gdn_dt_bias = din("gdn_dt_bias", [8])
    gdn_o_norm = din("gdn_o_norm", [128])
    gdn_w_out = din("gdn_w_out", [D, D])
    rwkv_mu = din("rwkv_mu", [6, D])
    rwkv_w0 = din("rwkv_w0", [D])
    rwkv_w1 = din("rwkv_w1", [D, 64])
    rwkv_w2 = din("rwkv_w2", [64, D])
    rwkv_a0 = din("rwkv_a0", [D])
    rwkv_a1 = din("rwkv_a1", [D, 64])
    rwkv_a2 = din("rwkv_a2", [64, D])
    rwkv_g1 = din("rwkv_g1", [D, 160])
    rwkv_g2 = din("rwkv_g2", [160, D])
    rwkv_k_k = din("rwkv_k_k", [D])
    rwkv_k_a = din("rwkv_k_a", [D])
    rwkv_r_k = din("rwkv_r_k", [D])
    rwkv_w_r = din("rwkv_w_r", [D, D])
    rwkv_w_k = din("rwkv_w_k", [D, D])
    rwkv_w_v = din("rwkv_w_v", [D, D])
    rwkv_w_o = din("rwkv_w_o", [D, D])
    rwkv_ln_w = din("rwkv_ln_w", [D])
    rwkv_ln_b = din("rwkv_ln_b", [D])
    ffn_w_gate = din("ffn_w_gate", [2, D, DFF])
    ffn_w_up = din("ffn_w_up", [2, D, DFF])
    ffn_w_down = din("ffn_w_down", [2, DFF, D])

    y_prompt = dout("y_prompt", [SEQ, D])
    y_sample = dout("y_sample", [DEC, D])
    o_gdn_conv = {"p": dout("p_gdn_conv", [3, 3072]), "s": dout("s_gdn_conv", [3, 3072])}
    o_gdn_state = {"p": dout("p_gdn_state", [8, 128, 128]), "s": dout("s_gdn_state", [8, 128, 128])}
    o_rwkv_shift = {"p": dout("p_rwkv_shift", [1, D]), "s": dout("s_rwkv_shift", [1, D])}
    o_rwkv_state = {"p": dout("p_rwkv_state", [16, 64, 64]), "s": dout("s_rwkv_state", [16, 64, 64])}

    hs = [nc.dram_tensor(f"hscr{i}", [NROWS, D], F32, kind="Internal").ap() for i in range(3)]

    tiles = [("p", 0, NMETA, ("meta", 0))]
    for k in range(SEQ // 128):
        tiles.append(("p", NMETA + 128 * k, 128, ("x", k)))
    tiles.append(("s", TP, DEC, ("samp", 0)))

    with ExitStack() as es:
        sems = [es.enter_context(nc.semaphore(f"sm{i}")) for i in range(100)]
        P = Prog(sems)
        NW = 53100
        arena_t = es.enter_context(nc.sbuf_tensor("arena", [128, NW], F32))
        A = Arena(arena_t[:], NW)
        psf = [es.enter_context(nc.psum_tensor(f"ps{i}", [128, 512], F32)) for i in range(8)]
        psr = [Reg(f"ps{i}", psum=True) for i in range(8)]
        pcount = [0]

        def bank():
            i = pcount[0] % 8
            pcount[0] += 1
            return psf[i][:], psr[i]

        def bankb(b):
            return b.bitcast(BF16)

        rr = [0]

        def ev2():
            rr[0] += 1
            return "act" if rr[0] % 2 else "dve"

        def mm(out, lhsT, rhs, start, stop, reads, writes):
            P.op("pe", lambda e: e.matmul(out, lhsT, rhs, start=start, stop=stop), reads, writes)

        def tr(out, in_, ident, reads, writes):
            P.op("pe", lambda e: e.transpose(out, in_, ident), reads, writes)

        def act(out, in_, func, reads, writes, bias=None, scale=None, accum_out=None):
            kw = {}
            if bias is not None:
                kw["bias"] = bias
            if scale is not None:
                kw["scale"] = scale
            if accum_out is not None:
                kw["accum_out"] = accum_out
            P.op("act", lambda e: e.activation(out, in_, func, **kw), reads, writes)

        def tt(eng, out, in0, in1, op, reads, writes):
            P.op(eng, lambda e: e.tensor_tensor(out, in0, in1, op), reads, writes)

        def ts(eng, out, in0, s1, op0, reads, writes, s2=None, op1=None):
            if op1 is None:
                P.op(eng, lambda e: e.tensor_scalar(out, in0, s1, None, op0), reads, writes)
            else:
                P.op(eng, lambda e: e.tensor_scalar(out, in0, s1, s2, op0, op1), reads, writes)

        def stt(out, in0, scalar, in1, op0, op1, reads, writes):
            P.op("dve", lambda e: e.scalar_tensor_tensor(out, in0, scalar, in1, op0, op1), reads, writes)

        def cp(eng, out, in_, reads, writes):
            if eng == "act":
                P.op("act", lambda e: e.copy(out, in_), reads, writes)
            else:
                P.op(eng, lambda e: e.tensor_copy(out, in_), reads, writes)

        def recip(out, in_, reads, writes):
            P.op("dve", lambda e: e.reciprocal(out, in_), reads, writes)

        def memset(eng, ap, val, writes):
            P.op(eng, lambda e: e.memset(ap, val), (), writes)

        def dma(eng, out, in_, reads, writes, key, skip_waw=False, **kw):
            P.op(eng, lambda e: e.dma_start(out=out, in_=in_, **kw), reads, writes, dma=key, skip_waw=skip_waw)

        Rc = Reg("consts")
        ident_f = A.alloc([128, 128], F32)
        ident_b = A.alloc([128, 128], BF16)
        ones_f = A.alloc([128, 128], F32)
        ones_b = A.alloc([128, 128], BF16)
        zeros_f = A.alloc([128, 128], F32)
        imask_u = A.alloc([128, 128], F32)
        smask_u = A.alloc([128, 128], F32)
        smask_l = A.alloc([128, 128], F32)
        bdmask = A.alloc([128, 128], F32)
        offmask = A.alloc([128, 128], F32)
        negmask = A.alloc([128, 128], F32)
        eps_t = A.alloc([128, 4], F32)
        memset("pool", ones_f, 1.0, [Rc])
        memset("pool", zeros_f, 0.0, [Rc])
        memset("pool", eps_t[:, 0:1], EPS, [Rc])
        memset("pool", eps_t[:, 1:2], 1.0, [Rc])
        memset("pool", eps_t[:, 2:3], GN_EPS, [Rc])
        memset("pool", eps_t[:, 3:4], 0.0, [Rc])

        def asel(out, in_, cmp_op, fill, base, cm, step):
            P.op("pool", lambda e: e.affine_select(out, in_, [[step, 128]], cmp_op, fill,
                                                   base=base, channel_multiplier=cm), [Rc], [Rc])

        asel(imask_u, ones_f, ALU.is_ge, 0.0, 0, -1, 1)
        asel(smask_u, ones_f, ALU.is_gt, 0.0, 0, -1, 1)
        asel(smask_l, ones_f, ALU.is_gt, 0.0, 0, 1, -1)
        asel(ident_f, ones_f, ALU.is_equal, 0.0, 0, -1, 1)
        asel(negmask, zeros_f, ALU.is_ge, NEG, 0, -1, 1)
        cp("pool", bdmask, smask_u, [Rc], [Rc])
        memset("pool", bdmask[0:64, 64:128], 0.0, [Rc])
        memset("pool", offmask, 0.0, [Rc])
        memset("pool", offmask[0:64, 64:128], 1.0, [Rc])
        cp("pool", ident_b, ident_f, [Rc], [Rc])
        cp("pool", ones_b, ones_f, [Rc], [Rc])
        persist_off = A.off

        def load_w(dst, src, K, key, reg):
            for kc in range(K):
                dma("pool", dst[:, kc, :], src[kc * 128:(kc + 1) * 128, :], [], [reg], key,
                    skip_waw=(kc > 0), max_dma_last_dim=4096)

        def load_featmajor(dst, src_vec, C, stage, stage_reg, dst_reg, key):
            dma("sp", stage[0:C, 0:128], src_vec.rearrange("(c p) -> c p", p=128), [], [stage_reg], key)
            b, br = bank()
            tr(b[:, 0:C], stage[0:C, 0:128], ident_f[0:C, 0:C], [stage_reg, Rc], [br])
            cp("dve", dst, b[:, 0:C], [br], [dst_reg])

        def rmsnorm_T(xt, Rx, n, gfm, Rg, hn, Rhn, hnT, RhnT, small, Rsm, junk, Rjunk):
            act(junk[:n, :], xt[:n, :], AF.Square, [Rx], [Rjunk, Rsm], accum_out=small[:n, 0:1])
            act(small[:n, 1:2], small[:n, 0:1], AF.Ln, [Rsm, Rc], [Rsm], bias=eps_t[:n, 0:1], scale=1.0 / D)
            act(small[:n, 2:3], small[:n, 1:2], AF.Exp, [Rsm], [Rsm], scale=-0.5)
            ts("dve", hn[:n, :], xt[:n, :], small[:n, 2:3], ALU.mult, [Rx, Rsm], [Rhn])
            b, br = bank()
            bb = bankb(b).rearrange("p (a b) -> p a b", a=8, b=128)
            for kc in range(8):
                tr(bb[:, kc, :n], hn[:n, kc * 128:(kc + 1) * 128], ident_b[:n, :n], [Rhn, Rc], [br])
            for kc in range(8):
                eng = ev2()
                if eng == "act":
                    act(hnT[:, kc, :n], bb[:, kc, :n], AF.Copy, [br, Rg], [RhnT], scale=gfm[:, kc:kc + 1])
                else:
                    ts("dve", hnT[:, kc, :n], bb[:, kc, :n], gfm[:, kc:kc + 1], ALU.mult, [br, Rg], [RhnT])

        def neumann_gen(Bsrc, BTsrc, RB, RBT, n, nf, PR, RPR, PT, RPT, G, res, psum_acc=False):
            cur = 0
            tt("pool", PR[0][:n, :, 1, :n], Bsrc, ident_f[:n, :n].unsqueeze(1).to_broadcast([n, G, n]), ALU.add,
               [RB, Rc], [RPR[0]])
            Pk = Bsrc
            PTk = BTsrc
            RPk, RPTk = RB, RBT
            Rk = PR[0][:n, :, 1, :n]
            RRk = RPR[0]
            for k in range(0, nf):
                last = (k == nf - 1)
                if k == 0:
                    if nf == 1:
                        break
                    nxt = 1 - cur
                    b1, b1r = bank()
                    b2, b2r = bank()
                    v1 = b1.rearrange("p (g c) -> p g c", g=4, c=128)
                    v2 = b2.rearrange("p (g c) -> p g c", g=4, c=128)
                    for hh in range(G):
                        mm(v1[:n, hh, :n], PTk[:, hh, :], Pk[:, hh, :], True, True, [RPk, RPTk], [b1r])
                        mm(v2[:n, hh, :n], Pk[:, hh, :], PTk[:, hh, :], True, True, [RPk, RPTk], [b2r])
                    cp("act", PR[nxt][:n, :, 0, :n], v1[:n, 0:G, :n], [b1r], [RPR[nxt]])
                    cp("dve", PT[nxt][:n, :, :n], v2[:n, 0:G, :n], [b2r], [RPT[nxt]])
                    cp("pool", PR[nxt][:n, :, 1, :n], Rk, [RRk], [RPR[nxt]])
                    cur = nxt
                    Pk = PR[cur][:n, :, 0, :n]
                    PTk = PT[cur][:n, :, :n]
                    RPk, RPTk = RPR[cur], RPT[cur]
                    Rk = PR[cur][:n, :, 1, :n]
                    RRk = RPR[cur]
                    yield
                    continue
                nxt = 1 - cur
                if not last:
                    ba, bar = bank()
                    bb_, bbr = bank()
                    bc, bcr = bank()
                    va = ba.rearrange("p (g t c) -> p g t c", g=2, t=2, c=128)
                    vb = bb_.rearrange("p (g t c) -> p g t c", g=2, t=2, c=128)
                    vc = bc.rearrange("p (g c) -> p g c", g=4, c=128)
                    for hh in range(G):
                        tgt, tgr = (va, bar) if hh < 2 else (vb, bbr)
                        if n == 128:
                            mm(tgt[:n, hh % 2, :, :n], PTk[:, hh, :], PR[cur][:n, hh, :, :n], True, not psum_acc,
                               [RPk, RPTk, RRk], [tgr])
                        else:
                            for t_ in range(2):
                                mm(tgt[:n, hh % 2, t_, :n], PTk[:, hh, :], PR[cur][:n, hh, t_, :n], True,
                                   not (psum_acc and t_ == 1), [RPk, RPTk, RRk], [tgr])
                        if psum_acc:
                            mm(tgt[:n, hh % 2, 1, :n], ident_b[:n, :n], PR[cur][:n, hh, 1, :n], False, True,
                               [Rc, RRk], [tgr])
                        mm(vc[:n, hh, :n], Pk[:, hh, :], PTk[:, hh, :], True, True, [RPk, RPTk], [bcr])
                    for (vv, vr, h0) in ((va, bar, 0), (vb, bbr, 2)):
                        gcount = min(2, G - h0)
                        if gcount <= 0:
                            continue
                        cp("act", PR[nxt][:n, h0:h0 + gcount, 0, :n], vv[:n, 0:gcount, 0, :n], [vr], [RPR[nxt]])
                        if psum_acc:
                            cp("act", PR[nxt][:n, h0:h0 + gcount, 1, :n], vv[:n, 0:gcount, 1, :n], [vr], [RPR[nxt]])
                        else:
                            tt("dve", PR[nxt][:n, h0:h0 + gcount, 1, :n], vv[:n, 0:gcount, 1, :n],
                               PR[cur][:n, h0:h0 + gcount, 1, :n], ALU.add, [vr, RPR[cur]], [RPR[nxt]])
                    cp("act", PT[nxt][:n, :, :n], vc[:n, 0:G, :n], [bcr], [RPT[nxt]])
                else:
                    ba, bar = bank()
                    va = ba.rearrange("p (g c) -> p g c", g=4, c=128)
                    for hh in range(G):
                        mm(va[:n, hh, :n], PTk[:, hh, :], PR[cur][:n, hh, 1, :n], True, not psum_acc, [RPTk, RRk], [bar])
                        if psum_acc:
                            mm(va[:n, hh, :n], ident_b[:n, :n], PR[cur][:n, hh, 1, :n], False, True, [Rc, RRk], [bar])
                    if psum_acc:
                        cp("act", PR[nxt][:n, :, 1, :n], va[:n, 0:G, :n], [bar], [RPR[nxt]])
                    else:
                        tt("dve", PR[nxt][:n, :, 1, :n], va[:n, 0:G, :n], PR[cur][:n, :, 1, :n], ALU.add,
                           [bar, RPR[cur]], [RPR[nxt]])
                cur = nxt
                Pk = PR[cur][:n, :, 0, :n]
                PTk = PT[cur][:n, :, :n]
                RPk, RPTk = RPR[cur], RPT[cur]
                Rk = PR[cur][:n, :, 1, :n]
                RRk = RPR[cur]
                yield
            res.append((Rk, RRk))

        def neumann(Bsrc, BTsrc, RB, RBT, n, nf, PR, RPR, PT, RPT, G):
            res = []
            for _ in neumann_gen(Bsrc, BTsrc, RB, RBT, n, nf, PR, RPR, PT, RPT, G, res):
                pass
            return res[0]

        def phase_gdn():
            A.off = persist_off
            Rw = Reg("w_in")
            Rwo = Reg("w_out")
            w_in = A.alloc([128, 8, GIN], BF16)
            w_out = A.alloc([128, 8, D], BF16)
            diag = A.alloc([128, 96, 128], BF16)
            Rdiag = Reg("diag")
            load_w(w_in, gdn_w_in, 8, "w_in", Rw)
            load_w(w_out, gdn_w_out, 8, "w_out", Rwo)

            stage = A.alloc([128, 128], F32)
            Rst = Reg("stage")
            gfm = A.alloc([128, 8], F32)
            Rg = Reg("gfm")
            load_featmajor(gfm, norm_mix[0], 8, stage, Rst, Rg, "st")
            cw = A.alloc([128, 96], F32)
            Rcw = Reg("cw")
            dma("sp", stage[0:96, :], gdn_conv_w.rearrange("t (c p) -> (t c) p", p=128), [], [Rst], "st")
            b, br = bank()
            tr(b[:, 0:96], stage[0:96, :], ident_f[0:96, 0:96], [Rst, Rc], [br])
            cp("dve", cw, b[:, 0:96], [br], [Rcw])
            for idx in range(96):
                ts("pool" if idx % 2 else "dve", diag[:, idx, :], ident_f, cw[:, idx:idx + 1],
                   ALU.mult, [Rc, Rcw], [Rdiag])
            prm = A.alloc([128, 32], F32)
            Rprm = Reg("prm")
            dma("sp", prm[:, 0:8], gdn_a_log.partition_broadcast(128), [], [Rprm], "prm")
            dma("sp", prm[:, 8:16], gdn_dt_bias.partition_broadcast(128), [], [Rprm], "prm2")
            act(prm[:, 0:8], prm[:, 0:8], AF.Exp, [Rprm], [Rprm])
            ts("dve", prm[:, 0:8], prm[:, 0:8], -1.0, ALU.mult, [Rprm], [Rprm])
            onb = A.alloc([128, 128], F32)
            Ronb = Reg("onb")
            dma("sp", onb, gdn_o_norm.partition_broadcast(128), [], [Ronb], "onb")

            xt = [A.alloc([128, D], F32) for _ in range(2)]
            Rxt = [Reg("xt0"), Reg("xt1")]
            junk = A.alloc([128, D], BF16)
            Rjunk = Reg("junk")
            small = A.alloc([128, 8], F32)
            Rsm = Reg("small")
            hn = A.alloc([128, D], BF16)
            Rhn = Reg("hn")
            hnT = A.alloc([128, 8, 128], BF16)
            RhnT = Reg("hnT")
            yg = A.alloc([128, D], BF16)
            Ryg = Reg("yg")
            ygT = A.alloc([128, 8, 128], BF16)
            RygT = Reg("ygT")
            pre = A.alloc([128, 24, 132], BF16)
            Rpre = [Reg(f"pre{i}") for i in range(6)]
            nbT = A.alloc([128, 3, 24], F32)
            RnbT = Reg("nbT")
            tmpc = [A.alloc([128, 4, 128], F32) for _ in range(2)]
            Rtmpc = [Reg("tmpc0"), Reg("tmpc1")]
            sq = [A.alloc([128, 4, 128], BF16)]
            Rsq = [Reg("sq0")]
            tmp2 = [A.alloc([128, 4, 128], F32) for _ in range(2)]
            Rtmp2 = [Reg("tmp20"), Reg("tmp21")]
            qkn = A.alloc([128, 16, 128], BF16)
            Rqkn = [Reg(f"qkn{i}") for i in range(4)]
            vT = A.alloc([128, 8, 128], BF16)
            RvT = [Reg("vT0"), Reg("vT1")]
            kG = A.alloc([128, 8, 128], BF16)
            kd = A.alloc([128, 8, 128], BF16)
            vtok = A.alloc([128, 8, 128], BF16)
            Rktok = [Reg("ktok0"), Reg("ktok1")]
            zss = [A.alloc([128, D], BF16) for _ in range(2)]
            Rzss = [Reg("zs0"), Reg("zs1")]
            gs = A.alloc([128, 96], F32)
            Rgs = Reg("gs")
            Rgs2 = Reg("gs2")
            ET = A.alloc([128, 8, 128], F32)
            RET = [Reg(f"ET{i}") for i in range(4)]
            qgT = A.alloc([128, 8, 128], BF16)
            RqgT = [Reg(f"qgT{i}") for i in range(4)]
            qkT = A.alloc([128, 8, 128], BF16)
            RqkT = [Reg(f"qkT{i}") for i in range(4)]
            NSETS = 2
            NACT = NSETS
            SETS = []
            for si in range(NSETS):
                PRs = [A.alloc([128, 2, 2, 128], F32) for _ in range(2)]
                PTs = [A.alloc([128, 2, 128], F32) for _ in range(2)]
                SETS.append(dict(PR=PRs, RPR=[Reg(f"gPR{si}a"), Reg(f"gPR{si}b")], PT=PTs,
                                 RPT=[Reg(f"gPT{si}a"), Reg(f"gPT{si}b")],
                                 B0=A.alloc([128, 2, 128], F32), RB0=Reg(f"gB0{si}"),
                                 B0T=A.alloc([128, 2, 128], F32), RB0T=Reg(f"gB0T{si}"),
                                 tmpA=A.alloc([128, 2, 128], F32), RtmpA=Reg(f"gtmpA{si}"),
                                 OffT=A.alloc([128, 2, 128], F32), ROff=Reg(f"gOff{si}"),
                                 tmpE=A.alloc([128, 2, 128], F32), RtmpE=Reg(f"gtmpE{si}")))
            MinvT = A.alloc([128, 8, 128], BF16)
            RMinv = [Reg(f"Minv{i}") for i in range(4)]
            nsolw = A.alloc([128, 8, 128], BF16)
            Rnsolw = [Reg(f"nsolw{i}") for i in range(4)]
            vnew = A.alloc([128, 8, 128], BF16)
            Rvnew = [Reg(f"vnew{i}") for i in range(4)]
            S = A.alloc([128, 8, 128], F32)
            RS = [Reg(f"S{i}") for i in range(4)]
            Sb = A.alloc([128, 8, 128], BF16)
            RSb = [Reg(f"Sb{i}") for i in range(4)]
            o = A.alloc([128, D], F32)
            Ro = [Reg(f"o{i}") for i in range(4)]

            def load_x(ti):
                seq, row0, n, (kind, k) = tiles[ti]
                slot = ti % 2
                if kind == "meta":
                    src = meta_tokens
                elif kind == "x":
                    src = x_prompt[k * 128:(k + 1) * 128, :]
                else:
                    src = x_sample
                dma("sp", xt[slot][:n, :], src, [], [Rxt[slot]], f"xt{slot}")

            ext = {"g": None, "done": None}

            def step_ext():
                if ext["g"] is not None:
                    try:
                        next(ext["g"])
                    except StopIteration:
                        ext["g"] = None
                        if ext["done"] is not None:
                            ext["done"]()
                            ext["done"] = None

            def drain_ext():
                while ext["g"] is not None:
                    step_ext()

            def head(ti):
                seq, row0, n, (kind, k) = tiles[ti]
                slot = ti % 2
                x_ = xt[slot]
                Rx = Rxt[slot]
                first = (ti == 0) or (kind == "samp")
                lastt = (ti == len(tiles) - 2) or (kind == "samp")
                prev_n = tiles[ti - 1][2] if ti > 0 else None
                zs, Rzs = zss[ti % 2], Rzss[ti % 2]
                if first and seq == "p":
                    memset("pool", S, 0.0, RS)
                    memset("pool", Sb, 0.0, RSb)
                    memset("pool", pre[:, :, 0:3], 0.0, Rpre)
                elif first:
                    dma("sp", S, state_gdn.rearrange("h d e -> d h e"), [], RS, "Sld")
                    cp("pool", Sb, S, RS, RSb)
                    if DBG_B:
                        memset("pool", pre[:, :, 0:3], 0.0, Rpre)
                    else:
                        dma("sp", stage[0:72, :], cache_gdn_conv.rearrange("r (c p) -> (r c) p", p=128), [], [Rst], "st")
                        b, br = bank()
                        mm(b[:, 0:72], stage[0:72, :], ident_f[0:72, 0:72], True, True, [Rst, Rc], [br])
                        cp("dve", pre[:, :, 0:3], b[:, 0:72].rearrange("p (r c) -> p c r", r=3, c=24), [br], Rpre)
                else:
                    cp("pool", pre[:, :, 0:3], pre[:, :, prev_n:prev_n + 3], Rpre, Rpre)

                yield
                rmsnorm_T(x_, Rx, n, gfm, Rg, hn, Rhn, hnT, RhnT, small, Rsm, junk, Rjunk)

                yield
                for g4 in range(6):
                    b, br = bank()
                    bv = b.rearrange("p (a b) -> p a b", a=4, b=128)
                    for j in range(4):
                        c = g4 * 4 + j
                        for kc in range(8):
                            mm(bv[:, j, :n], w_in[:, kc, c * 128:(c + 1) * 128], hnT[:, kc, :n], kc == 0, kc == 7,
                               [Rw, RhnT], [br])
                    cp(ev2(), pre[:, g4 * 4:(g4 + 1) * 4, 3:3 + n], bv[:, :, :n], [br], [Rpre[g4]])
                    if lastt and not DBG_A:
                        cp("dve", nbT.rearrange("p r c -> p c r")[:, g4 * 4:(g4 + 1) * 4, :], bv[:, :, n - 3:n], [br], [RnbT])
                    yield
                if lastt and not DBG_A:
                    b, br = bank()
                    mm(b[0:72, 0:128], nbT.rearrange("p r c -> p (r c)"), ident_f, True, True, [RnbT, Rc], [br])
                    cp("dve", stage[0:72, :], b[0:72, 0:128], [br], [Rst])
                    dma("sp", o_gdn_conv[seq].rearrange("r (c p) -> (r c) p", p=128), stage[0:72, :], [Rst], [], "nb3" + seq)
                for s in range(2):
                    b, br = bank()
                    for kc in range(8):
                        mm(b[:n, :], hnT[:, kc, :n], w_in[:, kc, 3072 + s * 512:3072 + (s + 1) * 512], kc == 0, kc == 7,
                           [Rw, RhnT], [br])
                    act(zs[:n, s * 512:(s + 1) * 512], b[:n, :], AF.Silu, [br], [Rzs])
                    yield
                b, br = bank()
                for kc in range(8):
                    mm(b[:n, 0:16], hnT[:, kc, :n], w_in[:, kc, 4096:4112], kc == 0, kc == 7, [Rw, RhnT], [br])
                act(gs[:n, 0:8], b[:n, 8:16], AF.Sigmoid, [br], [Rgs])
                ts("dve", gs[:n, 8:16], gs[:n, 0:8], -1.0, ALU.mult, [Rgs], [Rgs])
                tt("dve", gs[:n, 16:24], b[:n, 0:8], prm[:n, 8:16], ALU.add, [br, Rprm], [Rgs])
                yield
                stt(gs[:n, 24:32], gs[:n, 16:24], -1.0, gs[:n, 16:24], ALU.mult, ALU.max, [Rgs], [Rgs])
                act(gs[:n, 24:32], gs[:n, 24:32], AF.Exp, [Rgs], [Rgs], scale=-1.0)
                act(gs[:n, 24:32], gs[:n, 24:32], AF.Ln, [Rgs, Rc], [Rgs], bias=eps_t[:n, 1:2])
                stt(gs[:n, 16:24], gs[:n, 16:24], 0.0, gs[:n, 24:32], ALU.max, ALU.add, [Rgs], [Rgs])
                tt("dve", gs[:n, 16:24], gs[:n, 16:24], prm[:n, 0:8], ALU.mult, [Rgs, Rprm], [Rgs])

                yield
                b, br = bank()
                mm(b[:n, 0:8], imask_u[:n, :n], gs[:n, 16:24], True, True, [Rc, Rgs], [br])
                mm(b[:, 8:16], ones_f[:n, :], gs[:n, 16:24], True, True, [Rc, Rgs], [br])
                cp("dve", gs[:n, 32:40], b[:n, 0:8], [br], [Rgs])
                act(gs[:n, 40:48], b[:n, 0:8], AF.Exp, [br], [Rgs])
                cp("dve", gs[:, 48:56], b[:, 8:16], [br], [Rgs])
                act(gs[:, 56:64], b[:, 8:16], AF.Exp, [br], [Rgs])
                tt("dve", gs[:n, 64:72], gs[:n, 48:56], gs[:n, 32:40], ALU.subtract, [Rgs], [Rgs])
                act(gs[:n, 64:72], gs[:n, 64:72], AF.Exp, [Rgs], [Rgs])


                yield

            def mid(ti):
                seq, row0, n, (kind, k) = tiles[ti]
                slot = ti % 2
                x_ = xt[slot]
                Rx = Rxt[slot]
                first = (ti == 0) or (kind == "samp")
                lastt = (ti == len(tiles) - 2) or (kind == "samp")
                prev_n = tiles[ti - 1][2] if ti > 0 else None
                zs, Rzs = zss[ti % 2], Rzss[ti % 2]
                nf = int(math.log2(min(n, 64)))

                def cd_gen():
                    for half in range(2):
                        bvs = []
                        for g4 in (half, 2 + half, 4 + half):
                            b, br = bank()
                            bv = b.rearrange("p (a b) -> p a b", a=4, b=128)
                            for j in range(4):
                                c = g4 * 4 + j
                                for tap in range(4):
                                    mm(bv[:, j, :n], diag[:, tap * 24 + c, :], pre[:, c, tap:tap + n], tap == 0, tap == 3,
                                       [Rdiag, Rpre[g4]], [br])
                            bvs.append((bv, br))
                        act(tmpc[0][:, :, :n], bvs[0][0][:, :, :n], AF.Silu, [bvs[0][1]], [Rtmpc[0]])
                        act(tmpc[1][:, :, :n], bvs[1][0][:, :, :n], AF.Silu, [bvs[1][1]], [Rtmpc[1]])
                        act(vT[:, half * 4:(half + 1) * 4, :n], bvs[2][0][:, :, :n], AF.Silu, [bvs[2][1]], [RvT[half]])
                        yield None
                        for qk in range(2):
                            g4 = half + 2 * qk
                            tt("pool", sq[0][:, :, :n], tmpc[qk][:, :, :n], tmpc[qk][:, :, :n], ALU.mult,
                               [Rtmpc[qk]], [Rsq[0]])
                            b2, b2r = bank()
                            b2v = b2.rearrange("p (a b) -> p a b", a=4, b=128)
                            if n == 128:
                                mm(b2, ones_b, sq[0].rearrange("p a b -> p (a b)"), True, True, [Rc, Rsq[0]], [b2r])
                            else:
                                for j in range(4):
                                    mm(b2v[:, j, :n], ones_b, sq[0][:, j, :n], True, True, [Rc, Rsq[0]], [b2r])
                            act(tmp2[qk][:, :, :n], b2v[:, :, :n], AF.Ln, [b2r, Rc], [Rtmp2[qk]], bias=eps_t[:, 0:1])
                            act(tmp2[qk][:, :, :n], tmp2[qk][:, :, :n], AF.Exp, [Rtmp2[qk]], [Rtmp2[qk]], scale=-0.5)
                            if qk == 0:
                                stt(qkn[:, g4 * 4:(g4 + 1) * 4, :n], tmpc[0][:, :, :n], 128.0 ** -0.5, tmp2[0][:, :, :n],
                                    ALU.mult, ALU.mult, [Rtmpc[0], Rtmp2[0]], [Rqkn[g4]])
                            else:
                                tt("pool", qkn[:, g4 * 4:(g4 + 1) * 4, :n], tmpc[1][:, :, :n], tmp2[1][:, :, :n], ALU.mult,
                                   [Rtmpc[1], Rtmp2[1]], [Rqkn[g4]])
                            yield None
                        hs_ = slice(half * 4, half * 4 + 4)
                        b, br = bank()
                        bb = bankb(b).rearrange("p (a b) -> p a b", a=8, b=128)
                        for hh in range(4):
                            tr(bb[:n, hh, :], qkn[:, 8 + half * 4 + hh, :n], ident_b, [Rqkn[2 + half], Rc], [br])
                        bV2, bV2r = bank()
                        bbv = bankb(bV2).rearrange("p (a b) -> p a b", a=8, b=128)
                        for hh in range(4):
                            tr(bbv[:n, hh, :], vT[:, half * 4 + hh, :n], ident_b, [RvT[half], Rc], [bV2r])
                        tt("dve", kG[:n, hs_, :], bb[:n, 0:4, :], gs[:n, 40 + half * 4:44 + half * 4].unsqueeze(2).to_broadcast([n, 4, 128]),
                           ALU.mult, [br, Rgs], [Rktok[half]])
                        tt("dve", kd[:n, hs_, :], bb[:n, 0:4, :], gs[:n, 64 + half * 4:68 + half * 4].unsqueeze(2).to_broadcast([n, 4, 128]),
                           ALU.mult, [br, Rgs], [Rktok[half]])
                        cp("act", vtok[:n, hs_, :], bbv[:n, 0:4, :], [bV2r], [Rktok[half]])
                        yield half

                def f_gen(p, S_):
                    h0 = p * 2
                    half = p // 2
                    PR, RPR, PT, RPT = S_["PR"], S_["RPR"], S_["PT"], S_["RPT"]
                    B0, RB0, B0T, RB0T = S_["B0"], S_["RB0"], S_["B0T"], S_["RB0T"]
                    tmpA, RtmpA, OffT, ROff, tmpE, RtmpE = S_["tmpA"], S_["RtmpA"], S_["OffT"], S_["ROff"], S_["tmpE"], S_["RtmpE"]
                    Rq, Rk_ = Rqkn[half], Rqkn[2 + half]
                    bG, bGr = bank()
                    vG = bG.rearrange("p (g c) -> p g c", g=4, c=128)
                    for hh in range(2):
                        h = h0 + hh
                        mm(vG[:, hh, :n], gs[:n, 16 + h:17 + h].to_broadcast([n, 128]), imask_u[:n, :n], True, True,
                           [Rgs, Rc], [bGr])
                    for hh in range(2):
                        h = h0 + hh
                        stt(ET[:n, h, :n], vG[:n, hh, :n], gs[:n, 32 + h:33 + h], negmask[:n, :n], ALU.subtract, ALU.add,
                            [bGr, Rgs, Rc], [RET[p]])
                    act(ET[:n, h0:h0 + 2, :n], ET[:n, h0:h0 + 2, :n], AF.Exp, [RET[p]], [RET[p]])
                    act(tmpE[:, :, :n], vG[:, 0:2, :n], AF.Exp, [bGr], [RtmpE])
                    tt("pool", qgT[:, h0:h0 + 2, :n], qkn[:, h0:h0 + 2, :n], tmpE[:, :, :n], ALU.mult, [Rq, RtmpE], [RqgT[p]])
                    yield
                    bK, bKr = bank()
                    vK = bK.rearrange("p (g c) -> p g c", g=4, c=128)
                    for hh in range(2):
                        h = h0 + hh
                        mm(vK[:n, hh, :n], qkn[:, 8 + h, :n], qkn[:, 8 + h, :n], True, True, [Rk_], [bKr])
                        mm(vK[:n, 2 + hh, :n], qkn[:, 8 + h, :n], qkn[:, h, :n], True, True, [Rk_, Rq], [bKr])
                    for hh in range(2):
                        h = h0 + hh
                        stt(tmpA[:n, hh, :n], vK[:n, hh, :n], gs[:n, 8 + h:9 + h], ET[:n, h, :n], ALU.mult, ALU.mult,
                            [bKr, Rgs, RET[p]], [RtmpA])
                    tt("dve", qkT[:n, h0:h0 + 2, :n], vK[:n, 2:4, :n], ET[:n, h0:h0 + 2, :n], ALU.mult, [bKr, RET[p]], [RqkT[p]])
                    tt("pool", B0[:n, :, :n], tmpA[:n, :, :n], bdmask[:n, :n].unsqueeze(1).to_broadcast([n, 2, n]), ALU.mult,
                       [RtmpA, Rc], [RB0])
                    if n == 128:
                        tt("pool", OffT[:n, :, :n], tmpA[:n, :, :n], offmask[:n, :n].unsqueeze(1).to_broadcast([n, 2, n]), ALU.mult,
                           [RtmpA, Rc], [ROff])
                    yield
                    bT, bTr = bank()
                    vT_ = bT.rearrange("p (g c) -> p g c", g=4, c=128)
                    for hh in range(2):
                        mm(vT_[:n, hh, :n], B0[:n, hh, :n], ident_f[:n, :n], True, True, [RB0, Rc], [bTr])
                    cp("act", B0T[:n, :, :n], vT_[:n, 0:2, :n], [bTr], [RB0T])
                    yield
                    res = []
                    for _ in neumann_gen(B0[:n, :, :n], B0T[:n, :, :n], RB0, RB0T, n, nf, PR, RPR, PT, RPT, 2, res):
                        yield
                    Rfin, RRfin = res[0]
                    if n == 128:
                        Dm, RDm = PT[0], RPT[0]
                        T1, RT1 = PT[1], RPT[1]
                        T1T, RT1T = tmpA, RtmpA
                        b1, b1r = bank()
                        v1 = b1.rearrange("p (g c) -> p g c", g=4, c=128)
                        for hh in range(2):
                            mm(v1[:, hh, :], Rfin[:, hh, :], ident_f, True, True, [RRfin, Rc], [b1r])
                        cp("act", Dm, v1[:, 0:2, :], [b1r], [RDm])
                        yield
                        b2, b2r = bank()
                        v2 = b2.rearrange("p (g c) -> p g c", g=4, c=128)
                        for hh in range(2):
                            mm(v2[:, hh, :], Dm[:, hh, :], OffT[:, hh, :], True, True, [RDm, ROff], [b2r])
                        cp("dve", T1, v2[:, 0:2, :], [b2r], [RT1])
                        yield
                        b3, b3r = bank()
                        v3 = b3.rearrange("p (g c) -> p g c", g=4, c=128)
                        for hh in range(2):
                            mm(v3[:, hh, :], T1[:, hh, :], ident_f, True, True, [RT1, Rc], [b3r])
                        cp("act", T1T, v3[:, 0:2, :], [b3r], [RT1T])
                        yield
                        b4, b4r = bank()
                        v4 = b4.rearrange("p (g c) -> p g c", g=4, c=128)
                        for hh in range(2):
                            mm(v4[:, hh, :], T1T[:, hh, :], Rfin[:, hh, :], True, True, [RT1T, RRfin], [b4r])
                        tt("dve", MinvT[:, h0:h0 + 2, :], v4[:, 0:2, :], Rfin, ALU.add, [b4r, RRfin], [RMinv[p]])
                    else:
                        cp("dve", MinvT[:n, h0:h0 + 2, :n], Rfin, [RRfin], [RMinv[p]])
                    yield
                    bW, bWr = bank()
                    vW = bW.rearrange("p (g c) -> p g c", g=4, c=128)
                    for hh in range(2):
                        h = h0 + hh
                        mm(vW[:, hh, :n], kG[:n, h, :], MinvT[:n, h, :n], True, True, [Rktok[half], RMinv[p]], [bWr])
                    ts("dve", nsolw[:, h0:h0 + 2, :n], vW[:, 0:2, :n], -1.0, ALU.mult, [bWr], [Rnsolw[p]])
                    yield
                    bV, bVr = bank()
                    vV = bV.rearrange("p (g c) -> p g c", g=4, c=128)
                    for hh in range(2):
                        h = h0 + hh
                        mm(vV[:n, hh, :], MinvT[:n, h, :n], vtok[:n, h, :], True, False, [RMinv[p], Rktok[half]], [bVr])
                        mm(vV[:n, hh, :], nsolw[:, h, :n], Sb[:, h, :], False, True, [Rnsolw[p], RSb[p]], [bVr])
                    tt("dve", vnew[:n, h0:h0 + 2, :], vV[:n, 0:2, :],
                       gs[:n, h0:h0 + 2].unsqueeze(2).to_broadcast([n, 2, 128]), ALU.mult, [bVr, Rgs], [Rvnew[p]])
                    yield
                    drain_ext()
                    bO, bOr = bank()
                    vO = bO.rearrange("p (g c) -> p g c", g=4, c=128)
                    for hh in range(2):
                        h = h0 + hh
                        mm(vO[:n, hh, :], qgT[:, h, :n], Sb[:, h, :], True, False, [RqgT[p], RSb[p]], [bOr])
                        mm(vO[:n, hh, :], qkT[:n, h, :n], vnew[:n, h, :], False, True, [RqkT[p], Rvnew[p]], [bOr])
                    cp("act", o[:n, h0 * 128:(h0 + 2) * 128], bO[:n, 0:256], [bOr], [Ro[p]])
                    bS, bSr = bank()
                    vS = bS.rearrange("p (g c) -> p g c", g=4, c=128)
                    for hh in range(2):
                        h = h0 + hh
                        mm(vS[:, hh, :], kd[:n, h, :], vnew[:n, h, :], True, True, [Rktok[half], Rvnew[p]], [bSr])
                    for hh in range(2):
                        h = h0 + hh
                        stt(S[:, h, :], S[:, h, :], gs[:, 56 + h:57 + h], vS[:, hh, :], ALU.mult, ALU.add,
                            [RS[p], Rgs, bSr], [RS[p]])
                    cp("pool", Sb[:, h0:h0 + 2, :], S[:, h0:h0 + 2, :], [RS[p]], [RSb[p]])

                cg = cd_gen()
                ready = set()
                next_p = 0
                active = []
                cd_alive = True
                while cd_alive or active or next_p < 4:
                    if cd_alive:
                        try:
                            r = next(cg)
                            if r is not None:
                                ready.add(r)
                        except StopIteration:
                            cd_alive = False
                    while next_p < 4 and (next_p // 2) in ready and len(active) < NACT and (not DBG_B or not cd_alive):
                        active.append(f_gen(next_p, SETS[next_p % NSETS]))
                        next_p += 1
                    for _rep in range(2):
                        for g in list(active):
                            try:
                                next(g)
                            except StopIteration:
                                active.remove(g)
                    step_ext()
                drain_ext()


                if lastt:
                    dma("sp", o_gdn_state[seq].rearrange("h d e -> d h e"), S, RS, [], "Sst" + seq)

            def tail(ti):
                seq, row0, n, (kind, k) = tiles[ti]
                slot = ti % 2
                x_ = xt[slot]
                Rx = Rxt[slot]
                first = (ti == 0) or (kind == "samp")
                lastt = (ti == len(tiles) - 2) or (kind == "samp")
                prev_n = tiles[ti - 1][2] if ti > 0 else None
                zs, Rzs = zss[ti % 2], Rzss[ti % 2]
                for h in range(8):
                    act(junk[:n, h * 128:(h + 1) * 128], o[:n, h * 128:(h + 1) * 128], AF.Square, Ro, [Rjunk, Rgs2],
                        accum_out=gs[:n, 72 + h:73 + h])
                yield
                act(gs[:n, 80:88], gs[:n, 72:80], AF.Ln, [Rgs2, Rc], [Rgs2], bias=eps_t[:n, 0:1], scale=1.0 / 128)
                act(gs[:n, 80:88], gs[:n, 80:88], AF.Exp, [Rgs2], [Rgs2], scale=-0.5)
                ov = o.rearrange("p (a b) -> p a b", a=8, b=128)
                tt("dve", ov[:n], ov[:n], gs[:n, 80:88].unsqueeze(2).to_broadcast([n, 8, 128]), ALU.mult, Ro + [Rgs2], Ro)
                yield
                tt("pool", ov[:n], ov[:n], onb[:n, :].unsqueeze(1).to_broadcast([n, 8, 128]), ALU.mult, Ro + [Ronb], Ro)
                yield
                tt("dve", yg[:n, :], o[:n, :], zs[:n, :], ALU.mult, Ro + [Rzs], [Ryg])
                yield
                b, br = bank()
                bb = bankb(b).rearrange("p (a b) -> p a b", a=8, b=128)
                for kc in range(8):
                    tr(bb[:, kc, :n], yg[:n, kc * 128:(kc + 1) * 128], ident_b[:n, :n], [Ryg, Rc], [br])
                cp("act", ygT[:, :, :n], bb[:, :, :n], [br], [RygT])
                yield
                for s in range(2):
                    b, br = bank()
                    for kc in range(8):
                        mm(b[:n, :], ygT[:, kc, :n], w_out[:, kc, s * 512:(s + 1) * 512], kc == 0, kc == 7, [RygT, Rwo], [br])
                    tt("dve", x_[:n, s * 512:(s + 1) * 512], b[:n, :], x_[:n, s * 512:(s + 1) * 512], ALU.add, [br, Rx], [Rx])
                    yield
                dma("sp", hs[0][row0:row0 + n, :], x_[:n, :], [Rx], [], f"xst{slot}")

                yield

            def run_il(gens):
                gens = [g for g in gens if g is not None]
                while gens:
                    for g in list(gens):
                        try:
                            next(g)
                        except StopIteration:
                            gens.remove(g)

            load_x(0)
            if len(tiles) > 1:
                load_x(1)
            run_il([head(0)])
            for ti in range(len(tiles)):
                mid(ti)
                drain_ext()
                tg = tail(ti)
                hg = head(ti + 1) if ti + 1 < len(tiles) else None
                nxt_load = (lambda t2=ti + 2: load_x(t2)) if ti + 2 < len(tiles) else None
                tail_alive = True
                while hg is not None:
                    try:
                        next(hg)
                    except StopIteration:
                        hg = None
                    if tail_alive:
                        try:
                            next(tg)
                        except StopIteration:
                            tail_alive = False
                if tail_alive:
                    ext["g"], ext["done"] = tg, nxt_load
                    if ti + 1 >= len(tiles):
                        drain_ext()
                elif nxt_load is not None:
                    nxt_load()
            drain_ext()
            P.barrier()

        def phase_ffn(layer, src, dst, final):
            A.off = persist_off
            Rwg, Rwu, Rwd = Reg("wg"), Reg("wu"), Reg("wd")
            wg = A.alloc([128, 8, DFF], BF16)
            wu = A.alloc([128, 8, DFF], BF16)
            wd = A.alloc([128, NFC, D], BF16)
            load_w(wg, ffn_w_gate[layer], 8, "wg", Rwg)
            load_w(wu, ffn_w_up[layer], 8, "wu", Rwu)
            load_w(wd, ffn_w_down[layer], NFC, "wd", Rwd)
            stage = A.alloc([128, 128], F32)
            Rst = Reg("stage")
            gfm = A.alloc([128, 8], F32)
            Rg = Reg("gfm")
            load_featmajor(gfm, norm_ffn[layer], 8, stage, Rst, Rg, "st")
            if final:
                gfin = A.alloc([128, D], F32)
                Rgfin = Reg("gfin")
                dma("sp", gfin, norm_final.partition_broadcast(128), [], [Rgfin], "gfin")
            NSL = 6
            xt = [A.alloc([128, D], F32) for _ in range(NSL)]
            Rxt = [Reg(f"xt{i}") for i in range(NSL)]
            junk = A.alloc([128, D], BF16)
            Rjunk = Reg("junk")
            small = A.alloc([128, 8], F32)
            Rsm = Reg("small")
            hn = A.alloc([128, D], BF16)
            Rhn = Reg("hn")
            hnT = A.alloc([128, 8, 512], BF16)
            RhnT = Reg("hnT")
            sg = [A.alloc([128, 512], F32) for _ in range(2)]
            Rsg = [Reg("sg0"), Reg("sg1")]
            hT = A.alloc([128, NFC, 512], BF16)
            RhT = Reg("hT")

            macros = []
            cur, tot = [], 0
            for ti, tl in enumerate(tiles):
                if tot + tl[2] > 512:
                    macros.append(cur)
                    cur, tot = [], 0
                cur.append(ti)
                tot += tl[2]
            if cur:
                macros.append(cur)

            def load_x(ti):
                seq, row0, n, _ = tiles[ti]
                slot = ti % NSL
                dma("sp", xt[slot][:n, :], src[row0:row0 + n, :], [], [Rxt[slot]], f"fx{slot}")

            NPRE = 5
            for ti in range(min(NPRE, len(tiles))):
                load_x(ti)
            nloaded = min(NPRE, len(tiles))
            for mac in macros:
                offs = []
                NT = 0
                for ti in mac:
                    offs.append(NT)
                    NT += tiles[ti][2]
                for j, ti in enumerate(mac):
                    seq, row0, n, (kind, k) = tiles[ti]
                    slot = ti % NSL
                    rmsnorm_T(xt[slot], Rxt[slot], n, gfm, Rg, hn, Rhn, hnT[:, :, offs[j]:offs[j] + n], RhnT,
                              small, Rsm, junk, Rjunk)
                for fc in range(NFC):
                    bg, bgr = bank()
                    bu, bur = bank()
                    for kc in range(8):
                        mm(bg[:, :NT], wg[:, kc, fc * 128:(fc + 1) * 128], hnT[:, kc, :NT], kc == 0, kc == 7, [Rwg, RhnT], [bgr])
                    for kc in range(8):
                        mm(bu[:, :NT], wu[:, kc, fc * 128:(fc + 1) * 128], hnT[:, kc, :NT], kc == 0, kc == 7, [Rwu, RhnT], [bur])
                    sl = fc % 2
                    act(sg[sl][:, :NT], bg[:, :NT], AF.Silu, [bgr], [Rsg[sl]])
                    tt("dve", hT[:, fc, :NT], bu[:, :NT], sg[sl][:, :NT], ALU.mult, [bur, Rsg[sl]], [RhT])
                for j, ti in enumerate(mac):
                    seq, row0, n, (kind, k) = tiles[ti]
                    slot = ti % NSL
                    x_ = xt[slot]
                    Rx = Rxt[slot]
                    o0 = offs[j]
                    for s in range(2):
                        b, br = bank()
                        for fc in range(NFC):
                            mm(b[:n, :], hT[:, fc, o0:o0 + n], wd[:, fc, s * 512:(s + 1) * 512], fc == 0, fc == NFC - 1,
                               [RhT, Rwd], [br])
                        tt("dve", x_[:n, s * 512:(s + 1) * 512], b[:n, :], x_[:n, s * 512:(s + 1) * 512], ALU.add, [br, Rx], [Rx])
                    if not final:
                        dma("sp", dst[row0:row0 + n, :], x_[:n, :], [Rx], [], f"fs{slot}")
                    else:
                        act(junk[:n, :], x_[:n, :], AF.Square, [Rx], [Rjunk, Rsm], accum_out=small[:n, 4:5])
                        act(small[:n, 5:6], small[:n, 4:5], AF.Ln, [Rsm, Rc], [Rsm], bias=eps_t[:n, 0:1], scale=1.0 / D)
                        act(small[:n, 6:7], small[:n, 5:6], AF.Exp, [Rsm], [Rsm], scale=-0.5)
                        stt(x_[:n, :], x_[:n, :], small[:n, 6:7], gfin[:n, :], ALU.mult, ALU.mult, [Rx, Rsm, Rgfin], [Rx])
                        if kind == "x":
                            dma("sp", y_prompt[k * 128:(k + 1) * 128, :], x_[:n, :], [Rx], [], f"fs{slot}")
                        elif kind == "samp":
                            dma("sp", y_sample, x_[:n, :], [Rx], [], f"fs{slot}")
                    if nloaded < len(tiles):
                        load_x(nloaded)
                        nloaded += 1
            P.barrier()

        def phase_rwkv(src, dst):
            phase_rwkv_impl(src, dst)

        def phase_rwkv_impl(src, dst):
            A.off = persist_off
            Rwr, Rwk, Rwv, Rwo = Reg("w_r"), Reg("w_k"), Reg("w_v"), Reg("w_o")
            w_r = A.alloc([128, 8, D], BF16)
            w_k = A.alloc([128, 8, D], BF16)
            w_v = A.alloc([128, 8, D], BF16)
            w_o = A.alloc([128, 8, D], BF16)
            w1 = A.alloc([128, 8, 64], BF16)
            a1 = A.alloc([128, 8, 64], BF16)
            g1 = A.alloc([128, 8, 160], BF16)
            w2 = A.alloc([128, D], BF16)
            a2 = A.alloc([128, D], BF16)
            g2 = A.alloc([128, 2, D], BF16)
            Rlo = Reg("lora_w")
            load_w(w_r, rwkv_w_r, 8, "w_r", Rwr)
            load_w(w_k, rwkv_w_k, 8, "w_k", Rwk)
            load_w(w_v, rwkv_w_v, 8, "w_v", Rwv)
            load_w(w_o, rwkv_w_o, 8, "w_o", Rwo)
            load_w(w1, rwkv_w1, 8, "lo", Rlo)
            load_w(a1, rwkv_a1, 8, "lo", Rlo)
            load_w(g1, rwkv_g1, 8, "lo", Rlo)
            dma("pool", w2[0:64, :], rwkv_w2, [], [Rlo], "lo", max_dma_last_dim=4096)
            dma("pool", a2[0:64, :], rwkv_a2, [], [Rlo], "lo", max_dma_last_dim=4096)
            dma("pool", g2[:, 0, :], rwkv_g2[0:128, :], [], [Rlo], "lo", max_dma_last_dim=4096)
            dma("pool", g2[0:32, 1, :], rwkv_g2[128:160, :], [], [Rlo], "lo", max_dma_last_dim=4096)
            stage = A.alloc([128, 128], F32)
            Rst = Reg("stage")
            prm = A.alloc([128, 112], F32)
            Rprm = Reg("prm")
            dma("sp", stage[0:48, :], rwkv_mu.rearrange("m (c p) -> (m c) p", p=128), [], [Rst], "st")
            for i, vec in enumerate([rwkv_w0, rwkv_a0, rwkv_k_k, rwkv_k_a, rwkv_r_k, norm_mix[1]]):
                dma("sp", stage[48 + 8 * i:56 + 8 * i, :], vec.rearrange("(c p) -> c p", p=128), [], [Rst], "st")
            b, br = bank()
            tr(b[:, 0:96], stage[0:96, :], ident_f[0:96, 0:96], [Rst, Rc], [br])
            cp("dve", prm[:, 0:96], b[:, 0:96], [br], [Rprm])
            MU, W0, A0, KK, KA, RK, GM, OMKA = 0, 48, 56, 64, 72, 80, 88, 96
            ts("dve", prm[:, OMKA:OMKA + 8], prm[:, KA:KA + 8], -1.0, ALU.mult, [Rprm], [Rprm], s2=1.0, op1=ALU.add)
            lnw = A.alloc([128, D], F32)
            lnb = A.alloc([128, D], F32)
            Rln = Reg("ln")
            dma("sp", lnw, rwkv_ln_w.partition_broadcast(128), [], [Rln], "lnw")
            dma("sp", lnb, rwkv_ln_b.partition_broadcast(128), [], [Rln], "lnb")
            Rk2 = Reg("consts2")
            blk64 = A.alloc([128, 128], BF16)
            hm = A.alloc([128, 2], F32)
            headind = A.alloc([128, 2], BF16)
            nhmrow = A.alloc([128, 2, 128], BF16)
            hmrow = A.alloc([128, 2, 128], BF16)
            memset("pool", blk64, 0.0, [Rk2])
            memset("pool", blk64[0:64, 0:64], 1.0, [Rk2])
            memset("pool", blk64[64:128, 64:128], 1.0, [Rk2])
            memset("pool", hm, 0.0, [Rk2])
            memset("pool", hm[0:64, 0:1], 1.0, [Rk2])
            memset("pool", hm[64:128, 1:2], 1.0, [Rk2])
            cp("pool", headind, hm, [Rk2], [Rk2])
            memset("pool", hmrow, 0.0, [Rk2])
            memset("pool", hmrow[:, 0, 0:64], 1.0, [Rk2])
            memset("pool", hmrow[:, 1, 64:128], 1.0, [Rk2])
            memset("pool", nhmrow, 0.0, [Rk2])
            memset("pool", nhmrow[:, 0, 0:64], -1.0, [Rk2])
            memset("pool", nhmrow[:, 1, 64:128], -1.0, [Rk2])

            xt = [A.alloc([128, D], F32) for _ in range(2)]
            Rxt = [Reg("xt0"), Reg("xt1")]
            junk = A.alloc([128, D], BF16)
            Rjunk = Reg("junk")
            small = A.alloc([128, 8], F32)
            Rsm = Reg("small")
            hn = A.alloc([128, D], BF16)
            Rhn = Reg("hn")
            hnTe = A.alloc([128, 8, 130], F32)
            RhnT = Reg("hnTe")
            xm = A.alloc([128, 6, 8, 128], BF16)
            Rxm = [Reg(f"xm{m}") for m in range(6)]
            NTMP = 14
            tpblk = A.alloc([128, 2 * NTMP * 128], F32)
            tp = [[tpblk[:, (s_ * NTMP + i_) * 128:(s_ * NTMP + i_ + 1) * 128] for i_ in range(NTMP)] for s_ in range(2)]
            Rtp = [[Reg(f"tp{s}_{i}") for i in range(NTMP)] for s in range(2)]
            xx = tpblk[:, 0:1024].rearrange("p (a b) -> p a b", a=8, b=128)
            Rxx = Rtp[0][0:8]
            sqb = [A.alloc([128, 128], BF16) for _ in range(2)]
            Rsqb = [Reg("sqb0"), Reg("sqb1")]
            bt = A.alloc([128, 8, 128], BF16)
            kt = A.alloc([128, 8, 128], BF16)
            rkr = A.alloc([128, 8, 128], BF16)
            Rbt = [Reg(f"bt{i}") for i in range(8)]
            Rkt = [Reg(f"kt{i}") for i in range(8)]
            Rrkr = Reg("rkr")
            krp = A.alloc([128, 8, 4, 128], BF16)
            Rkrp = [Reg(f"krp{i}") for i in range(8)]
            nbtok = A.alloc([128, 8, 2, 128], BF16)
            ktok = A.alloc([128, 8, 2, 128], BF16)
            Rtok = [Reg(f"tok{i}") for i in range(8)]
            GL = A.alloc([128, 8], F32)
            RGL = [Reg(f"GL{i}") for i in range(8)]
            lo1 = A.alloc([128, 2, 128], BF16)
            lo3 = A.alloc([128, 2, 128], BF16)
            Rlo1, Rlo3 = Reg("lo1"), Reg("lo3")
            vfs = [A.alloc([128, D], F32) for _ in range(2)]
            Rvfs = [Reg("vf0"), Reg("vf1")]
            vb = A.alloc([128, D], BF16)
            Rv = Reg("v")
            gates = [A.alloc([128, D], BF16) for _ in range(2)]
            Rgates = [Reg("gate0"), Reg("gate1")]
            NSETS = 3
            SETS = []
            for si in range(NSETS):
                PRs = [A.alloc([128, 2, 2, 128], BF16) for _ in range(2)]
                RPRs = [Reg(f"PR{si}a"), Reg(f"PR{si}b")]
                PTs = [A.alloc([128, 2, 128], BF16) for _ in range(2)]
                RPTs = [Reg(f"PT{si}a"), Reg(f"PT{si}b")]
                B0s = A.alloc([128, 2, 128], BF16)
                B0Ts = A.alloc([128, 2, 128], BF16)
                SETS.append((PRs, RPRs, PTs, RPTs, B0s, Reg(f"B0_{si}"), B0Ts, Reg(f"B0T_{si}"),
                             A.alloc([128, 2, 128], BF16), A.alloc([128, 2, 128], BF16), A.alloc([128, 2, 128], BF16),
                             Reg(f"mats{si}"), A.alloc([128, 2, 128], BF16), Reg(f"Minv{si}"),
                             A.alloc([128, 2, 64], BF16), A.alloc([128, 2, 64], BF16), Reg(f"RHS{si}"), Reg(f"U{si}")))
            yv = A.alloc([128, D], F32)
            Ry = Reg("y")
            ysq = A.alloc([128, D], F32)
            Rysq = Reg("ysq")
            st16 = A.alloc([128, 96], F32)
            Rst16 = Reg("st16")
            Pst = A.alloc([128, 8, 64], F32)
            Pb = A.alloc([128, 8, 64], BF16)
            PG = A.alloc([128, 2, 64], F32)
            RP = [Reg(f"P{i}") for i in range(8)]
            RPb = [Reg(f"Pb{i}") for i in range(8)]
            RPG = [Reg("PG0"), Reg("PG1")]
            Snat = tpblk[:, NTMP * 128:NTMP * 128 + 1024].rearrange("p (a b) -> p a b", a=16, b=64)
            RSnat = Rtp[1][0:8]
            yg = A.alloc([128, D], BF16)
            Ryg = Reg("yg")
            ygT = A.alloc([128, 8, 128], BF16)
            RygT = Reg("ygT")
            hnT = hnTe[:, :, 1:129]

            def load_x(ti):
                seq, row0, n, _ = tiles[ti]
                slot = ti % 2
                dma("sp", xt[slot][:n, :], src[row0:row0 + n, :], [], [Rxt[slot]], f"xt{slot}")

            ext = {"g": None, "done": None}

            def step_ext():
                if ext["g"] is not None:
                    try:
                        next(ext["g"])
                    except StopIteration:
                        ext["g"] = None
                        if ext["done"] is not None:
                            ext["done"]()
                            ext["done"] = None

            def drain_ext():
                while ext["g"] is not None:
                    step_ext()

            def head(ti):
                seq, row0, n, (kind, k) = tiles[ti]
                slot = ti % 2
                x_ = xt[slot]
                Rx = Rxt[slot]
                first = (ti == 0) or (kind == "samp")
                lastt = (ti == len(tiles) - 2) or (kind == "samp")
                prev_n = tiles[ti - 1][2] if ti > 0 else None
                gate, Rgate = gates[ti % 2], Rgates[ti % 2]
                vf, Rvf = vfs[ti % 2], Rvfs[ti % 2]
                XR, XW, XK, XV, XA, XG = 0, 1, 2, 3, 4, 5
                if first and seq == "p":
                    memset("pool", Pst, 0.0, RP)
                    memset("pool", Pb, 0.0, RPb)
                    memset("pool", hnTe[:, :, 0:1], 0.0, [RhnT])
                elif first:
                    dma("sp", Snat[0:64, :, :], state_rwkv.rearrange("h i j -> i h j"), [], RSnat, "Sld")
                    for half in range(2):
                        b, br = bank()
                        bv = b.rearrange("p (a b) -> p a b", a=8, b=64)
                        for j in range(4):
                            oc = half * 4 + j
                            tr(bv[:, j, :], Snat[0:64, 2 * oc:2 * oc + 2, :].rearrange("p a b -> p (a b)"),
                               ident_f[0:64, 0:64], RSnat + [Rc], [br])
                        cp("dve", Pst[:, half * 4:(half + 1) * 4, :], bv[:, 0:4, :], [br], RP)
                    cp("pool", Pb, Pst, RP, RPb)
                    dma("sp", stage[0:8, :], cache_rwkv_shift.rearrange("o (c p) -> (o c) p", p=128), [], [Rst], "st")
                    b, br = bank()
                    tr(b[:, 0:8], stage[0:8, :], ident_f[0:8, 0:8], [Rst, Rc], [br])
                    cp("dve", hnTe[:, :, 0:1], b[:, 0:8].unsqueeze(2), [br], [RhnT])
                else:
                    cp("pool", hnTe[:, :, 0:1], hnTe[:, :, prev_n:prev_n + 1], [RhnT], [RhnT])

                yield
                rmsnorm_T(x_, Rx, n, prm[:, GM:GM + 8], Rprm, hn, Rhn, hnT, RhnT, small, Rsm, junk, Rjunk)
                if lastt:
                    b, br = bank()
                    tr(b[0:8, 0:128], hnTe[:, :, n:n + 1].rearrange("p a b -> p (a b)"), ident_f, [RhnT, Rc], [br])
                    cp("dve", stage[0:8, :], b[0:8, 0:128], [br], [Rst])
                    dma("sp", o_rwkv_shift[seq].rearrange("o (c p) -> (o c) p", p=128), stage[0:8, :], [Rst], [], "shst" + seq)
                yield
                tt("pool", xx[:, :, :n], hnTe[:, :, 0:n], hnTe[:, :, 1:n + 1], ALU.subtract, [RhnT], Rxx)
                for m in range(6):
                    eng = "dve" if m % 2 == 0 else "pool"
                    tt(eng, xm[:, m, :, :n], xx[:, :, :n],
                       prm[:, MU + 8 * m:MU + 8 * m + 8].unsqueeze(2).to_broadcast([128, 8, n]), ALU.mult,
                       Rxx + [Rprm], [Rxm[m]])
                    tt(eng, xm[:, m, :, :n], xm[:, m, :, :n], hnTe[:, :, 1:n + 1], ALU.add, [Rxm[m], RhnT], [Rxm[m]])
                    if m % 2 == 1:
                        yield
                XR, XW, XK, XV, XA, XG = 0, 1, 2, 3, 4, 5
                b, br = bank()
                for kc in range(8):
                    mm(b[0:64, 0:n], w1[:, kc, :], xm[:, XW, kc, :n], kc == 0, kc == 7, [Rlo, Rxm[XW]], [br])
                for kc in range(8):
                    mm(b[0:64, 128:128 + n], a1[:, kc, :], xm[:, XA, kc, :n], kc == 0, kc == 7, [Rlo, Rxm[XA]], [br])
                act(lo1[0:64, 0, :n], b[0:64, 0:n], AF.Tanh, [br], [Rlo1])
                cp("dve", lo1[0:64, 1, :n], b[0:64, 128:128 + n], [br], [Rlo1])
                yield
                b, br = bank()
                for kc in range(8):
                    mm(b[:, 0:n], g1[:, kc, 0:128], xm[:, XG, kc, :n], kc == 0, kc == 7, [Rlo, Rxm[XG]], [br])
                for kc in range(8):
                    mm(b[0:32, 128:128 + n], g1[:, kc, 128:160], xm[:, XG, kc, :n], kc == 0, kc == 7, [Rlo, Rxm[XG]], [br])
                act(lo3[:, 0, :n], b[:, 0:n], AF.Sigmoid, [br], [Rlo3])
                act(lo3[0:32, 1, :n], b[0:32, 128:128 + n], AF.Sigmoid, [br], [Rlo3])
                yield
                for s in range(2):
                    b, br = bank()
                    mm(b[:n, :], lo3[:, 0, :n], g2[:, 0, s * 512:(s + 1) * 512], True, False, [Rlo3, Rlo], [br])
                    mm(b[:n, :], lo3[0:32, 1, :n], g2[0:32, 1, s * 512:(s + 1) * 512], False, True, [Rlo3, Rlo], [br])
                    cp("act", gate[:n, s * 512:(s + 1) * 512], b[:n, :], [br], [Rgate])
                    yield
                for s in range(2):
                    b, br = bank()
                    for kc in range(8):
                        mm(b[:n, :], xm[:, XV, kc, :n], w_v[:, kc, s * 512:(s + 1) * 512], kc == 0, kc == 7, [Rxm[XV], Rwv], [br])
                    cp("act", vf[:n, s * 512:(s + 1) * 512], b[:n, :], [br], [Rvf])
                    cp("dve", vb[:n, s * 512:(s + 1) * 512], b[:n, :], [br], [Rv])
                    yield


                yield

            def mid(ti):
                seq, row0, n, (kind, k) = tiles[ti]
                slot = ti % 2
                x_ = xt[slot]
                Rx = Rxt[slot]
                first = (ti == 0) or (kind == "samp")
                lastt = (ti == len(tiles) - 2) or (kind == "samp")
                prev_n = tiles[ti - 1][2] if ti > 0 else None
                gate, Rgate = gates[ti % 2], Rgates[ti % 2]
                vf, Rvf = vfs[ti % 2], Rvfs[ti % 2]
                XR, XW, XK, XV, XA, XG = 0, 1, 2, 3, 4, 5
                nf = int(math.log2(n))
                by = [(psf[6][:], psr[6]), (psf[7][:], psr[7])]
                LD, C_, AT, KX, RT_, KKn, TKA, K2, TMP, E1, E2, E3, RR, KR = range(14)

                def prep_gen():
                    for pair in range(4):
                        ocs = (2 * pair, 2 * pair + 1)
                        T2 = {oc: tp[oc % 2] for oc in ocs}
                        R2 = {oc: Rtp[oc % 2] for oc in ocs}
                        bl2 = {}
                        for oc in ocs:
                            T_, RT = T2[oc], R2[oc]
                            osl = slice(oc * 128, (oc + 1) * 128)
                            br_, brr = bank()
                            bk_, bkr = bank()
                            bl_, blr = bank()
                            bl2[oc] = (bl_, blr)
                            for kc in range(8):
                                mm(br_[:, 0:n], w_r[:, kc, osl], xm[:, XR, kc, :n], kc == 0, kc == 7, [Rwr, Rxm[XR]], [brr])
                            for kc in range(8):
                                mm(bk_[:, 0:n], w_k[:, kc, osl], xm[:, XK, kc, :n], kc == 0, kc == 7, [Rwk, Rxm[XK]], [bkr])
                            mm(bl_[:, 0:n], w2[0:64, osl], lo1[0:64, 0, :n], True, True, [Rlo, Rlo1], [blr])
                            mm(bl_[:, 128:128 + n], a2[0:64, osl], lo1[0:64, 1, :n], True, True, [Rlo, Rlo1], [blr])
                            cp("act", T_[RR][:, :n], br_[:, 0:n], [brr], [RT[RR]])
                            cp("act", T_[KR][:, :n], bk_[:, 0:n], [bkr], [RT[KR]])
                            act(T_[KX][:, :n], bk_[:, 0:n], AF.Copy, [bkr, Rprm], [RT[KX]], scale=prm[:, KK + oc:KK + oc + 1])
                            act(T_[LD][:, :n], bl_[:, 0:n], AF.Sigmoid, [blr, Rprm], [RT[LD]], bias=prm[:, W0 + oc:W0 + oc + 1])
                            act(T_[AT][:, :n], bl_[:, 128:128 + n], AF.Sigmoid, [blr, Rprm], [RT[AT]], bias=prm[:, A0 + oc:A0 + oc + 1])
                        steps = []
                        for oc in ocs:
                            T_, RT = T2[oc], R2[oc]
                            sl = oc % 2
                            L = []
                            L.append(lambda T_=T_, RT=RT: ts("pool", T_[LD][:, :n], T_[LD][:, :n], -math.exp(-0.5), ALU.mult, [RT[LD]], [RT[LD]]))
                            L.append(lambda T_=T_, RT=RT: P.op("dve", lambda e, o_=T_[C_][:, :n], d0=ones_f[:, :n], d1=T_[LD][:, :n]:
                                     e.tensor_tensor_scan(o_, d0, d1, 0.0, ALU.mult, ALU.add), [Rc, RT[LD]], [RT[C_]]))
                            L.append(lambda T_=T_, RT=RT, sl=sl: tt("pool", sqb[sl][:, :n], T_[KX][:, :n], T_[KX][:, :n], ALU.mult, [RT[KX]], [Rsqb[sl]]))

                            def ssq(T_=T_, RT=RT, sl=sl):
                                bs_, bsr = bank()
                                mm(bs_[:, 0:n], blk64, sqb[sl][:, :n], True, True, [Rk2, Rsqb[sl]], [bsr])
                                act(T_[RT_][:, :n], bs_[:, 0:n], AF.Ln, [bsr, Rc], [RT[RT_]], bias=eps_t[:, 0:1])
                            L.append(ssq)
                            L.append(lambda T_=T_, RT=RT, oc=oc: ts("dve", T_[TKA][:, :n], T_[AT][:, :n], prm[:, KA + oc:KA + oc + 1], ALU.mult, [RT[AT], Rprm], [RT[TKA]],
                                     s2=prm[:, OMKA + oc:OMKA + oc + 1], op1=ALU.add))
                            L.append(lambda T_=T_, RT=RT: act(T_[RT_][:, :n], T_[RT_][:, :n], AF.Exp, [RT[RT_]], [RT[RT_]], scale=-0.5))
                            L.append(lambda T_=T_, RT=RT: tt("pool", T_[TMP][:, :n], T_[C_][:, :n], T_[LD][:, :n], ALU.subtract, [RT[C_], RT[LD]], [RT[TMP]]))
                            L.append(lambda T_=T_, RT=RT: act(T_[E1][:, :n], T_[TMP][:, :n], AF.Exp, [RT[TMP]], [RT[E1]]))
                            L.append(lambda T_=T_, RT=RT: tt("dve", T_[K2][:, :n], T_[KR][:, :n], T_[TKA][:, :n], ALU.mult, [RT[KR], RT[TKA]], [RT[K2]]))
                            L.append(lambda T_=T_, RT=RT: act(T_[E2][:, :n], T_[C_][:, :n], AF.Exp, [RT[C_]], [RT[E2]], scale=-1.0))
                            L.append(lambda T_=T_, RT=RT: tt("pool", T_[KKn][:, :n], T_[KX][:, :n], T_[RT_][:, :n], ALU.mult, [RT[KX], RT[RT_]], [RT[KKn]]))
                            L.append(lambda T_=T_, RT=RT: act(T_[E3][:, :n], T_[C_][:, :n], AF.Exp, [RT[C_]], [RT[E3]]))
                            L.append(lambda T_=T_, RT=RT, oc=oc: tt("pool", kt[:, oc, :n], T_[K2][:, :n], T_[E2][:, :n], ALU.mult, [RT[K2], RT[E2]], [Rkt[oc]]))
                            for h2 in range(2):
                                L.append(lambda T_=T_, RT=RT, oc=oc, h2=h2: stt(krp[:, oc, 2 * h2 + 0, :n], T_[KKn][:, :n], hm[:, h2:h2 + 1], T_[E1][:, :n], ALU.mult, ALU.mult,
                                         [RT[KKn], Rk2, RT[E1]], [Rkrp[oc]]))
                            L.append(lambda T_=T_, RT=RT: tt("pool", T_[TMP][:, :n], T_[KKn][:, :n], T_[AT][:, :n], ALU.mult, [RT[KKn], RT[AT]], [RT[TMP]]))
                            for h2 in range(2):
                                L.append(lambda T_=T_, RT=RT, oc=oc, h2=h2: stt(krp[:, oc, 2 * h2 + 1, :n], T_[RR][:, :n], hm[:, h2:h2 + 1], T_[E3][:, :n], ALU.mult, ALU.mult,
                                         [RT[RR], Rk2, RT[E3]], [Rkrp[oc]]))
                            L.append(lambda T_=T_, RT=RT, oc=oc: tt("pool", bt[:, oc, :n], T_[TMP][:, :n], T_[E2][:, :n], ALU.mult, [RT[TMP], RT[E2]], [Rbt[oc]]))
                            L.append(lambda T_=T_, RT=RT, oc=oc: cp("pool", GL[:, oc:oc + 1], T_[E3][:, n - 1:n], [RT[E3]], [RGL[oc]]))
                            L.append(lambda T_=T_, RT=RT, oc=oc: stt(rkr[:, oc, :n], T_[RR][:, :n], prm[:, RK + oc:RK + oc + 1], T_[K2][:, :n], ALU.mult, ALU.mult,
                                     [RT[RR], Rprm, RT[K2]], [Rrkr]))
                            steps.append(L)
                        for i in range(len(steps[0])):
                            for L in steps:
                                L[i]()
                            if i % 6 == 5:
                                yield None
                        yield ocs[0]
                        yield ocs[1]

                def g_gen(oc, S_):
                    (PRs, RPRs, PTs, RPTs, B0s, RB0s, B0Ts, RB0Ts, BrTn, AkT, BkT, Rmats, MinvTs, RMinvs, RHSb, Ub, RRHS, RU) = S_
                    b, br = bank()
                    bb = bankb(b).rearrange("p (a b) -> p a b", a=8, b=128)
                    tr(bb[:n, 0, :], bt[:, oc, :n], ident_b, [Rbt[oc], Rc], [br])
                    tr(bb[:n, 1, :], kt[:, oc, :n], ident_b, [Rkt[oc], Rc], [br])
                    tt("dve", nbtok[:n, oc], bb[:n, 0:1, :].to_broadcast([n, 2, 128]), nhmrow[:n], ALU.mult, [br, Rk2], [Rtok[oc]])
                    tt("dve", ktok[:n, oc], bb[:n, 1:2, :].to_broadcast([n, 2, 128]), hmrow[:n], ALU.mult, [br, Rk2], [Rtok[oc]])
                    yield
                    bm1, bm1r = bank()
                    bm2, bm2r = bank()
                    bm3, bm3r = bank()
                    vm1 = bm1.rearrange("p (h t c) -> p h t c", h=2, t=2, c=128)
                    vm2 = bm2.rearrange("p (h t c) -> p h t c", h=2, t=2, c=128)
                    vm3 = bm3.rearrange("p (g c) -> p g c", g=4, c=128)
                    for h2 in range(2):
                        if n == 128:
                            mm(vm1[:n, h2, :, :n], bt[:, oc, :n], krp[:, oc, 2 * h2:2 * h2 + 2, :n], True, True, [Rbt[oc], Rkrp[oc]], [bm1r])
                            mm(vm2[:n, h2, :, :n], kt[:, oc, :n], krp[:, oc, 2 * h2:2 * h2 + 2, :n], True, True, [Rkt[oc], Rkrp[oc]], [bm2r])
                        else:
                            for t_ in range(2):
                                mm(vm1[:n, h2, t_, :n], bt[:, oc, :n], krp[:, oc, 2 * h2 + t_, :n], True, True, [Rbt[oc], Rkrp[oc]], [bm1r])
                                mm(vm2[:n, h2, t_, :n], kt[:, oc, :n], krp[:, oc, 2 * h2 + t_, :n], True, True, [Rkt[oc], Rkrp[oc]], [bm2r])
                        mm(vm3[:n, h2, :n], krp[:, oc, 2 * h2, :n], bt[:, oc, :n], True, True, [Rkrp[oc], Rbt[oc]], [bm3r])
                    stt(B0s[:n, :, :n], vm1[:n, :, 0, :n], -1.0, smask_u[:n, :n].unsqueeze(1).to_broadcast([n, 2, n]),
                        ALU.mult, ALU.mult, [bm1r, Rc], [RB0s])
                    stt(BrTn[:n, :, :n], vm1[:n, :, 1, :n], -1.0, imask_u[:n, :n].unsqueeze(1).to_broadcast([n, 2, n]),
                        ALU.mult, ALU.mult, [bm1r, Rc], [Rmats])
                    tt("dve", AkT[:n, :, :n], vm2[:n, :, 0, :n], smask_u[:n, :n].unsqueeze(1).to_broadcast([n, 2, n]),
                       ALU.mult, [bm2r, Rc], [Rmats])
                    tt("dve", BkT[:n, :, :n], vm2[:n, :, 1, :n], imask_u[:n, :n].unsqueeze(1).to_broadcast([n, 2, n]),
                       ALU.mult, [bm2r, Rc], [Rmats])
                    stt(B0Ts[:n, :, :n], vm3[:n, 0:2, :n], -1.0, smask_l[:n, :n].unsqueeze(1).to_broadcast([n, 2, n]),
                        ALU.mult, ALU.mult, [bm3r, Rc], [RB0Ts])
                    yield
                    res = []
                    for _ in neumann_gen(B0s[:n, :, :n], B0Ts[:n, :, :n], RB0s, RB0Ts, n, nf, PRs, RPRs, PTs, RPTs, 2, res, psum_acc=True):
                        yield
                    Rfin, RRfin = res[0]
                    cp("act", MinvTs[:n, :, :n], Rfin, [RRfin], [RMinvs])
                    bR, bRr = bank()
                    vR = bR[:, 0:128].rearrange("p (g c) -> p g c", g=2, c=64)
                    for h2 in range(2):
                        hd = 2 * oc + h2
                        mm(vR[:n, h2, :], krp[:, oc, 2 * h2, :n], Pb[:, oc, :], True, False, [Rkrp[oc], RPb[oc]], [bRr])
                        mm(vR[:n, h2, :], AkT[:n, h2, :n], vb[:n, hd * 64:(hd + 1) * 64], False, True, [Rmats, Rv], [bRr])
                    cp("dve", RHSb[:n], vR[:n], [bRr], [RRHS])
                    yield
                    bU, bUr = bank()
                    vU = bU[:, 0:128].rearrange("p (g c) -> p g c", g=2, c=64)
                    for h2 in range(2):
                        mm(vU[:n, h2, :], MinvTs[:n, h2, :n], RHSb[:n, h2, :], True, True, [RMinvs, RRHS], [bUr])
                    cp("act", Ub[:n], vU[:n], [bUr], [RU])
                    yield
                    drain_ext()
                    bY, bYr = bank()
                    vY = bY[:, 0:128].rearrange("p (g c) -> p g c", g=2, c=64)
                    for h2 in range(2):
                        hd = 2 * oc + h2
                        mm(vY[:n, h2, :], krp[:, oc, 2 * h2 + 1, :n], Pb[:, oc, :], True, False, [Rkrp[oc], RPb[oc]], [bYr])
                        mm(vY[:n, h2, :], BrTn[:n, h2, :n], Ub[:n, h2, :], False, False, [Rmats, RU], [bYr])
                        mm(vY[:n, h2, :], BkT[:n, h2, :n], vb[:n, hd * 64:(hd + 1) * 64], False, True, [Rmats, Rv], [bYr])
                    cp("act", yv[:n, oc * 128:(oc + 1) * 128], bY[:n, 0:128], [bYr], [Ry])
                    bP, bPr = bank()
                    for h2 in range(2):
                        hd = 2 * oc + h2
                        mm(bP[:, 0:64], nbtok[:n, oc, h2, :], Ub[:n, h2, :], h2 == 0, False, [Rtok[oc], RU], [bPr])
                        mm(bP[:, 0:64], ktok[:n, oc, h2, :], vb[:n, hd * 64:(hd + 1) * 64], False, h2 == 1, [Rtok[oc], Rv], [bPr])
                    ts("pool", PG[:, oc % 2, :], Pst[:, oc, :], GL[:, oc:oc + 1], ALU.mult, [RP[oc], RGL[oc]], [RPG[oc % 2]])
                    stt(Pst[:, oc, :], bP[:, 0:64], GL[:, oc:oc + 1], PG[:, oc % 2, :], ALU.mult, ALU.add,
                        [bPr, RGL[oc], RPG[oc % 2]], [RP[oc]])
                    cp("pool", Pb[:, oc, :], Pst[:, oc, :], [RP[oc]], [RPb[oc]])

                pg = prep_gen()
                ready = set()
                next_g = 0
                active = []
                prep_alive = True
                while prep_alive or active or next_g < 8:
                    if prep_alive:
                        try:
                            r = next(pg)
                            if r is not None:
                                ready.add(r)
                        except StopIteration:
                            prep_alive = False
                    while next_g < 8 and next_g in ready and len(active) < NSETS:
                        active.append(g_gen(next_g, SETS[next_g % NSETS]))
                        next_g += 1
                    for g in list(active):
                        try:
                            next(g)
                        except StopIteration:
                            active.remove(g)
                    step_ext()
                drain_ext()

                b, br = bank()
                for oc in range(8):
                    mm(b[:n, 2 * oc:2 * oc + 2], rkr[:, oc, :n], headind, True, True, [Rrkr, Rk2], [br])
                cp("dve", st16[:n, 32:48], b[:n, 0:16], [br], [Rst16])


                if lastt:
                    for half in range(2):
                        b, br = bank()
                        bv = b.rearrange("p (a b) -> p a b", a=4, b=128)
                        for j in range(4):
                            oc = half * 4 + j
                            tr(bv[0:64, j, :], Pst[:, oc, :], ident_f, RP + [Rc], [br])
                        cp("dve", Snat[0:64, half * 8:(half + 1) * 8, :].rearrange("p a b -> p (a b)"),
                           b[0:64, :], [br], RSnat)
                    dma("sp", o_rwkv_state[seq].rearrange("h i j -> i h j"), Snat[0:64, :, :], RSnat, [], "Sst" + seq)

            def tail(ti):
                seq, row0, n, (kind, k) = tiles[ti]
                slot = ti % 2
                x_ = xt[slot]
                Rx = Rxt[slot]
                first = (ti == 0) or (kind == "samp")
                lastt = (ti == len(tiles) - 2) or (kind == "samp")
                prev_n = tiles[ti - 1][2] if ti > 0 else None
                gate, Rgate = gates[ti % 2], Rgates[ti % 2]
                vf, Rvf = vfs[ti % 2], Rvfs[ti % 2]
                XR, XW, XK, XV, XA, XG = 0, 1, 2, 3, 4, 5
                y3 = yv.rearrange("p (a b) -> p a b", a=16, b=64)
                q3 = ysq.rearrange("p (a b) -> p a b", a=16, b=64)
                v3 = vf.rearrange("p (a b) -> p a b", a=16, b=64)
                P.op("dve", lambda e, o_=st16[:n, 0:16], i_=y3[:n]: e.tensor_reduce(o_, i_, mybir.AxisListType.X, ALU.add),
                     [Ry], [Rst16])
                tt("pool", ysq[:n, :], yv[:n, :], yv[:n, :], ALU.mult, [Ry], [Rysq])
                P.op("dve", lambda e, o_=st16[:n, 16:32], i_=q3[:n]: e.tensor_reduce(o_, i_, mybir.AxisListType.X, ALU.add),
                     [Rysq], [Rst16])
                yield
                ts("dve", st16[:n, 0:16], st16[:n, 0:16], 1.0 / 64, ALU.mult, [Rst16], [Rst16])
                tt("dve", st16[:n, 48:64], st16[:n, 0:16], st16[:n, 0:16], ALU.mult, [Rst16], [Rst16])
                stt(st16[:n, 16:32], st16[:n, 16:32], 1.0 / 64, st16[:n, 48:64], ALU.mult, ALU.subtract, [Rst16], [Rst16])
                act(st16[:n, 16:32], st16[:n, 16:32], AF.Ln, [Rst16, Rc], [Rst16], bias=eps_t[:n, 2:3])
                act(st16[:n, 16:32], st16[:n, 16:32], AF.Exp, [Rst16], [Rst16], scale=-0.5)
                yield
                tt("dve", y3[:n], y3[:n], st16[:n, 0:16].unsqueeze(2).to_broadcast([n, 16, 64]), ALU.subtract, [Ry, Rst16], [Ry])
                tt("pool", y3[:n], y3[:n], st16[:n, 16:32].unsqueeze(2).to_broadcast([n, 16, 64]), ALU.mult, [Ry, Rst16], [Ry])
                yield
                tt("dve", yv[:n, :], yv[:n, :], lnw[:n, :], ALU.mult, [Ry, Rln], [Ry])
                tt("pool", yv[:n, :], yv[:n, :], lnb[:n, :], ALU.add, [Ry, Rln], [Ry])
                yield
                tt("dve", q3[:n], v3[:n], st16[:n, 32:48].unsqueeze(2).to_broadcast([n, 16, 64]), ALU.mult, [Rvf, Rst16], [Rysq])
                tt("pool", yv[:n, :], yv[:n, :], ysq[:n, :], ALU.add, [Ry, Rysq], [Ry])
                yield
                tt("dve", yg[:n, :], yv[:n, :], gate[:n, :], ALU.mult, [Ry, Rgate], [Ryg])
                b, br = bank()
                bb = bankb(b).rearrange("p (a b) -> p a b", a=8, b=128)
                for kc in range(8):
                    tr(bb[:, kc, :n], yg[:n, kc * 128:(kc + 1) * 128], ident_b[:n, :n], [Ryg, Rc], [br])
                cp("act", ygT[:, :, :n], bb[:, :, :n], [br], [RygT])
                yield
                for s in range(2):
                    b, br = bank()
                    for kc in range(8):
                        mm(b[:n, :], ygT[:, kc, :n], w_o[:, kc, s * 512:(s + 1) * 512], kc == 0, kc == 7, [RygT, Rwo], [br])
                    tt("dve", x_[:n, s * 512:(s + 1) * 512], b[:n, :], x_[:n, s * 512:(s + 1) * 512], ALU.add, [br, Rx], [Rx])
                    yield
                dma("sp", dst[row0:row0 + n, :], x_[:n, :], [Rx], [], f"xst{slot}")

                yield

            def run_il(gens):
                gens = [g for g in gens if g is not None]
                while gens:
                    for g in list(gens):
                        try:
                            next(g)
                        except StopIteration:
                            gens.remove(g)

            load_x(0)
            if len(tiles) > 1:
                load_x(1)
            run_il([head(0)])
            for ti in range(len(tiles)):
                mid(ti)
                drain_ext()
                tg = tail(ti)
                hg = head(ti + 1) if ti + 1 < len(tiles) else None
                nxt_load = (lambda t2=ti + 2: load_x(t2)) if ti + 2 < len(tiles) else None
                tail_alive = True
                while hg is not None:
                    try:
                        next(hg)
                    except StopIteration:
                        hg = None
                    if tail_alive:
                        try:
                            next(tg)
                        except StopIteration:
                            tail_alive = False
                if tail_alive:
                    ext["g"], ext["done"] = tg, nxt_load
                    if ti + 1 >= len(tiles):
                        drain_ext()
                elif nxt_load is not None:
                    nxt_load()
            drain_ext()
            P.barrier()

        if 1 in phases:
            phase_gdn()
        if 2 in phases:
            phase_ffn(0, hs[0], hs[1], False)
        if 3 in phases:
            phase_rwkv(hs[1], hs[2])
        if 4 in phases:
            phase_ffn(1, hs[2], None, True)

        with nc.Block() as block:
            @block.tensor
            def _(e):
                P.replay("pe", e)

            @block.scalar
            def _(e):
                P.replay("act", e)

            @block.vector
            def _(e):
                P.replay("dve", e)

            @block.gpsimd
            def _(e):
                P.replay("pool", e)

            @block.sync
            def _(e):
                P.replay("sp", e)
    return nc


_NC = None

IN_NAMES_PER_CORE = {
    "x_prompt": lambda a, i: a[i],
    "x_sample": lambda a, i: a[i],
    "cache_gdn_conv": lambda a, i: a[0, i],
    "state_gdn": lambda a, i: a[0, i],
    "cache_rwkv_shift": lambda a, i: a[0, i],
    "state_rwkv": lambda a, i: a[0, i],
}
SQUEEZE0 = ["gdn_w_in", "gdn_conv_w", "gdn_a_log", "gdn_dt_bias", "gdn_o_norm", "gdn_w_out", "rwkv_mu", "rwkv_w0",
            "rwkv_w1", "rwkv_w2", "rwkv_a0", "rwkv_a1", "rwkv_a2", "rwkv_g1", "rwkv_g2", "rwkv_k_k", "rwkv_k_a",
            "rwkv_w_r", "rwkv_w_k", "rwkv_w_v", "rwkv_w_o", "rwkv_ln_w", "rwkv_ln_b"]


def kernel(**inputs):
    global _NC
    if _NC is None:
        _NC = build_program()
    nc = _NC
    f = lambda a: np.ascontiguousarray(np.asarray(a, dtype=np.float32))
    shared = {}
    for k in ["meta_tokens", "norm_mix", "norm_ffn", "norm_final", "ffn_w_gate", "ffn_w_up", "ffn_w_down"]:
        shared[k] = f(inputs[k])
    for k in SQUEEZE0:
        shared[k] = f(np.asarray(inputs[k])[0])
    shared["rwkv_r_k"] = f(np.asarray(inputs["rwkv_r_k"])[0].reshape(D))
    in_maps = []
    for i in range(8):
        m = dict(shared)
        for k, fn in IN_NAMES_PER_CORE.items():
            m[k] = f(fn(np.asarray(inputs[k]), i))
        in_maps.append(m)
    res = run_bass_kernel_spmd(nc, in_maps, core_ids=list(range(8)))
    R = res.results

    def st(name, lead=None):
        a = np.stack([np.asarray(R[i][name], dtype=np.float32) for i in range(8)], axis=0)
        return a if lead is None else a[None]

    return (st("y_prompt"), st("y_sample"),
            st("p_gdn_conv", 1), st("p_gdn_state", 1), st("p_rwkv_shift", 1), st("p_rwkv_state", 1),
            st("s_gdn_conv", 1), st("s_gdn_state", 1), st("s_rwkv_shift", 1), st("s_rwkv_state", 1))
```
